# Optimizing a Trainium2 kernel written in Bass

```python
import jax, jax.numpy as jnp
from jax import lax
import numpy as np

D_MODEL = 2048
BATCH = 32
SEQ = 256
DEPTH = 2
DEC_BATCH = 2
DEC_SEQ = 2048
PAST_LEN = 512

GRID_W = 64
HEAD_DIM = 128
CONV_WIDTH = D_MODEL // 2
CONV_TAPS = 3
GQA_HEADS = (D_MODEL // 2) // HEAD_DIM
GQA_KV_HEADS = 2
NA_HEADS = D_MODEL // HEAD_DIM
NA_WIN_ROWS = 8
NA_WIN_COLS = 16
D_FF = 4 * D_MODEL
N_EVEN = (DEPTH + 1) // 2
N_ODD = DEPTH // 2
QUERY_BLOCK = 128
ROPE_THETA = 10000.0
ROPE_AXIS_DIM = HEAD_DIM // 2
NORM_EPS = 1e-6
MIX_WIDTH = CONV_WIDTH + GQA_HEADS * HEAD_DIM
AB_IN = 3 * CONV_WIDTH + (GQA_HEADS + 2 * GQA_KV_HEADS) * HEAD_DIM

kernel_name = "hybrid_dit_prefix_conv_gqa_natten"


def _rms_norm(x, g):
    xf = x.astype(jnp.float32)
    y = xf * lax.rsqrt(jnp.mean(xf * xf, axis=-1, keepdims=True) + NORM_EPS)
    return (y * g.astype(jnp.float32)).astype(x.dtype)


def _modulation(cond, w, b):
    m = jax.nn.silu(cond) @ w + b
    return [t[:, None, :] for t in jnp.split(m, 6, axis=-1)]


def _ada(x, g, shift, scale):
    return _rms_norm(x, g) * (1 + scale) + shift


def _short_conv(u, w):
    up = jnp.pad(u, ((0, 0), (1, 1), (0, 0)))
    return up[:, :-2] * w[0] + up[:, 1:-1] * w[1] + up[:, 2:] * w[2]


def _axial_rope(x):
    T = x.shape[1]
    t = jnp.arange(T)
    half = ROPE_AXIS_DIM // 2
    inv = ROPE_THETA ** (-jnp.arange(half, dtype=jnp.float32) / half)

    def rot(xa, pos):
        ang = pos.astype(jnp.float32)[:, None] * inv
        cos = jnp.cos(ang)[None, :, None, :]
        sin = jnp.sin(ang)[None, :, None, :]
        x1, x2 = xa[..., :half], xa[..., half:]
        return jnp.concatenate([x1 * cos - x2 * sin, x1 * sin + x2 * cos], axis=-1)

    xf = x.astype(jnp.float32)
    out = jnp.concatenate([rot(xf[..., :ROPE_AXIS_DIM], t // GRID_W),
                           rot(xf[..., ROPE_AXIS_DIM:], t % GRID_W)], axis=-1)
    return out.astype(x.dtype)


def _block_attention(q, k, v):
    B, S, Hq, D = q.shape
    Hkv = k.shape[2]
    G = Hq // Hkv
    nb = S // QUERY_BLOCK
    qb = q.reshape(B, nb, QUERY_BLOCK, Hkv, G, D).transpose(1, 0, 2, 3, 4, 5)
    scale = D ** -0.5

    def one(qblk):
        s = jnp.einsum('bqhgd,bkhd->bhgqk', qblk, k).astype(jnp.float32) * scale
        p = jax.nn.softmax(s, axis=-1).astype(v.dtype)
        return jnp.einsum('bhgqk,bkhd->bqhgd', p, v)

    o = lax.map(one, qb)
    return o.transpose(1, 0, 2, 3, 4, 5).reshape(B, S, Hq * D)


def _neighbourhood_attention(q, k, v, ck, cv, rel_bias):
    B, T, H, D = q.shape
    rows = T // GRID_W
    wr = min(NA_WIN_ROWS, rows)
    wc = NA_WIN_COLS
    nk = wr * wc
    r = jnp.arange(rows)
    col = jnp.arange(GRID_W)
    krow = jnp.clip(r - wr // 2, 0, rows - wr)[:, None] + jnp.arange(wr)
    kcol = jnp.clip(col - wc // 2, 0, GRID_W - wc)[:, None] + jnp.arange(wc)
    idx = (krow[:, None, :, None] * GRID_W + kcol[None, :, None, :]).reshape(rows, GRID_W, nk)
    drow = krow - r[:, None] + (NA_WIN_ROWS - 1)
    dcol = kcol - col[:, None] + (NA_WIN_COLS - 1)
    qr = q.reshape(B, rows, GRID_W, H, D).transpose(1, 0, 2, 3, 4)
    scale = D ** -0.5

    def one(args):
        q_blk, idx_blk, drow_blk = args
        kg = k[:, idx_blk]
        vg = v[:, idx_blk]
        bias = rel_bias[:, drow_blk[None, :, None], dcol[:, None, :]].reshape(H, GRID_W, nk)
        s_loc = jnp.einsum('bqhd,bqkhd->bhqk', q_blk, kg).astype(jnp.float32) * scale \
            + bias.astype(jnp.float32)[None]
        s_ctx = jnp.einsum('bqhd,bkhd->bhqk', q_blk, ck).astype(jnp.float32) * scale
        p = jax.nn.softmax(jnp.concatenate([s_loc, s_ctx], axis=-1), axis=-1).astype(v.dtype)
        return (jnp.einsum('bhqk,bqkhd->bqhd', p[..., :nk], vg)
                + jnp.einsum('bhqk,bkhd->bqhd', p[..., nk:], cv))

    o = lax.map(one, (qr, idx, drow))
    return o.transpose(1, 0, 2, 3, 4).reshape(B, T, H * D)


def _ab_project(h, w_in, q_norm, k_norm):
    B, T, _ = h.shape
    z = h @ w_in
    c1 = CONV_WIDTH
    c2 = 2 * c1
    c3 = 3 * c1
    c4 = c3 + GQA_HEADS * HEAD_DIM
    c5 = c4 + GQA_KV_HEADS * HEAD_DIM
    gate_b, gate_c, xa = z[..., :c1], z[..., c1:c2], z[..., c2:c3]
    q = _rms_norm(z[..., c3:c4].reshape(B, T, GQA_HEADS, HEAD_DIM), q_norm)
    k = _rms_norm(z[..., c4:c5].reshape(B, T, GQA_KV_HEADS, HEAD_DIM), k_norm)
    v = z[..., c5:].reshape(B, T, GQA_KV_HEADS, HEAD_DIM)
    return gate_b, gate_c, xa, q, k, v


def _ab_context(h, w_in, conv_w, q_norm, k_norm, w_out):
    gb, gc, xa, q, k, v = _ab_project(h, w_in, q_norm, k_norm)
    a = gb * _short_conv(gc * xa, conv_w)
    b = _block_attention(q, k, v)
    return jnp.concatenate([a, b], axis=-1) @ w_out, k, v


def _ab_latent(h, ctx_k, ctx_v, w_in, conv_w, q_norm, k_norm, w_out):
    gb, gc, xa, q, k, v = _ab_project(h, w_in, q_norm, k_norm)
    a = gb * _short_conv(gc * xa, conv_w)
    q = _axial_rope(q)
    k = _axial_rope(k)
    b = _block_attention(q, jnp.concatenate([ctx_k, k], axis=1),
                         jnp.concatenate([ctx_v, v], axis=1))
    return jnp.concatenate([a, b], axis=-1) @ w_out


def _na_project(h, w_qkv):
    B, T, _ = h.shape
    q, k, v = jnp.split(h @ w_qkv, 3, axis=-1)
    shp = (B, T, NA_HEADS, HEAD_DIM)
    return q.reshape(shp), k.reshape(shp), v.reshape(shp)


def _na_context(h, w_qkv, w_out):
    q, k, v = _na_project(h, w_qkv)
    return _block_attention(q, k, v) @ w_out, k, v


def _na_latent(h, ctx_k, ctx_v, w_qkv, rel_bias, w_out):
    q, k, v = _na_project(h, w_qkv)
    return _neighbourhood_attention(q, k, v, ctx_k, ctx_v, rel_bias) @ w_out


def _mlp(h, w1, w2):
    return jnp.square(jax.nn.relu(h @ w1)) @ w2


def setup_inputs(seed: int = 0) -> dict:
    key = jax.random.key(seed)
    ks = jax.random.split(key, 23)

    def nrm(k, shape, scale=1.0):
        return jax.random.normal(k, shape, jnp.float32) * scale

    return {
        "x_prompt": nrm(ks[0], (BATCH, SEQ, D_MODEL)),
        "x_sample": nrm(ks[1], (DEC_BATCH, DEC_SEQ, D_MODEL)),
        "cache_attn_k": nrm(ks[2], (DEC_BATCH, N_EVEN, PAST_LEN, GQA_KV_HEADS, HEAD_DIM)),
        "cache_attn_v": nrm(ks[3], (DEC_BATCH, N_EVEN, PAST_LEN, GQA_KV_HEADS, HEAD_DIM)),
        "cache_na_k": nrm(ks[4], (DEC_BATCH, N_ODD, PAST_LEN, NA_HEADS, HEAD_DIM)),
        "cache_na_v": nrm(ks[5], (DEC_BATCH, N_ODD, PAST_LEN, NA_HEADS, HEAD_DIM)),
        "c": nrm(ks[6], (DEC_BATCH, D_MODEL)),
        "c_ctx": nrm(ks[7], (D_MODEL,)),
        "mod_w": nrm(ks[8], (DEPTH, D_MODEL, 6 * D_MODEL), D_MODEL ** -0.5),
        "mod_b": nrm(ks[9], (DEPTH, 6 * D_MODEL), 0.01),
        "norm1_g": 1.0 + nrm(ks[10], (DEPTH, D_MODEL), 0.02),
        "norm2_g": 1.0 + nrm(ks[11], (DEPTH, D_MODEL), 0.02),
        "ab_w_in": nrm(ks[12], (N_EVEN, D_MODEL, AB_IN), D_MODEL ** -0.5),
        "ab_conv_w": nrm(ks[13], (N_EVEN, CONV_TAPS, CONV_WIDTH), CONV_TAPS ** -0.5),
        "ab_q_norm": 1.0 + nrm(ks[14], (N_EVEN, HEAD_DIM), 0.02),
        "ab_k_norm": 1.0 + nrm(ks[15], (N_EVEN, HEAD_DIM), 0.02),
        "ab_w_out": nrm(ks[16], (N_EVEN, MIX_WIDTH, D_MODEL), MIX_WIDTH ** -0.5),
        "na_w_qkv": nrm(ks[17], (N_ODD, D_MODEL, 3 * NA_HEADS * HEAD_DIM), D_MODEL ** -0.5),
        "na_rel_bias": nrm(ks[18], (N_ODD, NA_HEADS, 2 * NA_WIN_ROWS - 1, 2 * NA_WIN_COLS - 1), 0.1),
        "na_w_out": nrm(ks[19], (N_ODD, NA_HEADS * HEAD_DIM, D_MODEL), (NA_HEADS * HEAD_DIM) ** -0.5),
        "mlp_w1": nrm(ks[20], (DEPTH, D_MODEL, D_FF), D_MODEL ** -0.5),
        "mlp_w2": nrm(ks[21], (DEPTH, D_FF, D_MODEL), D_FF ** -0.5),
        "final_norm_g": 1.0 + nrm(ks[22], (D_MODEL,), 0.02),
    }


def reference(x_prompt, x_sample, cache_attn_k, cache_attn_v, cache_na_k, cache_na_v, c, c_ctx,
              mod_w, mod_b, norm1_g, norm2_g, ab_w_in, ab_conv_w, ab_q_norm, ab_k_norm, ab_w_out,
              na_w_qkv, na_rel_bias, na_w_out, mlp_w1, mlp_w2, final_norm_g):
    xp = x_prompt
    xs = x_sample
    cond_ctx = c_ctx[None, :]
    new_ak, new_av, new_nk, new_nv = [], [], [], []
    for i in range(DEPTH):
        j = i // 2
        sp1, cp1, gp1, sp2, cp2, gp2 = _modulation(cond_ctx, mod_w[i], mod_b[i])
        ss1, cs1, gs1, ss2, cs2, gs2 = _modulation(c, mod_w[i], mod_b[i])
        hp = _ada(xp, norm1_g[i], sp1, cp1)
        hs = _ada(xs, norm1_g[i], ss1, cs1)
        if i % 2 == 0:
            op, kp, vp = _ab_context(hp, ab_w_in[j], ab_conv_w[j], ab_q_norm[j], ab_k_norm[j],
                                     ab_w_out[j])
            os_ = _ab_latent(hs, cache_attn_k[:, j], cache_attn_v[:, j], ab_w_in[j], ab_conv_w[j],
                             ab_q_norm[j], ab_k_norm[j], ab_w_out[j])
            new_ak.append(kp)
            new_av.append(vp)
        else:
            op, kp, vp = _na_context(hp, na_w_qkv[j], na_w_out[j])
            os_ = _na_latent(hs, cache_na_k[:, j], cache_na_v[:, j], na_w_qkv[j], na_rel_bias[j],
                             na_w_out[j])
            new_nk.append(kp)
            new_nv.append(vp)
        xp = xp + gp1 * op
        xs = xs + gs1 * os_
        xp = xp + gp2 * _mlp(_ada(xp, norm2_g[i], sp2, cp2), mlp_w1[i], mlp_w2[i])
        xs = xs + gs2 * _mlp(_ada(xs, norm2_g[i], ss2, cs2), mlp_w1[i], mlp_w2[i])
    y_prompt = _rms_norm(xp, final_norm_g)
    y_sample = _rms_norm(xs, final_norm_g)
    new_attn_k = jnp.stack(new_ak, axis=1)
    new_attn_v = jnp.stack(new_av, axis=1)
    new_na_k = jnp.stack(new_nk, axis=1)
    new_na_v = jnp.stack(new_nv, axis=1)
    return (y_prompt, y_sample, new_attn_k, new_attn_v, new_na_k, new_na_v)
```

```python
import contextlib
import numpy as np
import ml_dtypes
import concourse.bass as bass
import concourse.mybir as mybir
from concourse.bass_utils import run_bass_kernel_spmd

F32 = mybir.dt.float32
BF16 = mybir.dt.bfloat16
AF = mybir.ActivationFunctionType
ALU = mybir.AluOpType

D = 2048
NEG = -30000.0
SCALE = 128.0 ** -0.5
EPS = 1e-6


class Tok:
    __slots__ = ("w", "r", "rd", "excl")

    def __init__(self, excl=False):
        self.excl = excl
        self.w = None
        self.r = {}
        self.rd = {}


class Slot:
    __slots__ = ("sem", "count")

    def __init__(self, sem):
        self.sem = sem
        self.count = 0


class Op:
    __slots__ = ("fn", "deps", "flag", "count", "slot")

    def __init__(self, fn, deps, slot=None):
        self.fn = fn
        self.deps = deps
        self.flag = False
        self.count = 0
        self.slot = slot


class Prog:
    ENGS = ("pe", "act", "dve", "pool", "sp")

    def __init__(self, nc, stack):
        self.nc = nc
        self.stack = stack
        self.ops = {e: [] for e in self.ENGS}
        self.esem = {e: stack.enter_context(nc.semaphore("es_" + e)) for e in ("pe", "act", "dve")}
        self.nslots = 0

    def slot(self):
        self.nslots += 1
        return Slot(self.stack.enter_context(self.nc.semaphore("ds%d" % self.nslots)))

    def _deps(self, eng, reads, writes, is_dma):
        deps = []
        for t in reads:
            if t.w is not None:
                w = t.w
                if w[0] == "d" or is_dma or w[1] != eng or eng != "pe":
                    deps.append(w)
        for t in writes:
            if t.w is not None:
                w = t.w
                if w[0] == "d" or is_dma or w[1] != eng or eng != "pe":
                    deps.append(w)
            for e, idx in t.r.items():
                if is_dma or e != eng or eng != "pe":
                    deps.append(("e", e, idx))
            for s, c in t.rd.items():
                deps.append(("d", s, c))
        return deps

    def op(self, eng, fn, reads=(), writes=()):
        if any(t.excl for t in reads):
            writes = list(writes) + [t for t in reads if t.excl]
            reads = [t for t in reads if not t.excl]
        deps = self._deps(eng, reads, writes, False)
        idx = len(self.ops[eng])
        self.ops[eng].append(Op(fn, deps))
        for t in reads:
            t.r[eng] = idx
        for t in writes:
            t.w = ("e", eng, idx)
            t.r = {}
            t.rd = {}

    def dma(self, q, fn, slot, reads=(), writes=()):
        deps = self._deps(q, reads, writes, True)
        slot.count += 16
        self.ops[q].append(Op(fn, deps, slot))
        for t in reads:
            t.rd[slot] = slot.count
        for t in writes:
            t.w = ("d", slot, slot.count)
            t.r = {}
            t.rd = {}

    def emit(self, block, final_slots):
        for e in ("pe", "act", "dve"):
            for o in self.ops[e]:
                for d in o.deps:
                    if d[0] == "e":
                        self.ops[d[1]][d[2]].flag = True
        for e in self.ENGS:
            for o in self.ops[e]:
                for d in o.deps:
                    if d[0] == "e":
                        self.ops[d[1]][d[2]].flag = True
        for e in ("pe", "act", "dve"):
            c = 0
            for o in self.ops[e]:
                if o.flag:
                    c += 1
                    o.count = c

        def runner(ename):
            def f(eng):
                seen = {}
                for o in self.ops[ename]:
                    for d in o.deps:
                        if d[0] == "e":
                            sem = self.esem[d[1]]
                            val = self.ops[d[1]][d[2]].count
                            key = d[1]
                        else:
                            sem = d[1].sem
                            val = d[2]
                            key = d[1]
                        if seen.get(key, 0) >= val:
                            continue
                        seen[key] = val
                        eng.wait_ge(sem, val)
                    ins = o.fn(eng)
                    if o.slot is not None:
                        ins.then_inc(o.slot.sem, 16)
                    elif o.flag:
                        ins.then_inc(self.esem[ename], 1)
                if ename == "sp":
                    for s in final_slots:
                        if s.count:
                            eng.wait_ge(s.sem, s.count)
            return f

        block.tensor(runner("pe"))
        block.scalar(runner("act"))
        block.vector(runner("dve"))
        block.gpsimd(runner("pool"))
        block.sync(runner("sp"))


class _Stop(Exception):
    pass


def build_program(stop=None, groups=(0, 1)):
    nc = bass.Bass("TRN2", target_bir_lowering=False)

    def din(name, shape):
        return nc.dram_tensor(name, list(shape), F32, kind="ExternalInput").ap()

    def dout(name, shape):
        return nc.dram_tensor(name, list(shape), F32, kind="ExternalOutput").ap()

    xp_d = din("xp", (1024, D))
    xs_d = din("xs", (2048, D))
    ck0_d = din("ck0", (512, 256))
    cv0_d = din("cv0", (512, 256))
    ck1_d = din("ck1", (512, D))
    cv1_d = din("cv1", (512, D))
    cond_d = din("condT", (128, 32))
    modw_d = din("modw", (2, D, 6 * D))
    modb_d = din("modb", (128, 192))
    g1_d = din("g1", (128, 32))
    g2_d = din("g2", (128, 32))
    gf_d = din("gf", (128, 16))
    win_d = din("w_in", (D, 4608))
    convw_d = din("convw", (128, 24))
    qk_d = din("qkn", (128, 2))
    wo0_d = din("w_out0", (D, D))
    wqkv_d = din("w_qkv", (D, 3 * D))
    wo1_d = din("w_out1", (D, D))
    w1_d = din("w1", (2, D, 4 * D))
    w2_d = din("w2", (2, 4 * D, D))
    cm_d = din("cm", (16, 128, 1408))
    rm_d = din("rm", (2, 8 * 512))
    lsel_d = din("lsel", (2, 128))
    cos_d = din("cos", (128, 2048))
    sin_d = din("sin", (128, 2048))
    prot_d = din("prot", (128, 128))
    ident_d = din("ident", (128, 128))
    mlr_d = din("mlr", (128, 2))

    yp_d = dout("yp", (1024, D))
    ys_d = dout("ys", (512, D))
    ak_d = dout("ak", (1024, 256))
    av_d = dout("av", (1024, 256))
    nk_d = dout("nk", (1024, D))
    nv_d = dout("nv", (1024, D))

    stack = contextlib.ExitStack()
    with stack:
        P = Prog(nc, stack)

        def sb(name, shape, dt):
            return stack.enter_context(nc.sbuf_tensor("sb_" + name, list(shape), dt))

        xres = sb("xres", (128, 16, 1024), F32)
        hbuf = sb("hbuf", (128, 16, 1024), BF16)
        mixb = sb("mixb", (128, 8, 1024), BF16)
        wsl = [sb("wsl%d" % i, (128, 4096), BF16) for i in range(4)]
        scrA = sb("scrA", (128, 4096), F32)
        scrB = sb("scrB", (128, 10496), BF16)
        ropeb = sb("ropeb", (128, 1960), F32)
        ident = sb("ident", (128, 128), F32)
        prot = sb("prot", (128, 128), F32)
        onesD = sb("onesD", (128, 128), BF16)
        onesH = sb("onesH", (128, 128), BF16)
        ones1 = sb("ones1", (128, 128), BF16)
        lsel = sb("lsel", (2, 128), BF16)
        condT = sb("condT", (128, 16, 2), F32)
        sil = sb("sil", (128, 16, 2), BF16)
        modb = sb("modb", (128, 2, 96), F32)
        modv = sb("modv", (128, 2, 96, 2), F32)
        g1 = sb("g1", (128, 2, 16), F32)
        g2 = sb("g2", (128, 2, 16), F32)
        gf = sb("gf", (128, 16), F32)
        convw = sb("convw", (128, 8, 3), F32)
        qkn = sb("qkn", (128, 2), F32)
        mlr = sb("mlr", (128, 2), F32)
        avec = sb("avec", (128, 2, 2, 2, 16), F32)
        sqr = sb("sqr", (128, 2, 512), BF16)
        rsr = sb("rsr", (128, 2, 512), F32)
        tmpr = sb("tmpr", (128, 3, 512), F32)
        pr = sb("pr", (128, 2, 512), BF16)

        dummy = sb("dummy", (128, 4), F32)
        epsT = sb("epsT", (128, 4), F32)
        psall = stack.enter_context(nc.psum_tensor("psall", [128, 8, 512], F32))
        ps = [psall[:, i, :] for i in range(8)]
        ps_t = [Tok(excl=True) for _ in range(8)]
        bank_ctr = [0]

        reserved = set()
        bank_of = {}

        def bank(hold=False):
            while True:
                b = bank_ctr[0] % 8
                bank_ctr[0] += 1
                if b not in reserved:
                    break
            if hold:
                reserved.add(b)
            bank_of[id(ps[b])] = b
            return ps[b], ps_t[b]

        pair_ctr = [0]

        def bank_pair(hold=False):
            while True:
                p = pair_ctr[0] % 4
                pair_ctr[0] += 1
                if 2 * p not in reserved and 2 * p + 1 not in reserved:
                    break
            if hold:
                reserved.add(2 * p)
                reserved.add(2 * p + 1)
            for b in (2 * p, 2 * p + 1):
                bank_of[id(ps[b])] = b
            return p, psall[:, 2 * p:2 * p + 2, :], [(ps[2 * p], ps_t[2 * p]), (ps[2 * p + 1], ps_t[2 * p + 1])]

        def release(pb):
            reserved.discard(bank_of[id(pb)])

        t_x = [[Tok() for _ in range(2)] for _ in range(16)]
        t_h = [Tok(), Tok()]
        t_mix = [Tok(), Tok()]
        t_w = [Tok() for _ in range(4)]
        w_slots = [P.slot() for _ in range(4)]
        w_ctr = [0]
        t_scrA = [Tok() for _ in range(12)]
        t_scrB = [Tok() for _ in range(8)]
        t_rope = Tok()
        t_const = Tok()
        t_mod = Tok()
        t_avec = Tok()
        t_sq = [Tok() for _ in range(4)]
        t_rs = [Tok() for _ in range(2)]
        t_tmp = [Tok() for _ in range(3)]
        t_pr = [Tok() for _ in range(4)]
        sq_c = [0]
        rs_c = [0]
        tmp_c = [0]
        pr_c = [0]

        t_dummy = Tok()

        def ckpt(name):
            if stop is not None and name == stop:
                raise _Stop()

        def fence(toks):
            P.op("dve", lambda e: e.memset(dummy[:, :], 0.0), writes=list(toks) + [t_dummy])

        def ring(ctr, n):
            i = ctr[0] % n
            ctr[0] += 1
            return i

        s_const = P.slot()
        s_in = [P.slot() for _ in range(4)]
        s_out = [P.slot() for _ in range(6)]
        s_rope = P.slot()
        s_rm = P.slot()

        def wload(parts):
            i = w_ctr[0] % 4
            w_ctr[0] += 1
            for dfn, src in parts:
                dst = dfn(wsl[i])
                P.dma("pool", (lambda e, dst=dst, src=src: e.dma_start(out=dst, in_=src)),
                      w_slots[i], writes=[t_w[i]])
            return wsl[i], t_w[i]

        def v3(t, k, n):
            return t[:, 0:k * n].rearrange("p (k n) -> p k n", k=k)

        def wcols(w_ap, c0, n):
            src = w_ap.rearrange("(k p) n -> p k n", p=128)[:, :, c0:c0 + n]
            return [(lambda t, n=n: v3(t, 16, n), src)]

        def wrows(w_ap, r0, kk, c0, n):
            src = w_ap[r0:r0 + kk * 128, c0:c0 + n].rearrange("(k p) n -> p k n", p=128)
            return [(lambda t, kk=kk, n=n: v3(t, kk, n), src)]

        def cload(dst, src, q="sp"):
            P.dma(q, (lambda e, dst=dst, src=src: e.dma_start(out=dst, in_=src)), s_const, writes=[t_const])

        cload(ident[:, :], ident_d)
        cload(prot[:, :], prot_d)
        cload(condT[:, :, :], cond_d.rearrange("p (k c) -> p k c", c=2))
        cload(modb[:, :, :], modb_d.rearrange("p (l j) -> p l j", l=2))
        cload(g1[:, :, :], g1_d.rearrange("p (l j) -> p l j", l=2))
        cload(g2[:, :, :], g2_d.rearrange("p (l j) -> p l j", l=2))
        cload(gf[:, :], gf_d)
        cload(convw[:, :, :], convw_d.rearrange("p (c t) -> p c t", t=3))
        cload(qkn[:, :], qk_d)
        cload(mlr[:, :], mlr_d)
        s_c2 = P.slot()
        t_c2 = Tok()
        P.dma("pool", lambda e: e.dma_start(out=lsel[:, :], in_=lsel_d), s_c2, writes=[t_c2])
        t_ones = Tok()
        P.op("dve", lambda e: e.memset(onesD[:, :], 1.0 / D), writes=[t_ones])
        P.op("dve", lambda e: e.memset(onesH[:, :], 1.0 / 128), writes=[t_ones])
        P.op("dve", lambda e: e.memset(ones1[:, :], 1.0), writes=[t_ones])
        P.op("dve", lambda e: e.memset(epsT[:, :], EPS), writes=[t_ones])

        _skip_mod = (stop == "consts")
        _stop_mod = (stop == "mod")
        t_modsec = [[Tok() for _ in range(6)] for _ in range(2)]
        t_avs = [[Tok(), Tok()] for _ in range(2)]
        P.op("act", lambda e: e.activation(out=sil[:, :, :], in_=condT[:, :, :], func=AF.Silu),
             reads=[t_const], writes=[t_mod])
        mod_queue = [(l, jb) for l in range(2) for jb in range(48)]

        def mod_block(l, jb):
            sec = jb // 8
            wt, wtok = wload(wcols(modw_d[l], jb * 256, 256))
            wv = v3(wt, 16, 256)
            pb, pt = bank()
            for jj in range(2):
                for k in range(16):
                    P.op("pe", (lambda e, pb=pb, wv=wv, jj=jj, k=k: e.matmul(
                        pb[:, jj * 2:jj * 2 + 2], wv[:, k, jj * 128:(jj + 1) * 128], sil[:, k, :],
                        start=(k == 0), stop=(k == 15))), reads=[wtok, t_mod], writes=[pt])
            for jj in range(2):
                P.op("dve", (lambda e, pb=pb, l=l, jb=jb, jj=jj: e.tensor_scalar(
                    out=modv[:, l, jb * 2 + jj, :], in0=pb[:, jj * 2:jj * 2 + 2],
                    scalar1=modb[:, l, jb * 2 + jj:jb * 2 + jj + 1], scalar2=None, op0=ALU.add)),
                     reads=[pt, t_const], writes=[t_modsec[l][sec]])
            if jb % 8 == 7 and sec in (1, 4):
                ni, gg, off = (0, g1, 16) if sec == 1 else (1, g2, 64)
                for ci in range(2):
                    P.op("dve", (lambda e, l=l, ci=ci, ni=ni, gg=gg, off=off: e.scalar_tensor_tensor(
                        out=avec[:, l, ci, ni, :], in0=modv[:, l, off:off + 16, ci], scalar=1.0,
                        in1=gg[:, l, :], op0=ALU.add, op1=ALU.mult)),
                         reads=[t_modsec[l][sec], t_const], writes=[t_avs[l][ni]])

        def pump(n):
            for _ in range(n):
                if mod_queue and not _skip_mod:
                    mod_block(*mod_queue.pop(0))

        def need_mod(l, sec):
            while mod_queue and mod_queue[0] <= (l, sec * 8 + 7) and not _skip_mod:
                mod_block(*mod_queue.pop(0))

        if _stop_mod:
            need_mod(1, 5)

        def mvec(l, ci, j):
            return modv[:, l, j, ci:ci + 1]

        stage = [scrA[:, 0:2048], scrA[:, 2048:4096]]
        stage_ctr = [0]

        def load_xT(dst, dtoks_fn, dcol0, src_rows, ntok):
            r0 = 0
            ei = 0
            while r0 < ntok:
                n = min(128, ntok - r0)
                si = stage_ctr[0] % 2
                stage_ctr[0] += 1
                st, stt = stage[si], t_scrA[si]
                P.dma("sp", (lambda e, st=st, r0=r0, n=n: e.dma_start(out=st[:n, :], in_=src_rows[r0:r0 + n, :])),
                      s_in[si], writes=[stt])
                for quad in range(4):
                    pb, pt = bank()
                    for cc in range(4):
                        c = quad * 4 + cc
                        P.op("pe", (lambda e, pb=pb, st=st, cc=cc, c=c, n=n: e.transpose(
                            pb[:, cc * 128:cc * 128 + n], st[:n, c * 128:(c + 1) * 128], ident[:n, :n])),
                             reads=[stt, t_const], writes=[pt])
                    eng = "act" if ei % 2 == 0 else "dve"
                    ei += 1
                    dv = dst[:, quad * 4:quad * 4 + 4, dcol0 + r0:dcol0 + r0 + n]
                    sv = pb[:, :].rearrange("p (c t) -> p c t", c=4)[:, :, 0:n]
                    wt = dtoks_fn(quad, dcol0 + r0, n)
                    if eng == "act":
                        P.op("act", (lambda e, dv=dv, sv=sv: e.activation(out=dv, in_=sv, func=AF.Copy)),
                             reads=[pt], writes=wt)
                    else:
                        P.op("dve", (lambda e, dv=dv, sv=sv: e.tensor_copy(out=dv, in_=sv)),
                             reads=[pt], writes=wt)
                r0 += n

        def rstd_tile(xsrc_fn, xtoks, tn, ones_t, nchunks):
            pb, pt = bank()
            for c in range(nchunks):
                qi = ring(sq_c, 2)
                src = xsrc_fn(c)
                if c % 3 == 2:
                    P.op("dve", (lambda e, qi=qi, src=src: e.tensor_tensor(out=sqr[:, qi, 0:tn], in0=src, in1=src, op=ALU.mult)),
                         reads=xtoks(c), writes=[t_sq[qi]])
                else:
                    P.op("act", (lambda e, qi=qi, src=src: e.activation(out=sqr[:, qi, 0:tn], in_=src, func=AF.Square)),
                         reads=xtoks(c), writes=[t_sq[qi]])
                P.op("pe", (lambda e, pb=pb, qi=qi, c=c: e.matmul(
                    pb[:, 0:tn], ones_t[:, :], sqr[:, qi, 0:tn], start=(c == 0), stop=(c == nchunks - 1))),
                     reads=[t_sq[qi], t_ones], writes=[pt])
            ri = ring(rs_c, 2)
            P.op("act", (lambda e, pb=pb, ri=ri: e.activation(
                out=rsr[:, ri, 0:tn], in_=pb[:, 0:tn], func=AF.Ln, bias=epsT[:, 0:1], scale=1.0)),
                 reads=[pt, t_ones], writes=[t_rs[ri]])
            P.op("act", (lambda e, ri=ri: e.activation(
                out=rsr[:, ri, 0:tn], in_=rsr[:, ri, 0:tn], func=AF.Exp, scale=-0.5)),
                 reads=[t_rs[ri]], writes=[t_rs[ri]])
            return rsr[:, ri, 0:tn], t_rs[ri]

        def norm_h(l, ci, ni, xt0, ht0, tn, xti, hti):
            boff = 0 if ni == 0 else 48
            rs, rst = rstd_tile(lambda c: xres[:, c, xt0:xt0 + tn], lambda c: [t_x[c][xti]], tn, onesD, 16)
            need_mod(l, 1 if ni == 0 else 4)
            tsh = t_modsec[l][0 if ni == 0 else 3]
            tav = t_avs[l][ni]
            for c in range(16):
                ti = ring(tmp_c, 3)
                P.op("dve", (lambda e, c=c, ti=ti: e.scalar_tensor_tensor(
                    out=tmpr[:, ti, 0:tn], in0=xres[:, c, xt0:xt0 + tn], scalar=avec[:, l, ci, ni, c:c + 1],
                    in1=rs, op0=ALU.mult, op1=ALU.mult)), reads=[t_x[c][xti], rst, tav], writes=[t_tmp[ti]])
                P.op("act", (lambda e, c=c, ti=ti: e.activation(
                    out=hbuf[:, c, ht0:ht0 + tn], in_=tmpr[:, ti, 0:tn], func=AF.Identity,
                    bias=mvec(l, ci, boff + c), scale=1.0)), reads=[t_tmp[ti], tsh], writes=[t_h[hti]])

        def proj_fm(wv, wtok, col0, tiles, hts, consume):
            for ti, (h0, tn) in enumerate(tiles):
                pb, pt = bank()
                for k in range(16):
                    P.op("pe", (lambda e, pb=pb, k=k, h0=h0, tn=tn: e.matmul(
                        pb[:, 0:tn], wv[:, k, col0:col0 + 128], hbuf[:, k, h0:h0 + tn],
                        start=(k == 0), stop=(k == 15))), reads=[wtok, t_h[hts[ti]]], writes=[pt])
                consume(pb, pt, ti)

        def headnorm(pb, pt, tn, gcol):
            qi = ring(sq_c, 2)
            P.op("act", (lambda e, qi=qi: e.activation(out=sqr[:, qi, 0:tn], in_=pb[:, 0:tn], func=AF.Square)),
                 reads=[pt], writes=[t_sq[qi]])
            pb2, pt2 = bank()
            P.op("pe", (lambda e, qi=qi: e.matmul(pb2[:, 0:tn], onesH[:, :], sqr[:, qi, 0:tn], start=True, stop=True)),
                 reads=[t_sq[qi], t_ones], writes=[pt2])
            ri = ring(rs_c, 2)
            P.op("act", (lambda e, ri=ri: e.activation(
                out=rsr[:, ri, 0:tn], in_=pb2[:, 0:tn], func=AF.Ln, bias=epsT[:, 0:1], scale=1.0)),
                 reads=[pt2, t_ones], writes=[t_rs[ri]])
            P.op("act", (lambda e, ri=ri: e.activation(
                out=rsr[:, ri, 0:tn], in_=rsr[:, ri, 0:tn], func=AF.Exp, scale=-0.5)),
                 reads=[t_rs[ri]], writes=[t_rs[ri]])

            def write(out_ap, wtoks):
                P.op("dve", (lambda e: e.scalar_tensor_tensor(
                    out=out_ap, in0=pb[:, 0:tn], scalar=qkn[:, gcol:gcol + 1], in1=rsr[:, ri, 0:tn],
                    op0=ALU.mult, op1=ALU.mult)), reads=[pt, t_rs[ri], t_const], writes=wtoks)
            return write

        def rope(src_tmp_i, tn, tab0, out_ap, wtoks):
            src = tmpr[:, src_tmp_i, 0:tn]
            pb, pt = bank()
            P.op("pe", (lambda e: e.matmul(pb[:, 0:tn], prot[:, :], src, start=True, stop=True)),
                 reads=[t_tmp[src_tmp_i], t_const], writes=[pt])
            t2 = ring(tmp_c, 3)
            P.op("dve", (lambda e: e.tensor_tensor(out=tmpr[:, t2, 0:tn], in0=pb[:, 0:tn],
                                                   in1=ropeb[:, 992 + tab0:992 + tab0 + tn], op=ALU.mult)),
                 reads=[pt, t_rope], writes=[t_tmp[t2]])
            P.op("dve", (lambda e: e.tensor_tensor(out=src, in0=src, in1=ropeb[:, tab0:tab0 + tn], op=ALU.mult)),
                 reads=[t_tmp[src_tmp_i], t_rope], writes=[t_tmp[src_tmp_i]])
            P.op("dve", (lambda e: e.tensor_tensor(out=out_ap, in0=src, in1=tmpr[:, t2, 0:tn], op=ALU.add)),
                 reads=[t_tmp[src_tmp_i], t_tmp[t2]], writes=wtoks)

        def wout_part(w_ap, r0, kk, l, ci, gate_off, mix_fn, mtoks, tiles_x, tiles_m, xtis, npump=2):
            need_mod(l, 2)
            for cb in range(4):
                pump(npump)
                wt, wtok = wload(wrows(w_ap, r0, kk, cb * 512, 512))
                wv = v3(wt, kk, 512)
                tl_ = list(enumerate(zip(tiles_x, tiles_m)))
                if cb == 3:
                    order = [(oc4, t_) for t_ in tl_ for oc4 in range(4)]
                else:
                    order = [(oc4, t_) for oc4 in range(4) for t_ in tl_]
                for oc4, (ti, ((x0, tn), (m0, _))) in order:
                    oc = cb * 4 + oc4
                    if True:
                        pb, pt = bank()
                        for k in range(kk):
                            P.op("pe", (lambda e, pb=pb, k=k, oc4=oc4, m0=m0, tn=tn, wv=wv: e.matmul(
                                pb[:, 0:tn], wv[:, k, oc4 * 128:(oc4 + 1) * 128], mix_fn(k, m0, tn),
                                start=(k == 0), stop=(k == kk - 1))), reads=[wtok] + mtoks(ti), writes=[pt])
                        xt = t_x[oc][xtis[ti]]
                        P.op("dve", (lambda e, pb=pb, oc=oc, x0=x0, tn=tn: e.scalar_tensor_tensor(
                            out=xres[:, oc, x0:x0 + tn], in0=pb[:, 0:tn], scalar=mvec(l, ci, gate_off + oc),
                            in1=xres[:, oc, x0:x0 + tn], op0=ALU.mult, op1=ALU.add)),
                             reads=[pt, t_modsec[l][2], xt], writes=[xt])

        def mlp(l, ci, tiles_x, tiles_h, xtis, htis):
            hid = scrB[:, 0:4096].rearrange("p (r c t) -> p r c t", r=4, c=2)
            need_mod(l, 5)
            for jp in range(16):
                w1s = []
                for jj in range(2):
                    jb = jp * 2 + jj
                    w1t, w1tok = wload(wcols(w1_d[l], jb * 256, 256))
                    w1s.append((v3(w1t, 16, 256), w1tok))
                for ti, ((x0, tn), (h0, _)) in enumerate(zip(tiles_x, tiles_h)):
                    for jj in range(2):
                        w1v, w1tok = w1s[jj]
                        hr = jj * 2 + ti
                        for hc in range(2):
                            pb, pt = bank()
                            for k in range(16):
                                P.op("pe", (lambda e, pb=pb, k=k, hc=hc, h0=h0, tn=tn, w1v=w1v: e.matmul(
                                    pb[:, 0:tn], w1v[:, k, hc * 128:(hc + 1) * 128], hbuf[:, k, h0:h0 + tn],
                                    start=(k == 0), stop=(k == 15))), reads=[w1tok, t_h[htis[ti]]], writes=[pt])
                            t1 = ring(tmp_c, 3)
                            P.op("act", (lambda e, pb=pb, t1=t1, tn=tn: e.activation(
                                out=tmpr[:, t1, 0:tn], in_=pb[:, 0:tn], func=AF.Relu)), reads=[pt], writes=[t_tmp[t1]])
                            P.op("act", (lambda e, t1=t1, hr=hr, hc=hc, tn=tn: e.activation(
                                out=hid[:, hr, hc, 0:tn], in_=tmpr[:, t1, 0:tn], func=AF.Square)),
                                 reads=[t_tmp[t1]], writes=[t_scrB[hr]])
                pump(1)
                w2s = []
                for jj in range(2):
                    jb = jp * 2 + jj
                    w2t, w2tok = wload(wrows(w2_d[l], jb * 256, 2, 0, 2048))
                    w2s.append((v3(w2t, 2, 2048), w2tok))
                for ti, ((x0, tn), (h0, _)) in enumerate(zip(tiles_x, tiles_h)):
                    for oc in range(16):
                        pb, pt = bank()
                        for jj in range(2):
                            w2v, w2tok = w2s[jj]
                            hr = jj * 2 + ti
                            for hc in range(2):
                                P.op("pe", (lambda e, pb=pb, oc=oc, hc=hc, hr=hr, tn=tn, w2v=w2v, jj=jj: e.matmul(
                                    pb[:, 0:tn], w2v[:, hc, oc * 128:(oc + 1) * 128], hid[:, hr, hc, 0:tn],
                                    start=(jj == 0 and hc == 0), stop=(jj == 1 and hc == 1))),
                                     reads=[w2tok, t_scrB[hr]], writes=[pt])
                        xt = t_x[oc][xtis[ti]]
                        P.op("dve", (lambda e, pb=pb, oc=oc, x0=x0, tn=tn: e.scalar_tensor_tensor(
                            out=xres[:, oc, x0:x0 + tn], in0=pb[:, 0:tn], scalar=mvec(l, ci, 80 + oc),
                            in1=xres[:, oc, x0:x0 + tn], op0=ALU.mult, op1=ALU.add)),
                             reads=[pt, t_modsec[l][5], xt], writes=[xt])

        def attn_core(chunks, tn, o_out, big=False):
            nch = len(chunks)
            groups = []
            i = 0
            while i < nch:
                if (big and i + 1 < nch and chunks[i][0] == 128 and chunks[i + 1][0] == 128
                        and chunks[i][2] is None and chunks[i + 1][2] is None):
                    groups.append([i, i + 1])
                    i += 2
                else:
                    groups.append([i])
                    i += 1
            if big:
                LA = 2
                pview = lambda g: prA6[:, 2 * (g % 3):2 * (g % 3) + 2, :]
                ptok = lambda g: t_scrA[8 + g % 3]
                _, _, hp = bank_pair(hold=True)
                (pO, ptO), (pD, ptD) = hp
            else:
                LA = 1
                pview = lambda g: pr[:, g % 2:g % 2 + 1, :]
                ptok = lambda g: t_pr[g % 2]
                pO, ptO = bank(hold=True)
                pD, ptD = bank(hold=True)
            base = pr_c[0]
            pr_c[0] += len(groups)
            for gi_ in range(len(groups) + LA):
                if gi_ < len(groups):
                    g = groups[gi_]
                    pv_, ptk = pview(base + gi_), ptok(base + gi_)
                    if len(g) == 2:
                        _, pview2, bl = bank_pair()
                        for (pb, pt), ci_ in zip(bl, g):
                            chunks[ci_][1](pb, pt)
                        P.op("act", (lambda e, pview2=pview2, pv_=pv_: e.activation(
                            out=pv_[:, :, 0:tn], in_=pview2[:, :, 0:tn], func=AF.Exp, scale=SCALE)),
                             reads=[bl[0][1], bl[1][1]], writes=[ptk])
                    else:
                        n, score_fn, bias_ap, v_ap, vtoks = chunks[g[0]]
                        pb, pt = bank()
                        score_fn(pb, pt)
                        if bias_ap is None:
                            P.op("act", (lambda e, pb=pb, pv_=pv_, n=n: e.activation(
                                out=pv_[:n, 0, 0:tn], in_=pb[:n, 0:tn], func=AF.Exp, scale=SCALE)),
                                 reads=[pt], writes=[ptk])
                        else:
                            t1 = ring(tmp_c, 3)
                            P.op("dve", (lambda e, pb=pb, t1=t1, n=n, bias_ap=bias_ap: e.scalar_tensor_tensor(
                                out=tmpr[:n, t1, 0:tn], in0=pb[:n, 0:tn], scalar=SCALE, in1=bias_ap,
                                op0=ALU.mult, op1=ALU.add)), reads=[pt, t_rope], writes=[t_tmp[t1]])
                            P.op("act", (lambda e, pv_=pv_, t1=t1, n=n: e.activation(
                                out=pv_[:n, 0, 0:tn], in_=tmpr[:n, t1, 0:tn], func=AF.Exp)),
                                 reads=[t_tmp[t1]], writes=[ptk])
                gj = gi_ - LA
                if gj >= 0:
                    pv_, ptk = pview(base + gj), ptok(base + gj)
                    for k_, j in enumerate(groups[gj]):
                        n, _, _, v_ap, vtoks = chunks[j]
                        P.op("pe", (lambda e, pv_=pv_, n=n, v_ap=v_ap, j=j, k_=k_: e.matmul(
                            pO[:, 0:tn], v_ap, pv_[:n, k_, 0:tn], start=(j == 0), stop=(j == nch - 1))),
                             reads=[ptk] + vtoks, writes=[ptO])
                        P.op("pe", (lambda e, pv_=pv_, n=n, j=j, k_=k_: e.matmul(
                            pD[:, 0:tn], ones1[:n, :], pv_[:n, k_, 0:tn], start=(j == 0), stop=(j == nch - 1))),
                             reads=[ptk, t_ones], writes=[ptD])
            ri = ring(rs_c, 2)
            P.op("act", (lambda e, ri=ri: e.activation(out=rsr[:, ri, 0:tn], in_=pD[:, 0:tn], func=AF.Ln)),
                 reads=[ptD], writes=[t_rs[ri]])
            P.op("act", (lambda e, ri=ri: e.activation(out=rsr[:, ri, 0:tn], in_=rsr[:, ri, 0:tn], func=AF.Exp, scale=-1.0)),
                 reads=[t_rs[ri]], writes=[t_rs[ri]])
            o_out(pO, ptO, rsr[:, ri, 0:tn], t_rs[ri])
            release(pO)
            release(pD)

        prA6 = scrA[:, 2048:3584].bitcast(BF16).rearrange("p (r t) -> p r t", r=6)

        def final_out(out_d, x0, ntok, xti_of):
            t0 = 0
            while t0 < ntok:
                tn = min(512, ntok - t0)
                xti = xti_of(t0)
                rs, rst = rstd_tile(lambda c: xres[:, c, x0 + t0:x0 + t0 + tn], lambda c: [t_x[c][xti]], tn, onesD, 16)
                for c in range(16):
                    P.op("dve", (lambda e, c=c, t0=t0, tn=tn, rs=rs: e.scalar_tensor_tensor(
                        out=xres[:, c, x0 + t0:x0 + t0 + tn], in0=xres[:, c, x0 + t0:x0 + t0 + tn],
                        scalar=gf[:, c:c + 1], in1=rs, op0=ALU.mult, op1=ALU.mult)),
                         reads=[t_x[c][xti], rst, t_const], writes=[t_x[c][xti]])
                for tc in range(tn // 128):
                    si = stage_ctr[0] % 2
                    stage_ctr[0] += 1
                    st, stt = stage[si], t_scrA[si]
                    for quad in range(4):
                        pb, pt = bank()
                        for cc in range(4):
                            c = quad * 4 + cc
                            P.op("pe", (lambda e, pb=pb, cc=cc, c=c, tc=tc, t0=t0: e.transpose(
                                pb[:, cc * 128:(cc + 1) * 128],
                                xres[:, c, x0 + t0 + tc * 128:x0 + t0 + (tc + 1) * 128], ident[:, :])),
                                 reads=[t_x[c][xti], t_const], writes=[pt])
                        if quad % 2 == 0:
                            P.op("act", (lambda e, pb=pb, st=st, quad=quad: e.activation(
                                out=st[:, quad * 512:(quad + 1) * 512], in_=pb[:, :], func=AF.Copy)),
                                 reads=[pt], writes=[stt])
                        else:
                            P.op("dve", (lambda e, pb=pb, st=st, quad=quad: e.tensor_copy(
                                out=st[:, quad * 512:(quad + 1) * 512], in_=pb[:, :])), reads=[pt], writes=[stt])
                    r = t0 + tc * 128
                    P.dma("sp", (lambda e, st=st, r=r: e.dma_start(out=out_d[r:r + 128, :], in_=st[:, :])),
                          s_out[si], reads=[stt])
                t0 += tn

        for c in range(16):
            t_x[c].append(Tok())

        def fence_own():
            P.op("dve", lambda e: e.memset(dummy[:, :], 0.0),
                 writes=[t_x[c][i] for c in range(16) for i in range(3)] + [t_dummy])

        def run_group(gi):
            is_s = gi == 1
            ci = gi
            T = 962 if is_s else 1024
            tiles = [(0, 512), (512, T - 512)]
            xt_all = lambda quad, c0, n: [t_x[quad * 4 + cc][c0 // 512] for cc in range(4)] + \
                ([t_x[quad * 4 + cc][(c0 + n - 1) // 512] for cc in range(4)] if (c0 + n - 1) // 512 != c0 // 512 else [])

            fence(t_scrA)
            fence(t_scrB)
            fence(t_mix)
            if not is_s:
                load_xT(xres, xt_all, 0, xp_d, T)
            need_mod(0, 1)

            ckpt("g%d_load" % gi)
            l = 0
            kT = scrB[:, 0:5120].rearrange("p (g t) -> p g t", g=2)
            vtok = scrB[:, 5120:5120 + 21 * 256].rearrange("p (c n) -> p c n", n=256)
            t_kT, t_vt = t_scrB[4], t_scrB[5]
            koff = 512 if is_s else 0
            if is_s:
                kchunks = [(i * 128, 128, i) for i in range(4)]
                kchunks += [(512 + i * 128, 128, 4 + i) for i in range(7)] + [(512 + 896, 66, 11)]
                kchunks += [(512 + 962 + i * 128, 128, 12 + i) for i in range(8)] + [(512 + 962 + 1024, 62, 20)]

            wk, wktok = wload(wcols(win_d, 4096, 256))
            wkv = v3(wk, 16, 256)
            wv_, wvtok = wload(wcols(win_d, 4352, 256))
            wvv = v3(wv_, 16, 256)

            mixf = mixb[:, :, :].rearrange("p c t -> p (c t)").bitcast(F32)
            kst = mixf[:, 0:2048].rearrange("p (c n) -> p c n", n=256)
            vst = mixf[:, 2048:4096].rearrange("p (c n) -> p c n", n=256)

            def kv_for_tile(h0, tn, hti, key0, vchunk0, tab0, out_tok0):
                for hh in range(2):
                    pb, pt = bank()
                    for k in range(16):
                        P.op("pe", (lambda e, pb=pb, k=k, hh=hh: e.matmul(
                            pb[:, 0:tn], wkv[:, k, hh * 128:(hh + 1) * 128], hbuf[:, k, h0:h0 + tn],
                            start=(k == 0), stop=(k == 15))), reads=[wktok, t_h[hti]], writes=[pt])
                    wr = headnorm(pb, pt, tn, 1)
                    t1 = ring(tmp_c, 3)
                    wr(tmpr[:, t1, 0:tn], [t_tmp[t1]])
                    if is_s:
                        rope(t1, tn, tab0, kT[:, hh, key0:key0 + tn], [t_kT])
                    else:
                        P.op("act", (lambda e, t1=t1, hh=hh: e.activation(
                            out=kT[:, hh, key0:key0 + tn], in_=tmpr[:, t1, 0:tn], func=AF.Copy)),
                             reads=[t_tmp[t1]], writes=[t_kT])
                        pb2, pt2 = bank()
                        for tc in range(tn // 128):
                            P.op("pe", (lambda e, pb2=pb2, t1=t1, tc=tc: e.transpose(
                                pb2[:, tc * 128:(tc + 1) * 128], tmpr[:, t1, tc * 128:(tc + 1) * 128], ident[:, :])),
                                 reads=[t_tmp[t1], t_const], writes=[pt2])
                        c0 = out_tok0 // 128
                        P.op("dve", (lambda e, pb2=pb2, hh=hh, c0=c0: e.tensor_copy(
                            out=kst[:, c0:c0 + tn // 128, hh * 128:(hh + 1) * 128],
                            in_=pb2[:, 0:tn].rearrange("p (c d) -> p c d", d=128))), reads=[pt2], writes=[t_mix[0], t_mix[1]])
                ckpt("g%d_kvK" % gi)
                c = 0
                r0 = 0
                while r0 < tn:
                    n = min(128, tn - r0)
                    pb, pt = bank()
                    for k in range(16):
                        P.op("pe", (lambda e, pb=pb, k=k, r0=r0, n=n: e.matmul(
                            pb[:n, 0:256], hbuf[:, k, h0 + r0:h0 + r0 + n], wvv[:, k, :],
                            start=(k == 0), stop=(k == 15))), reads=[wvtok, t_h[hti]], writes=[pt])
                    vc = vchunk0 + c
                    P.op("act", (lambda e, pb=pb, vc=vc, n=n: e.activation(
                        out=vtok[:n, vc, :], in_=pb[:n, 0:256], func=AF.Copy)), reads=[pt], writes=[t_vt])
                    if not is_s:
                        oc_ = (out_tok0 + r0) // 128
                        P.op("dve", (lambda e, pb=pb, oc_=oc_: e.tensor_copy(out=vst[:, oc_, :], in_=pb[:, 0:256])),
                             reads=[pt], writes=[t_mix[0], t_mix[1]])
                    c += 1
                    r0 += n

            if is_s:
                cks = tmpr[:, 0:2, :].rearrange("p a (b n) -> p (a b) n", n=256)
                P.dma("sp", lambda e: e.dma_start(out=cks, in_=ck0_d.rearrange("(c p) n -> p c n", p=128)),
                      s_in[2], writes=[t_tmp[0], t_tmp[1]])
                for hh in range(2):
                    pb, pt = bank()
                    for c in range(4):
                        P.op("pe", (lambda e, pb=pb, c=c, hh=hh: e.transpose(
                            pb[:, c * 128:(c + 1) * 128], cks[:, c, hh * 128:(hh + 1) * 128], ident[:, :])),
                             reads=[t_tmp[0], t_tmp[1], t_const], writes=[pt])
                    P.op("act", (lambda e, pb=pb, hh=hh: e.activation(out=kT[:, hh, 0:512], in_=pb[:, :], func=AF.Copy)),
                         reads=[pt], writes=[t_kT])
                P.dma("pool", lambda e: e.dma_start(out=vtok[:, 0:4, :], in_=cv0_d.rearrange("(c p) n -> p c n", p=128)),
                      s_in[3], writes=[t_vt])
                rest_tiles = [(962, 0, 512, 12), (1474, 512, 512, 16), (1986, 0, 62, 20)]
                def rest_tail(r, xc, tn, vch):
                    ti_ = xc // 512
                    P.dma("sp", (lambda e, r=r, tn=tn: e.dma_start(out=ropeb[:, 0:tn], in_=cos_d[:, r:r + tn])),
                          s_rope, writes=[t_rope])
                    P.dma("sp", (lambda e, r=r, tn=tn: e.dma_start(out=ropeb[:, 992:992 + tn], in_=sin_d[:, r:r + tn])),
                          s_rope, writes=[t_rope])
                    norm_h(0, ci, 0, xc, xc, tn, ti_, ti_)
                    kv_for_tile(xc, tn, ti_, 512 + r, vch, 0, 0)
                for (r, xc, tn, vch) in rest_tiles[0:2]:
                    load_xT(xres, xt_all, xc, xs_d[r:r + tn, :], tn)
                for (r, xc, tn, vch) in rest_tiles[0:2]:
                    rest_tail(r, xc, tn, vch)
                r, xc, tn, vch = rest_tiles[2]
                load_xT(xres, xt_all, xc, xs_d[r:r + tn, :], tn)
                rest_tail(r, xc, tn, vch)
                P.dma("sp", lambda e: e.dma_start(out=ropeb[:, 0:962], in_=cos_d[:, 0:962]), s_rope, writes=[t_rope])
                P.dma("sp", lambda e: e.dma_start(out=ropeb[:, 992:992 + 962], in_=sin_d[:, 0:962]), s_rope,
                      writes=[t_rope])

            if is_s:
                load_xT(xres, xt_all, 0, xs_d, T)
            for ti, (t0, tn) in enumerate(tiles):
                norm_h(0, ci, 0, t0, t0, tn, ti, ti)

            ckpt("g%d_norm" % gi)
            for ti, (t0, tn) in enumerate(tiles):
                kv_for_tile(t0, tn, ti, koff + t0, (4 if is_s else 0) + t0 // 128, t0, t0)
            ckpt("g%d_kvV" % gi)
            if not is_s:
                P.dma("sp", lambda e: e.dma_start(out=ak_d.rearrange("(c p) n -> p c n", p=128), in_=kst),
                      s_out[2], reads=[t_mix[0], t_mix[1]])
                P.dma("sp", lambda e: e.dma_start(out=av_d.rearrange("(c p) n -> p c n", p=128), in_=vst),
                      s_out[3], reads=[t_mix[0], t_mix[1]])

            ckpt("g%d_kv" % gi)
            nseq, L = (1, 962) if is_s else (4, 256)
            ub = scrA[:, 0:nseq * (L + 2)].rearrange("p (s t) -> p s t", s=nseq)
            vb = [scrA[:, 1040:1040 + T], scrA[:, 2080:2080 + T]]
            t_u, t_v = t_scrA[3], [t_scrA[4], t_scrA[5]]
            P.op("dve", lambda e: e.memset(ub, 0.0), writes=list(t_scrA))
            for cp in range(4):
                for cc in range(2):
                    c = cp * 2 + cc
                    pump(1)
                    wt, wtok = wload([(lambda t: v3(t, 16, 256)[:, :, 0:128],
                                       win_d.rearrange("(k p) n -> p k n", p=128)[:, :, 2048 + c * 128:2048 + (c + 1) * 128]),
                                      (lambda t: v3(t, 16, 256)[:, :, 128:256],
                                       win_d.rearrange("(k p) n -> p k n", p=128)[:, :, 1024 + c * 128:1024 + (c + 1) * 128])])
                    wv = v3(wt, 16, 256)

                    def uview(t0, tn):
                        if is_s:
                            return ub[:, 0, 1 + t0:1 + t0 + tn]
                        return ub[:, t0 // 256:(t0 + tn) // 256, 1:257]

                    def cons_xa(pb, pt, ti):
                        t0, tn = tiles[ti]
                        src = pb[:, 0:tn] if is_s else pb[:, 0:tn].rearrange("p (s t) -> p s t", t=256)
                        P.op("act", (lambda e: e.activation(out=uview(t0, tn), in_=src, func=AF.Copy)),
                             reads=[pt], writes=[t_u])

                    def cons_gc(pb, pt, ti):
                        t0, tn = tiles[ti]
                        src = pb[:, 0:tn] if is_s else pb[:, 0:tn].rearrange("p (s t) -> p s t", t=256)
                        P.op("dve", (lambda e: e.tensor_tensor(out=uview(t0, tn), in0=src, in1=uview(t0, tn), op=ALU.mult)),
                             reads=[pt, t_u], writes=[t_u])
                    proj_fm(wv, wtok, 0, tiles, [0, 1], cons_xa)
                    proj_fm(wv, wtok, 128, tiles, [0, 1], cons_gc)
                    if is_s:
                        P.op("dve", lambda e: e.tensor_scalar(out=ub[:, 0, 257:258], in0=ub[:, 0, 257:258],
                                                              scalar1=mlr[:, 0:1], scalar2=None, op0=ALU.mult),
                             reads=[t_u, t_const], writes=[t_u])
                        P.op("dve", lambda e: e.tensor_scalar(out=ub[:, 0, 770:771], in0=ub[:, 0, 770:771],
                                                              scalar1=mlr[:, 1:2], scalar2=None, op0=ALU.mult),
                             reads=[t_u, t_const], writes=[t_u])
                    v3d = vb[cc].rearrange("p (s t) -> p s t", s=nseq)
                    P.op("dve", (lambda e, c=c, v3d=v3d: e.tensor_scalar(
                        out=v3d, in0=ub[:, :, 1:L + 1], scalar1=convw[:, c, 1:2], scalar2=None, op0=ALU.mult)),
                         reads=[t_u, t_const], writes=[t_v[cc]])
                    P.op("dve", (lambda e, c=c, v3d=v3d: e.scalar_tensor_tensor(
                        out=v3d, in0=ub[:, :, 0:L], scalar=convw[:, c, 0:1], in1=v3d, op0=ALU.mult, op1=ALU.add)),
                         reads=[t_u, t_const, t_v[cc]], writes=[t_v[cc]])
                    P.op("dve", (lambda e, c=c, v3d=v3d: e.scalar_tensor_tensor(
                        out=v3d, in0=ub[:, :, 2:L + 2], scalar=convw[:, c, 2:3], in1=v3d, op0=ALU.mult, op1=ALU.add)),
                         reads=[t_u, t_const, t_v[cc]], writes=[t_v[cc]])
                pump(1)
                wt, wtok = wload(wcols(win_d, cp * 256, 256))
                wv = v3(wt, 16, 256)
                for cc in range(2):
                    c = cp * 2 + cc

                    def cons_gb(pb, pt, ti, c=c, cc=cc):
                        t0, tn = tiles[ti]
                        P.op("dve", (lambda e: e.tensor_tensor(out=mixb[:, c, t0:t0 + tn], in0=pb[:, 0:tn],
                                                               in1=vb[cc][:, t0:t0 + tn], op=ALU.mult)),
                             reads=[pt, t_v[cc]], writes=[t_mix[ti]])
                    proj_fm(wv, wtok, cc * 128, tiles, [0, 1], cons_gb)
            ckpt("g%d_conv" % gi)
            wout_part(wo0_d, 0, 8, 0, ci, 32, lambda k, m0, tn: mixb[:, k, m0:m0 + tn],
                      lambda ti: [t_mix[ti]], tiles, tiles, [0, 1])
            ckpt("g%d_woutA" % gi)

            fence(t_scrA)
            qT = scrA[:, 0:2048].bitcast(BF16).rearrange("p (h t) -> p h t", h=4)
            t_q = t_scrA[6]
            for g in range(2):
                for qb in range(2):
                    pump(1)
                    wt, wtok = wload(wcols(win_d, 3072 + g * 512 + qb * 256, 256))
                    wv = v3(wt, 16, 256)
                    for hh in range(2):
                        hq = qb * 2 + hh

                        def cons_q(pb, pt, ti, hq=hq):
                            t0, tn = tiles[ti]
                            wr = headnorm(pb, pt, tn, 0)
                            if is_s:
                                t1 = ring(tmp_c, 3)
                                wr(tmpr[:, t1, 0:tn], [t_tmp[t1]])
                                rope(t1, tn, t0, qT[:, hq, t0:t0 + tn], [t_q])
                            else:
                                wr(qT[:, hq, t0:t0 + tn], [t_q])
                        proj_fm(wv, wtok, hh * 128, tiles, [0, 1], cons_q)
                if is_s:
                    for hq in range(4):
                        for ti, (t0, tn) in enumerate(tiles):
                            chunks = []
                            for (k0, n, vc) in kchunks:
                                def sfn(pb, pt, k0=k0, n=n, hq=hq, t0=t0, tn=tn, g=g):
                                    P.op("pe", (lambda e: e.matmul(pb[:n, 0:tn], kT[:, g, k0:k0 + n], qT[:, hq, t0:t0 + tn],
                                                                   start=True, stop=True)),
                                         reads=[t_kT, t_q], writes=[pt])
                                chunks.append((n, sfn, None, vtok[:n, vc, g * 128:(g + 1) * 128], [t_vt]))

                            def oout(pO, ptO, rc, rct, hq=hq, t0=t0, tn=tn, ti=ti, g=g):
                                P.op("dve", (lambda e: e.tensor_tensor(out=mixb[:, 4 * g + hq, t0:t0 + tn], in0=pO[:, 0:tn],
                                                                       in1=rc, op=ALU.mult)),
                                     reads=[ptO, rct], writes=[t_mix[ti]])
                            attn_core(chunks, tn, oout, True)
                else:
                    for s in range(4):
                        for hp in range(2):
                            chunks = []
                            for kc in range(2):
                                k0 = s * 256 + kc * 128

                                def sfn(pb, pt, k0=k0, hp=hp, s=s, g=g):
                                    P.op("pe", (lambda e: e.matmul(pb[:, 0:512], kT[:, g, k0:k0 + 128],
                                                                   qT[:, 2 * hp:2 * hp + 2, s * 256:(s + 1) * 256],
                                                                   start=True, stop=True)),
                                         reads=[t_kT, t_q], writes=[pt])
                                chunks.append((128, sfn, None, vtok[:, s * 2 + kc, g * 128:(g + 1) * 128], [t_vt]))

                            def oout(pO, ptO, rc, rct, hp=hp, s=s, g=g):
                                P.op("dve", (lambda e: e.tensor_tensor(
                                    out=mixb[:, 4 * g + 2 * hp:4 * g + 2 * hp + 2, s * 256:(s + 1) * 256],
                                    in0=pO[:, 0:512].rearrange("p (h t) -> p h t", h=2),
                                    in1=rc.rearrange("p (h t) -> p h t", h=2), op=ALU.mult)),
                                     reads=[ptO, rct], writes=[t_mix[s // 2]])
                            attn_core(chunks, 512, oout)
            ckpt("g%d_attn0" % gi)
            wout_part(wo0_d, 1024, 8, 0, ci, 32, lambda k, m0, tn: mixb[:, k, m0:m0 + tn],
                      lambda ti: [t_mix[ti]], tiles, tiles, [0, 1])

            fence(t_scrB)
            for ti, (t0, tn) in enumerate(tiles):
                norm_h(0, ci, 1, t0, t0, tn, ti, ti)
            ckpt("g%d_mlpnorm" % gi)
            mlp(0, ci, tiles, tiles, [0, 1], [0, 1])
            ckpt("g%d_l0" % gi)

            fence(t_scrA)
            fence(t_scrB)
            for ti, (t0, tn) in enumerate(tiles):
                norm_h(1, ci, 0, t0, t0, tn, ti, ti)
            q2 = scrA[:, 0:1024].bitcast(BF16).rearrange("p (h t) -> p h t", h=2)
            t_q2 = t_scrA[6]
            if is_s:
                kT2 = scrB[:, 0:2 * 1474].rearrange("p (h t) -> p h t", h=2)
                vt2 = scrB[:, 3072:3072 + 12 * 256].rearrange("p (c n) -> p c n", n=256)
                cks1 = tmpr[:, 0:2, :].rearrange("p a (b n) -> p (a b) n", n=256)
                rmb = scrB[0:2, 6144:6144 + 4096]
                P.dma("pool", lambda e: e.dma_start(out=rmb, in_=rm_d), s_rm, writes=[t_scrB[6]])
            else:
                kT2 = scrB[:, 0:2048].rearrange("p (h t) -> p h t", h=2)
                vt2 = scrB[:, 3072:3072 + 8 * 256].rearrange("p (c n) -> p c n", n=256)
                kst1 = scrA[:, 1024:2048].rearrange("p (c n) -> p c n", n=256)
                vst1 = scrA[:, 2048:4096].rearrange("p (c n) -> p c n", n=256)
            t_k2, t_v2 = t_scrB[4], t_scrB[5]
            for half in range(2):
                for pair in range(4):
                    hp0 = (half * 4 + pair) * 2
                    col = hp0 * 128
                    pump(3)
                    wq, wqtok = wload(wcols(wqkv_d, col, 256))
                    wqv = v3(wq, 16, 256)
                    wk2, wk2tok = wload(wcols(wqkv_d, 2048 + col, 256))
                    wk2v = v3(wk2, 16, 256)
                    wv2, wv2tok = wload(wcols(wqkv_d, 4096 + col, 256))
                    wv2v = v3(wv2, 16, 256)
                    for hh in range(2):
                        if is_s:
                            pb, pt = bank()
                            for k in range(16):
                                P.op("pe", (lambda e, pb=pb, k=k, hh=hh, wqv=wqv: e.matmul(
                                    pb[:, 0:512], wqv[:, k, hh * 128:(hh + 1) * 128], hbuf[:, k, 257:769],
                                    start=(k == 0), stop=(k == 15))), reads=[wqtok, t_h[0], t_h[1]], writes=[pt])
                            P.op("act", (lambda e, pb=pb, hh=hh: e.activation(out=q2[:, hh, 0:512], in_=pb[:, :], func=AF.Copy)),
                                 reads=[pt], writes=[t_q2])
                        else:
                            def cons_q2(pb, pt, ti, hh=hh):
                                t0, tn = tiles[ti]
                                P.op("act", (lambda e: e.activation(out=q2[:, hh, t0:t0 + tn], in_=pb[:, 0:tn], func=AF.Copy)),
                                     reads=[pt], writes=[t_q2])
                            proj_fm(wqv, wqtok, hh * 128, tiles, [0, 1], cons_q2)
                    if is_s:
                        P.dma("sp", (lambda e, col=col: e.dma_start(
                            out=cks1, in_=ck1_d[:, col:col + 256].rearrange("(c p) n -> p c n", p=128))),
                              s_in[2], writes=[t_tmp[0], t_tmp[1]])
                        for hh in range(2):
                            pb, pt = bank()
                            for c in range(4):
                                P.op("pe", (lambda e, pb=pb, c=c, hh=hh: e.transpose(
                                    pb[:, c * 128:(c + 1) * 128], cks1[:, c, hh * 128:(hh + 1) * 128], ident[:, :])),
                                     reads=[t_tmp[0], t_tmp[1], t_const], writes=[pt])
                            P.op("act", (lambda e, pb=pb, hh=hh: e.activation(out=kT2[:, hh, 0:512], in_=pb[:, :], func=AF.Copy)),
                                 reads=[pt], writes=[t_k2])
                        P.dma("pool", (lambda e, col=col: e.dma_start(
                            out=vt2[:, 0:4, :], in_=cv1_d[:, col:col + 256].rearrange("(c p) n -> p c n", p=128))),
                              s_in[3], writes=[t_v2])
                    for hh in range(2):
                        def cons_k2(pb, pt, ti, hh=hh, col=col):
                            t0, tn = tiles[ti]
                            if is_s:
                                P.op("act", (lambda e: e.activation(out=kT2[:, hh, 512 + t0:512 + t0 + tn], in_=pb[:, 0:tn],
                                                                    func=AF.Copy)), reads=[pt], writes=[t_k2])
                                return
                            t1 = ring(tmp_c, 3)
                            P.op("act", (lambda e: e.activation(out=tmpr[:, t1, 0:tn], in_=pb[:, 0:tn], func=AF.Copy)),
                                 reads=[pt], writes=[t_tmp[t1]])
                            P.op("dve", (lambda e: e.tensor_copy(out=kT2[:, hh, t0:t0 + tn], in_=pb[:, 0:tn])),
                                 reads=[pt], writes=[t_k2])
                            pb2, pt2 = bank()
                            for tc in range(4):
                                P.op("pe", (lambda e, tc=tc: e.transpose(
                                    pb2[:, tc * 128:(tc + 1) * 128], tmpr[:, t1, tc * 128:(tc + 1) * 128], ident[:, :])),
                                     reads=[t_tmp[t1], t_const], writes=[pt2])
                            P.op("dve", (lambda e: e.tensor_copy(
                                out=kst1[:, :, hh * 128:(hh + 1) * 128],
                                in_=pb2[:, :].rearrange("p (c d) -> p c d", d=128))), reads=[pt2], writes=[t_scrA[3]])
                            if hh == 1:
                                P.dma("sp", (lambda e: e.dma_start(
                                    out=nk_d[t0:t0 + 512, col:col + 256].rearrange("(c p) n -> p c n", p=128), in_=kst1)),
                                      s_out[2], reads=[t_scrA[3]])
                        if is_s:
                            proj_fm(wk2v, wk2tok, hh * 128, tiles, [0, 1], cons_k2)
                    if not is_s:
                        for ti in range(2):
                            for hh in range(2):
                                proj_fm(wk2v, wk2tok, hh * 128, [tiles[ti]], [ti],
                                        (lambda pb, pt, _ti, hh=hh, ti=ti, col=col: cons_k2(pb, pt, ti, hh, col)))
                    if is_s:
                        vrows = [(1 + 128 * m, 128 if m < 7 else 64, 4 + m) for m in range(8)]
                    else:
                        vrows = [(128 * m, 128, m) for m in range(8)]
                    for (r0, n, vc) in vrows:
                        pb, pt = bank()
                        for k in range(16):
                            P.op("pe", (lambda e, pb=pb, k=k, r0=r0, n=n, wv2v=wv2v: e.matmul(
                                pb[:n, 0:256], hbuf[:, k, r0:r0 + n], wv2v[:, k, :], start=(k == 0), stop=(k == 15))),
                                 reads=[wv2tok, t_h[0], t_h[1]], writes=[pt])
                        P.op("act", (lambda e, pb=pb, vc=vc, n=n: e.activation(out=vt2[:n, vc, :], in_=pb[:n, 0:256], func=AF.Copy)),
                             reads=[pt], writes=[t_v2])
                        if not is_s:
                            P.op("dve", (lambda e, pb=pb, vc=vc: e.tensor_copy(out=vst1[:, vc, :], in_=pb[:, 0:256])),
                                 reads=[pt], writes=[t_scrA[4]])
                    if not is_s:
                        P.dma("sp", (lambda e, col=col: e.dma_start(
                            out=nv_d[:, col:col + 256].rearrange("(c p) n -> p c n", p=128), in_=vst1)),
                              s_out[3], reads=[t_scrA[4]])
                    pend = []
                    for hh in range(2):
                        mi = pair * 2 + hh
                        if is_s:
                            head = hp0 + hh
                            P.dma("sp", (lambda e, head=head: e.dma_start(out=ropeb[:, 0:1408], in_=cm_d[head])),
                                  s_rope, writes=[t_rope])
                            chunks = []
                            for m in range(8):
                                n = 128 if m < 7 else 64
                                k0 = 512 + 1 + 128 * m

                                def sfn(pb, pt, m=m, n=n, k0=k0, hh=hh):
                                    P.op("pe", (lambda e: e.matmul(pb[:n, 0:512], kT2[:, hh, k0:k0 + n], q2[:, hh, 0:512],
                                                                   start=True, stop=False)),
                                         reads=[t_k2, t_q2], writes=[pt])
                                    P.op("pe", (lambda e: e.matmul(pb[:n, 0:512], lsel[:, 0:n], rmb[:, m * 512:(m + 1) * 512],
                                                                   start=False, stop=True)),
                                         reads=[t_c2, t_scrB[6]], writes=[pt])
                                b0 = (14 - 2 * m) * 64
                                chunks.append((n, sfn, ropeb[:n, b0:b0 + 512], vt2[:n, 4 + m, hh * 128:(hh + 1) * 128], [t_v2]))
                            for c in range(4):
                                def sfn(pb, pt, c=c, hh=hh):
                                    P.op("pe", (lambda e: e.matmul(pb[:, 0:512], kT2[:, hh, c * 128:(c + 1) * 128], q2[:, hh, 0:512],
                                                                   start=True, stop=True)),
                                         reads=[t_k2, t_q2], writes=[pt])
                                chunks.append((128, sfn, None, vt2[:, c, hh * 128:(hh + 1) * 128], [t_v2]))

                            def oout(pO, ptO, rc, rct, mi=mi):
                                P.op("dve", (lambda e: e.tensor_tensor(out=mixb[:, mi, 0:512], in0=pO[:, 0:512], in1=rc, op=ALU.mult)),
                                     reads=[ptO, rct], writes=[t_mix[0]])
                            attn_core(chunks, 512, oout, True)
                        else:
                            for s in range(4):
                                pb, pt = bank()
                                for kc in range(2):
                                    k0 = s * 256 + kc * 128
                                    P.op("pe", (lambda e, pb=pb, kc=kc, k0=k0, hh=hh, s=s: e.matmul(
                                        pb[:, kc * 256:(kc + 1) * 256], kT2[:, hh, k0:k0 + 128],
                                        q2[:, hh, s * 256:(s + 1) * 256], start=True, stop=True)),
                                         reads=[t_k2, t_q2], writes=[pt])
                                ui = ring(pr_c, 2)
                                P.op("act", (lambda e, pb=pb, ui=ui: e.activation(
                                    out=pr[:, ui, :], in_=pb[:, :], func=AF.Exp, scale=SCALE)),
                                     reads=[pt], writes=[t_pr[ui]])

                                def tail(ui=ui, s=s, hh=hh, mi=mi):
                                    pO, ptO = bank()
                                    for kc in range(2):
                                        P.op("pe", (lambda e, kc=kc: e.matmul(
                                            pO[:, 0:256], vt2[:, s * 2 + kc, hh * 128:(hh + 1) * 128],
                                            pr[:, ui, kc * 256:(kc + 1) * 256], start=(kc == 0), stop=(kc == 1))),
                                             reads=[t_pr[ui], t_v2], writes=[ptO])
                                    for kc in range(2):
                                        P.op("pe", (lambda e, kc=kc: e.matmul(
                                            pO[:, 256:512], ones1[:, :], pr[:, ui, kc * 256:(kc + 1) * 256],
                                            start=(kc == 0), stop=(kc == 1))), reads=[t_pr[ui], t_ones], writes=[ptO])
                                    ri = ring(rs_c, 2)
                                    P.op("act", (lambda e: e.activation(out=rsr[:, ri, 0:256], in_=pO[:, 256:512], func=AF.Ln)),
                                         reads=[ptO], writes=[t_rs[ri]])
                                    P.op("act", (lambda e: e.activation(out=rsr[:, ri, 0:256], in_=rsr[:, ri, 0:256],
                                                                        func=AF.Exp, scale=-1.0)),
                                         reads=[t_rs[ri]], writes=[t_rs[ri]])
                                    P.op("dve", (lambda e: e.tensor_tensor(out=mixb[:, mi, s * 256:(s + 1) * 256],
                                                                           in0=pO[:, 0:256], in1=rsr[:, ri, 0:256], op=ALU.mult)),
                                         reads=[ptO, t_rs[ri]], writes=[t_mix[s // 2]])
                                if pend:
                                    pend.pop(0)()
                                pend.append(tail)
                    while pend:
                        pend.pop(0)()
                if is_s:
                    if half == 0:
                        fence_own()
                    wout_part(wo1_d, half * 1024, 8, 1, ci, 32, lambda k, m0, tn: mixb[:, k, m0:m0 + tn],
                              lambda ti: [t_mix[0]], [(257, 512)], [(0, 512)], [2])
                else:
                    wout_part(wo1_d, half * 1024, 8, 1, ci, 32, lambda k, m0, tn: mixb[:, k, m0:m0 + tn],
                              lambda ti: [t_mix[ti]], tiles, tiles, [0, 1])

            ckpt("g%d_attn1" % gi)
            if is_s:
                pass
            return is_s

        def run_tail(gi):
            is_s = gi == 1
            ci = gi
            if not is_s:
                tiles = [(0, 512), (512, 512)]
                for ti, (t0, tn) in enumerate(tiles):
                    norm_h(1, ci, 1, t0, t0, tn, ti, ti)
                fence(t_scrB)
                mlp(1, ci, tiles, tiles, [0, 1], [0, 1])
                fence(t_scrA)
                final_out(yp_d, 0, 1024, lambda t0: t0 // 512)
            else:
                norm_h(1, ci, 1, 257, 0, 512, 2, 0)
                fence(t_scrB)
                mlp(1, ci, [(257, 512)], [(0, 512)], [2], [0])
                fence(t_scrA)
                final_out(ys_d, 257, 512, lambda t0: 2)


        try:
            if _skip_mod or _stop_mod:
                raise _Stop()
            if 0 in groups:
                run_group(0)
                run_tail(0)
            ckpt("g0")
            if 1 in groups:
                run_group(1)
                run_tail(1)
        except _Stop:
            pass

        with nc.Block() as block:
            P.emit(block, s_out)
    return nc


_NC_CACHE = {}


def _fm(v):
    v = np.asarray(v, np.float32)
    return np.ascontiguousarray(v.reshape(-1, 128).T)


def _na_tables(rel_bias):
    H = rel_bias.shape[0]
    tab = np.full((H, 128, 22, 64), NEG, np.float32)
    qc = np.arange(64)
    kc0 = np.clip(qc - 8, 0, 48)
    for half in range(2):
        for kc in range(64):
            p = half * 64 + kc
            inwin = (kc >= kc0) & (kc < kc0 + 16)
            dc = kc - qc + 15
            for j in range(22):
                dr = 17 - j + half
                if 0 <= dr < 15:
                    vals = rel_bias[:, dr, np.clip(dc, 0, 30)]
                    tab[:, p, j, :] = np.where(inwin[None, :], vals, NEG)
    return np.ascontiguousarray(tab.reshape(H, 128, 22 * 64))


def _rm_table(q):
    rm = np.full((2, 8, 8, 64), NEG, np.float32)
    for m in range(8):
        for half in range(2):
            lk = 2 * m + half
            kr = 8 * q - 4 + lk
            for lq in range(4, 12):
                qr = 8 * q - 4 + lq
                kr0 = min(max(qr - 4, 0), 24)
                if 0 <= kr < 32 and kr0 <= kr < kr0 + 8 and lk < 15:
                    rm[half, m, lq - 4, :] = 0.0
    return np.ascontiguousarray(rm.reshape(2, 8 * 512))


def _rope_tables(gtok):
    half = 32
    inv = (10000.0 ** (-np.arange(half, dtype=np.float32) / half)).astype(np.float32)
    row = (gtok // 64).astype(np.float32)
    colp = (gtok % 64).astype(np.float32)
    cos = np.zeros((128, gtok.shape[0]), np.float32)
    sin = np.zeros((128, gtok.shape[0]), np.float32)
    for m in range(128):
        pos = row if m < 64 else colp
        ang = pos * inv[m % 32]
        cos[m] = np.cos(ang.astype(np.float32))
        sin[m] = np.sin(ang.astype(np.float32))
    return cos, sin


def kernel(x_prompt, x_sample, cache_attn_k, cache_attn_v, cache_na_k, cache_na_v, c, c_ctx,
           mod_w, mod_b, norm1_g, norm2_g, ab_w_in, ab_conv_w, ab_q_norm, ab_k_norm, ab_w_out,
           na_w_qkv, na_rel_bias, na_w_out, mlp_w1, mlp_w2, final_norm_g):
    if "nc" not in _NC_CACHE:
        _NC_CACHE["nc"] = build_program()
    nc = _NC_CACHE["nc"]
    in_maps = make_in_maps(x_prompt, x_sample, cache_attn_k, cache_attn_v, cache_na_k, cache_na_v, c, c_ctx,
                           mod_w, mod_b, norm1_g, norm2_g, ab_w_in, ab_conv_w, ab_q_norm, ab_k_norm, ab_w_out,
                           na_w_qkv, na_rel_bias, na_w_out, mlp_w1, mlp_w2, final_norm_g)
    res = run_bass_kernel_spmd(nc, in_maps, core_ids=list(range(8)))
    return gather_outputs(res.results)


def make_in_maps(x_prompt, x_sample, cache_attn_k, cache_attn_v, cache_na_k, cache_na_v, c, c_ctx,
                 mod_w, mod_b, norm1_g, norm2_g, ab_w_in, ab_conv_w, ab_q_norm, ab_k_norm, ab_w_out,
                 na_w_qkv, na_rel_bias, na_w_out, mlp_w1, mlp_w2, final_norm_g):
    f32 = lambda a: np.ascontiguousarray(np.asarray(a, np.float32))

    x_prompt = f32(x_prompt)
    x_sample = f32(x_sample)
    shared = {
        "modw": f32(mod_w),
        "modb": np.ascontiguousarray(np.concatenate([_fm(mod_b[0]), _fm(mod_b[1])], axis=1)),
        "g1": np.ascontiguousarray(np.concatenate([_fm(norm1_g[0]), _fm(norm1_g[1])], axis=1)),
        "g2": np.ascontiguousarray(np.concatenate([_fm(norm2_g[0]), _fm(norm2_g[1])], axis=1)),
        "gf": _fm(final_norm_g),
        "w_in": f32(ab_w_in[0]),
        "convw": np.ascontiguousarray(np.asarray(ab_conv_w[0], np.float32).reshape(3, 8, 128).transpose(2, 1, 0).reshape(128, 24)),
        "qkn": np.ascontiguousarray(np.stack([np.asarray(ab_q_norm[0], np.float32), np.asarray(ab_k_norm[0], np.float32)], axis=1)),
        "w_out0": f32(ab_w_out[0]),
        "w_qkv": f32(na_w_qkv[0]),
        "w_out1": f32(na_w_out[0]),
        "w1": f32(mlp_w1),
        "w2": f32(mlp_w2),
        "cm": _na_tables(np.asarray(na_rel_bias[0], np.float32)),
        "ident": np.eye(128, dtype=np.float32),
    }
    lsel = np.zeros((2, 128), np.float32)
    lsel[0, 0:64] = 1.0
    lsel[1, 64:128] = 1.0
    shared["lsel"] = lsel
    prot = np.zeros((128, 128), np.float32)
    for m in range(128):
        if m % 64 < 32:
            prot[m + 32, m] = -1.0
        else:
            prot[m - 32, m] = 1.0
    shared["prot"] = prot

    in_maps = []
    for core in range(8):
        b, q = core // 4, core % 4
        W0 = 64 * (8 * q - 4)
        gtok = (W0 - 1 + np.arange(2048)) % 2048
        cos, sin = _rope_tables(gtok)
        cond = np.stack([_fm(c_ctx), _fm(c[b])], axis=2).reshape(128, 32)
        mlr = np.ones((128, 2), np.float32)
        if q == 0:
            mlr[:, 0] = 0.0
        if q == 3:
            mlr[:, 1] = 0.0
        m = dict(shared)
        m.update({
            "xp": np.ascontiguousarray(x_prompt[4 * core:4 * core + 4].reshape(1024, D)),
            "xs": np.ascontiguousarray(x_sample[b][gtok]),
            "ck0": f32(cache_attn_k[b, 0]).reshape(512, 256),
            "cv0": f32(cache_attn_v[b, 0]).reshape(512, 256),
            "ck1": f32(cache_na_k[b, 0]).reshape(512, D),
            "cv1": f32(cache_na_v[b, 0]).reshape(512, D),
            "condT": np.ascontiguousarray(cond),
            "rm": _rm_table(q),
            "cos": cos, "sin": sin, "mlr": mlr,
        })
        in_maps.append(m)
    return in_maps


def gather_outputs(r):
    y_prompt = np.concatenate([r[i]["yp"].reshape(4, 256, D) for i in range(8)], axis=0)
    y_sample = np.stack([np.concatenate([r[b * 4 + q]["ys"] for q in range(4)], axis=0) for b in range(2)], axis=0)
    ak = np.concatenate([r[i]["ak"].reshape(4, 1, 256, 2, 128) for i in range(8)], axis=0)
    av = np.concatenate([r[i]["av"].reshape(4, 1, 256, 2, 128) for i in range(8)], axis=0)
    nk = np.concatenate([r[i]["nk"].reshape(4, 1, 256, 16, 128) for i in range(8)], axis=0)
    nv = np.concatenate([r[i]["nv"].reshape(4, 1, 256, 16, 128) for i in range(8)], axis=0)
    return (y_prompt.astype(np.float32), y_sample.astype(np.float32), ak.astype(np.float32),
            av.astype(np.float32), nk.astype(np.float32), nv.astype(np.float32))
```

```python
import contextlib
import numpy as np
import ml_dtypes
import concourse.bass as bass
import concourse.mybir as mybir
from concourse.bass_utils import run_bass_kernel_spmd

F32 = mybir.dt.float32
BF16 = mybir.dt.bfloat16
AF = mybir.ActivationFunctionType
ALU = mybir.AluOpType

D = 2048
NEG = -30000.0
SCALE = 128.0 ** -0.5
EPS = 1e-6


class Tok:
    __slots__ = ("w", "r", "rd", "excl")

    def __init__(self, excl=False):
        self.excl = excl
        self.w = None
        self.r = {}
        self.rd = {}


class Slot:
    __slots__ = ("sem", "count")

    def __init__(self, sem):
        self.sem = sem
        self.count = 0


class Op:
    __slots__ = ("fn", "deps", "flag", "count", "slot")

    def __init__(self, fn, deps, slot=None):
        self.fn = fn
        self.deps = deps
        self.flag = False
        self.count = 0
        self.slot = slot


class Prog:
    ENGS = ("pe", "act", "dve", "pool", "sp")

    def __init__(self, nc, stack):
        self.nc = nc
        self.stack = stack
        self.ops = {e: [] for e in self.ENGS}
        self.esem = {e: stack.enter_context(nc.semaphore("es_" + e)) for e in ("pe", "act", "dve")}
        self.nslots = 0

    def slot(self):
        self.nslots += 1
        return Slot(self.stack.enter_context(self.nc.semaphore("ds%d" % self.nslots)))

    def _deps(self, eng, reads, writes, is_dma):
        deps = []
        for t in reads:
            if t.w is not None:
                w = t.w
                if w[0] == "d" or is_dma or w[1] != eng or eng != "pe":
                    deps.append(w)
        for t in writes:
            if t.w is not None:
                w = t.w
                if w[0] == "d" or is_dma or w[1] != eng or eng != "pe":
                    deps.append(w)
            for e, idx in t.r.items():
                if is_dma or e != eng or eng != "pe":
                    deps.append(("e", e, idx))
            for s, c in t.rd.items():
                deps.append(("d", s, c))
        return deps

    def op(self, eng, fn, reads=(), writes=()):
        if any(t.excl for t in reads):
            writes = list(writes) + [t for t in reads if t.excl]
            reads = [t for t in reads if not t.excl]
        deps = self._deps(eng, reads, writes, False)
        idx = len(self.ops[eng])
        self.ops[eng].append(Op(fn, deps))
        for t in reads:
            t.r[eng] = idx
        for t in writes:
            t.w = ("e", eng, idx)
            t.r = {}
            t.rd = {}

    def dma(self, q, fn, slot, reads=(), writes=()):
        deps = self._deps(q, reads, writes, True)
        slot.count += 16
        self.ops[q].append(Op(fn, deps, slot))
        for t in reads:
            t.rd[slot] = slot.count
        for t in writes:
            t.w = ("d", slot, slot.count)
            t.r = {}
            t.rd = {}

    def emit(self, block, final_slots):
        for e in ("pe", "act", "dve"):
            for o in self.ops[e]:
                for d in o.deps:
                    if d[0] == "e":
                        self.ops[d[1]][d[2]].flag = True
        for e in self.ENGS:
            for o in self.ops[e]:
                for d in o.deps:
                    if d[0] == "e":
                        self.ops[d[1]][d[2]].flag = True
        for e in ("pe", "act", "dve"):
            c = 0
            for o in self.ops[e]:
                if o.flag:
                    c += 1
                    o.count = c

        def runner(ename):
            def f(eng):
                seen = {}
                for o in self.ops[ename]:
                    for d in o.deps:
                        if d[0] == "e":
                            sem = self.esem[d[1]]
                            val = self.ops[d[1]][d[2]].count
                            key = d[1]
                        else:
                            sem = d[1].sem
                            val = d[2]
                            key = d[1]
                        if seen.get(key, 0) >= val:
                            continue
                        seen[key] = val
                        eng.wait_ge(sem, val)
                    ins = o.fn(eng)
                    if o.slot is not None:
                        ins.then_inc(o.slot.sem, 16)
                    elif o.flag:
                        ins.then_inc(self.esem[ename], 1)
                if ename == "sp":
                    for s in final_slots:
                        if s.count:
                            eng.wait_ge(s.sem, s.count)
            return f

        block.tensor(runner("pe"))
        block.scalar(runner("act"))
        block.vector(runner("dve"))
        block.gpsimd(runner("pool"))
        block.sync(runner("sp"))


class _Stop(Exception):
    pass


def build_program(stop=None, groups=(0, 1)):
    nc = bass.Bass("TRN2", target_bir_lowering=False)

    def din(name, shape):
        return nc.dram_tensor(name, list(shape), F32, kind="ExternalInput").ap()

    def dout(name, shape):
        return nc.dram_tensor(name, list(shape), F32, kind="ExternalOutput").ap()

    xp_d = din("xp", (1024, D))
    xs_d = din("xs", (2048, D))
    ck0_d = din("ck0", (512, 256))
    cv0_d = din("cv0", (512, 256))
    ck1_d = din("ck1", (512, D))
    cv1_d = din("cv1", (512, D))
    cond_d = din("condT", (128, 32))
    modw_d = din("modw", (2, D, 6 * D))
    modb_d = din("modb", (128, 192))
    g1_d = din("g1", (128, 32))
    g2_d = din("g2", (128, 32))
    gf_d = din("gf", (128, 16))
    win_d = din("w_in", (D, 4608))
    convw_d = din("convw", (128, 24))
    qk_d = din("qkn", (128, 2))
    wo0_d = din("w_out0", (D, D))
    wqkv_d = din("w_qkv", (D, 3 * D))
    wo1_d = din("w_out1", (D, D))
    w1_d = din("w1", (2, D, 4 * D))
    w2_d = din("w2", (2, 4 * D, D))
    cm_d = din("cm", (16, 128, 1408))
    rm_d = din("rm", (2, 8 * 512))
    lsel_d = din("lsel", (2, 128))
    cos_d = din("cos", (128, 2048))
    sin_d = din("sin", (128, 2048))
    prot_d = din("prot", (128, 128))
    ident_d = din("ident", (128, 128))
    mlr_d = din("mlr", (128, 2))

    yp_d = dout("yp", (1024, D))
    ys_d = dout("ys", (512, D))
    ak_d = dout("ak", (1024, 256))
    av_d = dout("av", (1024, 256))
    nk_d = dout("nk", (1024, D))
    nv_d = dout("nv", (1024, D))

    stack = contextlib.ExitStack()
    with stack:
        P = Prog(nc, stack)

        def sb(name, shape, dt):
            return stack.enter_context(nc.sbuf_tensor("sb_" + name, list(shape), dt))

        xres = sb("xres", (128, 16, 1024), F32)
        hbuf = sb("hbuf", (128, 16, 1024), BF16)
        mixb = sb("mixb", (128, 8, 1024), BF16)
        wsl = [sb("wsl%d" % i, (128, 4096), BF16) for i in range(4)]
        scrA = sb("scrA", (128, 4096), F32)
        scrB = sb("scrB", (128, 10496), BF16)
        ropeb = sb("ropeb", (128, 1960), F32)
        ident = sb("ident", (128, 128), F32)
        prot = sb("prot", (128, 128), F32)
        onesD = sb("onesD", (128, 128), BF16)
        onesH = sb("onesH", (128, 128), BF16)
        ones1 = sb("ones1", (128, 128), BF16)
        lsel = sb("lsel", (2, 128), BF16)
        condT = sb("condT", (128, 16, 2), F32)
        sil = sb("sil", (128, 16, 2), BF16)
        modb = sb("modb", (128, 2, 96), F32)
        modv = sb("modv", (128, 2, 96, 2), F32)
        g1 = sb("g1", (128, 2, 16), F32)
        g2 = sb("g2", (128, 2, 16), F32)
        gf = sb("gf", (128, 16), F32)
        convw = sb("convw", (128, 8, 3), F32)
        qkn = sb("qkn", (128, 2), F32)
        mlr = sb("mlr", (128, 2), F32)
        avec = sb("avec", (128, 2, 2, 2, 16), F32)
        sqr = sb("sqr", (128, 2, 512), BF16)
        rsr = sb("rsr", (128, 2, 512), F32)
        tmpr = sb("tmpr", (128, 3, 512), F32)
        pr = sb("pr", (128, 2, 512), BF16)

        dummy = sb("dummy", (128, 4), F32)
        epsT = sb("epsT", (128, 4), F32)
        psall = stack.enter_context(nc.psum_tensor("psall", [128, 8, 512], F32))
        ps = [psall[:, i, :] for i in range(8)]
        ps_t = [Tok(excl=True) for _ in range(8)]
        bank_ctr = [0]

        reserved = set()
        bank_of = {}

        def bank(hold=False):
            while True:
                b = bank_ctr[0] % 8
                bank_ctr[0] += 1
                if b not in reserved:
                    break
            if hold:
                reserved.add(b)
            bank_of[id(ps[b])] = b
            return ps[b], ps_t[b]

        pair_ctr = [0]

        def bank_pair(hold=False):
            while True:
                p = pair_ctr[0] % 4
                pair_ctr[0] += 1
                if 2 * p not in reserved and 2 * p + 1 not in reserved:
                    break
            if hold:
                reserved.add(2 * p)
                reserved.add(2 * p + 1)
            for b in (2 * p, 2 * p + 1):
                bank_of[id(ps[b])] = b
            return p, psall[:, 2 * p:2 * p + 2, :], [(ps[2 * p], ps_t[2 * p]), (ps[2 * p + 1], ps_t[2 * p + 1])]

        def release(pb):
            reserved.discard(bank_of[id(pb)])

        t_x = [[Tok() for _ in range(2)] for _ in range(16)]
        t_h = [Tok(), Tok()]
        t_mix = [Tok(), Tok()]
        t_w = [Tok() for _ in range(4)]
        w_slots = [P.slot() for _ in range(4)]
        w_ctr = [0]
        t_scrA = [Tok() for _ in range(12)]
        t_scrB = [Tok() for _ in range(8)]
        t_rope = Tok()
        t_const = Tok()
        t_mod = Tok()
        t_avec = Tok()
        t_sq = [Tok() for _ in range(4)]
        t_rs = [Tok() for _ in range(2)]
        t_tmp = [Tok() for _ in range(3)]
        t_pr = [Tok() for _ in range(4)]
        sq_c = [0]
        rs_c = [0]
        tmp_c = [0]
        pr_c = [0]

        t_dummy = Tok()

        def ckpt(name):
            if stop is not None and name == stop:
                raise _Stop()

        def fence(toks):
            P.op("dve", lambda e: e.memset(dummy[:, :], 0.0), writes=list(toks) + [t_dummy])

        def ring(ctr, n):
            i = ctr[0] % n
            ctr[0] += 1
            return i

        s_const = P.slot()
        s_in = [P.slot() for _ in range(4)]
        s_out = [P.slot() for _ in range(6)]
        s_rope = P.slot()
        s_rm = P.slot()

        def wload(parts):
            i = w_ctr[0] % 4
            w_ctr[0] += 1
            for dfn, src in parts:
                dst = dfn(wsl[i])
                P.dma("pool", (lambda e, dst=dst, src=src: e.dma_start(out=dst, in_=src)),
                      w_slots[i], writes=[t_w[i]])
            return wsl[i], t_w[i]

        def v3(t, k, n):
            return t[:, 0:k * n].rearrange("p (k n) -> p k n", k=k)

        def wcols(w_ap, c0, n):
            src = w_ap.rearrange("(k p) n -> p k n", p=128)[:, :, c0:c0 + n]
            return [(lambda t, n=n: v3(t, 16, n), src)]

        def wrows(w_ap, r0, kk, c0, n):
            src = w_ap[r0:r0 + kk * 128, c0:c0 + n].rearrange("(k p) n -> p k n", p=128)
            return [(lambda t, kk=kk, n=n: v3(t, kk, n), src)]

        def cload(dst, src, q="sp"):
            P.dma(q, (lambda e, dst=dst, src=src: e.dma_start(out=dst, in_=src)), s_const, writes=[t_const])

        cload(ident[:, :], ident_d)
        cload(prot[:, :], prot_d)
        cload(condT[:, :, :], cond_d.rearrange("p (k c) -> p k c", c=2))
        cload(modb[:, :, :], modb_d.rearrange("p (l j) -> p l j", l=2))
        cload(g1[:, :, :], g1_d.rearrange("p (l j) -> p l j", l=2))
        cload(g2[:, :, :], g2_d.rearrange("p (l j) -> p l j", l=2))
        cload(gf[:, :], gf_d)
        cload(convw[:, :, :], convw_d.rearrange("p (c t) -> p c t", t=3))
        cload(qkn[:, :], qk_d)
        cload(mlr[:, :], mlr_d)
        s_c2 = P.slot()
        t_c2 = Tok()
        P.dma("pool", lambda e: e.dma_start(out=lsel[:, :], in_=lsel_d), s_c2, writes=[t_c2])
        t_ones = Tok()
        P.op("dve", lambda e: e.memset(onesD[:, :], 1.0 / D), writes=[t_ones])
        P.op("dve", lambda e: e.memset(onesH[:, :], 1.0 / 128), writes=[t_ones])
        P.op("dve", lambda e: e.memset(ones1[:, :], 1.0), writes=[t_ones])
        P.op("dve", lambda e: e.memset(epsT[:, :], EPS), writes=[t_ones])

        _skip_mod = (stop == "consts")
        _stop_mod = (stop == "mod")
        t_modsec = [[Tok() for _ in range(6)] for _ in range(2)]
        t_avs = [[Tok(), Tok()] for _ in range(2)]
        P.op("act", lambda e: e.activation(out=sil[:, :, :], in_=condT[:, :, :], func=AF.Silu),
             reads=[t_const], writes=[t_mod])
        mod_queue = [(l, jb) for l in range(2) for jb in range(48)]

        def mod_block(l, jb):
            sec = jb // 8
            wt, wtok = wload(wcols(modw_d[l], jb * 256, 256))
            wv = v3(wt, 16, 256)
            pb, pt = bank()
            for jj in range(2):
                for k in range(16):
                    P.op("pe", (lambda e, pb=pb, wv=wv, jj=jj, k=k: e.matmul(
                        pb[:, jj * 2:jj * 2 + 2], wv[:, k, jj * 128:(jj + 1) * 128], sil[:, k, :],
                        start=(k == 0), stop=(k == 15))), reads=[wtok, t_mod], writes=[pt])
            for jj in range(2):
                P.op("dve", (lambda e, pb=pb, l=l, jb=jb, jj=jj: e.tensor_scalar(
                    out=modv[:, l, jb * 2 + jj, :], in0=pb[:, jj * 2:jj * 2 + 2],
                    scalar1=modb[:, l, jb * 2 + jj:jb * 2 + jj + 1], scalar2=None, op0=ALU.add)),
                     reads=[pt, t_const], writes=[t_modsec[l][sec]])
            if jb % 8 == 7 and sec in (1, 4):
                ni, gg, off = (0, g1, 16) if sec == 1 else (1, g2, 64)
                for ci in range(2):
                    P.op("dve", (lambda e, l=l, ci=ci, ni=ni, gg=gg, off=off: e.scalar_tensor_tensor(
                        out=avec[:, l, ci, ni, :], in0=modv[:, l, off:off + 16, ci], scalar=1.0,
                        in1=gg[:, l, :], op0=ALU.add, op1=ALU.mult)),
                         reads=[t_modsec[l][sec], t_const], writes=[t_avs[l][ni]])

        def pump(n):
            for _ in range(n):
                if mod_queue and not _skip_mod:
                    mod_block(*mod_queue.pop(0))

        def need_mod(l, sec):
            while mod_queue and mod_queue[0] <= (l, sec * 8 + 7) and not _skip_mod:
                mod_block(*mod_queue.pop(0))

        if _stop_mod:
            need_mod(1, 5)

        def mvec(l, ci, j):
            return modv[:, l, j, ci:ci + 1]

        stage = [scrA[:, 0:2048], scrA[:, 2048:4096]]
        stage_ctr = [0]

        def load_xT(dst, dtoks_fn, dcol0, src_rows, ntok):
            r0 = 0
            ei = 0
            while r0 < ntok:
                n = min(128, ntok - r0)
                si = stage_ctr[0] % 2
                stage_ctr[0] += 1
                st, stt = stage[si], t_scrA[si]
                P.dma("sp", (lambda e, st=st, r0=r0, n=n: e.dma_start(out=st[:n, :], in_=src_rows[r0:r0 + n, :])),
                      s_in[si], writes=[stt])
                for quad in range(4):
                    pb, pt = bank()
                    for cc in range(4):
                        c = quad * 4 + cc
                        P.op("pe", (lambda e, pb=pb, st=st, cc=cc, c=c, n=n: e.transpose(
                            pb[:, cc * 128:cc * 128 + n], st[:n, c * 128:(c + 1) * 128], ident[:n, :n])),
                             reads=[stt, t_const], writes=[pt])
                    eng = "act" if ei % 2 == 0 else "dve"
                    ei += 1
                    dv = dst[:, quad * 4:quad * 4 + 4, dcol0 + r0:dcol0 + r0 + n]
                    sv = pb[:, :].rearrange("p (c t) -> p c t", c=4)[:, :, 0:n]
                    wt = dtoks_fn(quad, dcol0 + r0, n)
                    if eng == "act":
                        P.op("act", (lambda e, dv=dv, sv=sv: e.activation(out=dv, in_=sv, func=AF.Copy)),
                             reads=[pt], writes=wt)
                    else:
                        P.op("dve", (lambda e, dv=dv, sv=sv: e.tensor_copy(out=dv, in_=sv)),
                             reads=[pt], writes=wt)
                r0 += n

        def rstd_tile(xsrc_fn, xtoks, tn, ones_t, nchunks):
            pb, pt = bank()
            for c in range(nchunks):
                qi = ring(sq_c, 2)
                src = xsrc_fn(c)
                if c % 3 == 2:
                    P.op("dve", (lambda e, qi=qi, src=src: e.tensor_tensor(out=sqr[:, qi, 0:tn], in0=src, in1=src, op=ALU.mult)),
                         reads=xtoks(c), writes=[t_sq[qi]])
                else:
                    P.op("act", (lambda e, qi=qi, src=src: e.activation(out=sqr[:, qi, 0:tn], in_=src, func=AF.Square)),
                         reads=xtoks(c), writes=[t_sq[qi]])
                P.op("pe", (lambda e, pb=pb, qi=qi, c=c: e.matmul(
                    pb[:, 0:tn], ones_t[:, :], sqr[:, qi, 0:tn], start=(c == 0), stop=(c == nchunks - 1))),
                     reads=[t_sq[qi], t_ones], writes=[pt])
            ri = ring(rs_c, 2)
            P.op("act", (lambda e, pb=pb, ri=ri: e.activation(
                out=rsr[:, ri, 0:tn], in_=pb[:, 0:tn], func=AF.Ln, bias=epsT[:, 0:1], scale=1.0)),
                 reads=[pt, t_ones], writes=[t_rs[ri]])
            P.op("act", (lambda e, ri=ri: e.activation(
                out=rsr[:, ri, 0:tn], in_=rsr[:, ri, 0:tn], func=AF.Exp, scale=-0.5)),
                 reads=[t_rs[ri]], writes=[t_rs[ri]])
            return rsr[:, ri, 0:tn], t_rs[ri]

        def warm(n):
            pb, pt = bank()
            for _ in range(n):
                P.op("pe", (lambda e, pb=pb: e.matmul(pb[:, 0:128], prot[:, :], ident[:, :], start=True, stop=True)),
                     reads=[t_const], writes=[pt])

        def norm_h(l, ci, ni, xt0, ht0, tn, xti, hti, warm_n=0):
            boff = 0 if ni == 0 else 48
            need_mod(l, 1 if ni == 0 else 4)
            tsh = t_modsec[l][0 if ni == 0 else 3]
            tav = t_avs[l][ni]
            rs, rst = rstd_tile(lambda c: xres[:, c, xt0:xt0 + tn], lambda c: [t_x[c][xti]], tn, onesD, 16)
            if warm_n:
                warm(warm_n)
            for c in range(16):
                ti = ring(tmp_c, 3)
                P.op("dve", (lambda e, c=c, ti=ti: e.scalar_tensor_tensor(
                    out=tmpr[:, ti, 0:tn], in0=xres[:, c, xt0:xt0 + tn], scalar=avec[:, l, ci, ni, c:c + 1],
                    in1=rs, op0=ALU.mult, op1=ALU.mult)), reads=[t_x[c][xti], rst, tav], writes=[t_tmp[ti]])
                P.op("act", (lambda e, c=c, ti=ti: e.activation(
                    out=hbuf[:, c, ht0:ht0 + tn], in_=tmpr[:, ti, 0:tn], func=AF.Identity,
                    bias=mvec(l, ci, boff + c), scale=1.0)), reads=[t_tmp[ti], tsh], writes=[t_h[hti]])

        def proj_fm(wv, wtok, col0, tiles, hts, consume):
            for ti, (h0, tn) in enumerate(tiles):
                pb, pt = bank()
                for k in range(16):
                    P.op("pe", (lambda e, pb=pb, k=k, h0=h0, tn=tn: e.matmul(
                        pb[:, 0:tn], wv[:, k, col0:col0 + 128], hbuf[:, k, h0:h0 + tn],
                        start=(k == 0), stop=(k == 15))), reads=[wtok, t_h[hts[ti]]], writes=[pt])
                consume(pb, pt, ti)

        def headnorm(pb, pt, tn, gcol):
            qi = ring(sq_c, 2)
            P.op("act", (lambda e, qi=qi: e.activation(out=sqr[:, qi, 0:tn], in_=pb[:, 0:tn], func=AF.Square)),
                 reads=[pt], writes=[t_sq[qi]])
            pb2, pt2 = bank()
            P.op("pe", (lambda e, qi=qi: e.matmul(pb2[:, 0:tn], onesH[:, :], sqr[:, qi, 0:tn], start=True, stop=True)),
                 reads=[t_sq[qi], t_ones], writes=[pt2])
            ri = ring(rs_c, 2)
            P.op("act", (lambda e, ri=ri: e.activation(
                out=rsr[:, ri, 0:tn], in_=pb2[:, 0:tn], func=AF.Ln, bias=epsT[:, 0:1], scale=1.0)),
                 reads=[pt2, t_ones], writes=[t_rs[ri]])
            P.op("act", (lambda e, ri=ri: e.activation(
                out=rsr[:, ri, 0:tn], in_=rsr[:, ri, 0:tn], func=AF.Exp, scale=-0.5)),
                 reads=[t_rs[ri]], writes=[t_rs[ri]])

            def write(out_ap, wtoks):
                P.op("dve", (lambda e: e.scalar_tensor_tensor(
                    out=out_ap, in0=pb[:, 0:tn], scalar=qkn[:, gcol:gcol + 1], in1=rsr[:, ri, 0:tn],
                    op0=ALU.mult, op1=ALU.mult)), reads=[pt, t_rs[ri], t_const], writes=wtoks)
            return write

        def rope(src_tmp_i, tn, tab0, out_ap, wtoks):
            src = tmpr[:, src_tmp_i, 0:tn]
            pb, pt = bank()
            P.op("pe", (lambda e: e.matmul(pb[:, 0:tn], prot[:, :], src, start=True, stop=True)),
                 reads=[t_tmp[src_tmp_i], t_const], writes=[pt])
            t2 = ring(tmp_c, 3)
            P.op("dve", (lambda e: e.tensor_tensor(out=tmpr[:, t2, 0:tn], in0=pb[:, 0:tn],
                                                   in1=ropeb[:, 992 + tab0:992 + tab0 + tn], op=ALU.mult)),
                 reads=[pt, t_rope], writes=[t_tmp[t2]])
            P.op("dve", (lambda e: e.tensor_tensor(out=src, in0=src, in1=ropeb[:, tab0:tab0 + tn], op=ALU.mult)),
                 reads=[t_tmp[src_tmp_i], t_rope], writes=[t_tmp[src_tmp_i]])
            P.op("dve", (lambda e: e.tensor_tensor(out=out_ap, in0=src, in1=tmpr[:, t2, 0:tn], op=ALU.add)),
                 reads=[t_tmp[src_tmp_i], t_tmp[t2]], writes=wtoks)

        def wout_part(w_ap, r0, kk, l, ci, gate_off, mix_fn, mtoks, tiles_x, tiles_m, xtis, npump=2):
            need_mod(l, 2)
            for cb in range(4):
                pump(npump)
                wt, wtok = wload(wrows(w_ap, r0, kk, cb * 512, 512))
                wv = v3(wt, kk, 512)
                for oc4 in range(4):
                    oc = cb * 4 + oc4
                    for ti, ((x0, tn), (m0, _)) in enumerate(zip(tiles_x, tiles_m)):
                        pb, pt = bank()
                        for k in range(kk):
                            P.op("pe", (lambda e, pb=pb, k=k, oc4=oc4, m0=m0, tn=tn, wv=wv: e.matmul(
                                pb[:, 0:tn], wv[:, k, oc4 * 128:(oc4 + 1) * 128], mix_fn(k, m0, tn),
                                start=(k == 0), stop=(k == kk - 1))), reads=[wtok] + mtoks(ti), writes=[pt])
                        xt = t_x[oc][xtis[ti]]
                        P.op("dve", (lambda e, pb=pb, oc=oc, x0=x0, tn=tn: e.scalar_tensor_tensor(
                            out=xres[:, oc, x0:x0 + tn], in0=pb[:, 0:tn], scalar=mvec(l, ci, gate_off + oc),
                            in1=xres[:, oc, x0:x0 + tn], op0=ALU.mult, op1=ALU.add)),
                             reads=[pt, t_modsec[l][2], xt], writes=[xt])

        def mlp(l, ci, tiles_x, tiles_h, xtis, htis):
            hid = scrB[:, 0:4096].rearrange("p (r c t) -> p r c t", r=4, c=2)
            need_mod(l, 5)
            for jp in range(16):
                w1s = []
                for jj in range(2):
                    jb = jp * 2 + jj
                    w1t, w1tok = wload(wcols(w1_d[l], jb * 256, 256))
                    w1s.append((v3(w1t, 16, 256), w1tok))
                for ti, ((x0, tn), (h0, _)) in enumerate(zip(tiles_x, tiles_h)):
                    for jj in range(2):
                        w1v, w1tok = w1s[jj]
                        hr = jj * 2 + ti
                        for hc in range(2):
                            pb, pt = bank()
                            for k in range(16):
                                P.op("pe", (lambda e, pb=pb, k=k, hc=hc, h0=h0, tn=tn, w1v=w1v: e.matmul(
                                    pb[:, 0:tn], w1v[:, k, hc * 128:(hc + 1) * 128], hbuf[:, k, h0:h0 + tn],
                                    start=(k == 0), stop=(k == 15))), reads=[w1tok, t_h[htis[ti]]], writes=[pt])
                            t1 = ring(tmp_c, 3)
                            P.op("act", (lambda e, pb=pb, t1=t1, tn=tn: e.activation(
                                out=tmpr[:, t1, 0:tn], in_=pb[:, 0:tn], func=AF.Relu)), reads=[pt], writes=[t_tmp[t1]])
                            P.op("act", (lambda e, t1=t1, hr=hr, hc=hc, tn=tn: e.activation(
                                out=hid[:, hr, hc, 0:tn], in_=tmpr[:, t1, 0:tn], func=AF.Square)),
                                 reads=[t_tmp[t1]], writes=[t_scrB[hr]])
                pump(1)
                w2s = []
                for jj in range(2):
                    jb = jp * 2 + jj
                    w2t, w2tok = wload(wrows(w2_d[l], jb * 256, 2, 0, 2048))
                    w2s.append((v3(w2t, 2, 2048), w2tok))
                for ti, ((x0, tn), (h0, _)) in enumerate(zip(tiles_x, tiles_h)):
                    for oc in range(16):
                        pb, pt = bank()
                        for jj in range(2):
                            w2v, w2tok = w2s[jj]
                            hr = jj * 2 + ti
                            for hc in range(2):
                                P.op("pe", (lambda e, pb=pb, oc=oc, hc=hc, hr=hr, tn=tn, w2v=w2v, jj=jj: e.matmul(
                                    pb[:, 0:tn], w2v[:, hc, oc * 128:(oc + 1) * 128], hid[:, hr, hc, 0:tn],
                                    start=(jj == 0 and hc == 0), stop=(jj == 1 and hc == 1))),
                                     reads=[w2tok, t_scrB[hr]], writes=[pt])
                        xt = t_x[oc][xtis[ti]]
                        P.op("dve", (lambda e, pb=pb, oc=oc, x0=x0, tn=tn: e.scalar_tensor_tensor(
                            out=xres[:, oc, x0:x0 + tn], in0=pb[:, 0:tn], scalar=mvec(l, ci, 80 + oc),
                            in1=xres[:, oc, x0:x0 + tn], op0=ALU.mult, op1=ALU.add)),
                             reads=[pt, t_modsec[l][5], xt], writes=[xt])

        def attn_core(chunks, tn, o_out, big=False):
            nch = len(chunks)
            groups = []
            i = 0
            while i < nch:
                if (big and i + 1 < nch and chunks[i][0] == 128 and chunks[i + 1][0] == 128
                        and chunks[i][2] is None and chunks[i + 1][2] is None):
                    groups.append([i, i + 1])
                    i += 2
                else:
                    groups.append([i])
                    i += 1
            if big:
                LA = 2
                pview = lambda g: prA6[:, 2 * (g % 3):2 * (g % 3) + 2, :]
                ptok = lambda g: t_scrA[8 + g % 3]
                _, _, hp = bank_pair(hold=True)
                (pO, ptO), (pD, ptD) = hp
            else:
                LA = 1
                pview = lambda g: pr[:, g % 2:g % 2 + 1, :]
                ptok = lambda g: t_pr[g % 2]
                pO, ptO = bank(hold=True)
                pD, ptD = bank(hold=True)
            base = pr_c[0]
            pr_c[0] += len(groups)
            for gi_ in range(len(groups) + LA):
                if gi_ < len(groups):
                    g = groups[gi_]
                    pv_, ptk = pview(base + gi_), ptok(base + gi_)
                    if len(g) == 2:
                        _, pview2, bl = bank_pair()
                        for (pb, pt), ci_ in zip(bl, g):
                            chunks[ci_][1](pb, pt)
                        P.op("act", (lambda e, pview2=pview2, pv_=pv_: e.activation(
                            out=pv_[:, :, 0:tn], in_=pview2[:, :, 0:tn], func=AF.Exp, scale=SCALE)),
                             reads=[bl[0][1], bl[1][1]], writes=[ptk])
                    else:
                        n, score_fn, bias_ap, v_ap, vtoks = chunks[g[0]]
                        pb, pt = bank()
                        score_fn(pb, pt)
                        if bias_ap is None:
                            P.op("act", (lambda e, pb=pb, pv_=pv_, n=n: e.activation(
                                out=pv_[:n, 0, 0:tn], in_=pb[:n, 0:tn], func=AF.Exp, scale=SCALE)),
                                 reads=[pt], writes=[ptk])
                        else:
                            t1 = ring(tmp_c, 3)
                            P.op("dve", (lambda e, pb=pb, t1=t1, n=n, bias_ap=bias_ap: e.scalar_tensor_tensor(
                                out=tmpr[:n, t1, 0:tn], in0=pb[:n, 0:tn], scalar=SCALE, in1=bias_ap,
                                op0=ALU.mult, op1=ALU.add)), reads=[pt, t_rope], writes=[t_tmp[t1]])
                            P.op("act", (lambda e, pv_=pv_, t1=t1, n=n: e.activation(
                                out=pv_[:n, 0, 0:tn], in_=tmpr[:n, t1, 0:tn], func=AF.Exp)),
                                 reads=[t_tmp[t1]], writes=[ptk])
                gj = gi_ - LA
                if gj >= 0:
                    pv_, ptk = pview(base + gj), ptok(base + gj)
                    for k_, j in enumerate(groups[gj]):
                        n, _, _, v_ap, vtoks = chunks[j]
                        P.op("pe", (lambda e, pv_=pv_, n=n, v_ap=v_ap, j=j, k_=k_: e.matmul(
                            pO[:, 0:tn], v_ap, pv_[:n, k_, 0:tn], start=(j == 0), stop=(j == nch - 1))),
                             reads=[ptk] + vtoks, writes=[ptO])
                        P.op("pe", (lambda e, pv_=pv_, n=n, j=j, k_=k_: e.matmul(
                            pD[:, 0:tn], ones1[:n, :], pv_[:n, k_, 0:tn], start=(j == 0), stop=(j == nch - 1))),
                             reads=[ptk, t_ones], writes=[ptD])
            ri = ring(rs_c, 2)
            P.op("act", (lambda e, ri=ri: e.activation(out=rsr[:, ri, 0:tn], in_=pD[:, 0:tn], func=AF.Ln)),
                 reads=[ptD], writes=[t_rs[ri]])
            P.op("act", (lambda e, ri=ri: e.activation(out=rsr[:, ri, 0:tn], in_=rsr[:, ri, 0:tn], func=AF.Exp, scale=-1.0)),
                 reads=[t_rs[ri]], writes=[t_rs[ri]])
            o_out(pO, ptO, rsr[:, ri, 0:tn], t_rs[ri])
            release(pO)
            release(pD)

        prA6 = scrA[:, 2048:3584].bitcast(BF16).rearrange("p (r t) -> p r t", r=6)

        def final_out(out_d, x0, ntok, xti_of):
            t0 = 0
            while t0 < ntok:
                tn = min(512, ntok - t0)
                xti = xti_of(t0)
                rs, rst = rstd_tile(lambda c: xres[:, c, x0 + t0:x0 + t0 + tn], lambda c: [t_x[c][xti]], tn, onesD, 16)
                warm(40)
                for c in range(16):
                    P.op("dve", (lambda e, c=c, t0=t0, tn=tn, rs=rs: e.scalar_tensor_tensor(
                        out=xres[:, c, x0 + t0:x0 + t0 + tn], in0=xres[:, c, x0 + t0:x0 + t0 + tn],
                        scalar=gf[:, c:c + 1], in1=rs, op0=ALU.mult, op1=ALU.mult)),
                         reads=[t_x[c][xti], rst, t_const], writes=[t_x[c][xti]])
                for tc in range(tn // 128):
                    si = stage_ctr[0] % 2
                    stage_ctr[0] += 1
                    st, stt = stage[si], t_scrA[si]
                    for quad in range(4):
                        pb, pt = bank()
                        for cc in range(4):
                            c = quad * 4 + cc
                            P.op("pe", (lambda e, pb=pb, cc=cc, c=c, tc=tc, t0=t0: e.transpose(
                                pb[:, cc * 128:(cc + 1) * 128],
                                xres[:, c, x0 + t0 + tc * 128:x0 + t0 + (tc + 1) * 128], ident[:, :])),
                                 reads=[t_x[c][xti], t_const], writes=[pt])
                        if quad % 2 == 0:
                            P.op("act", (lambda e, pb=pb, st=st, quad=quad: e.activation(
                                out=st[:, quad * 512:(quad + 1) * 512], in_=pb[:, :], func=AF.Copy)),
                                 reads=[pt], writes=[stt])
                        else:
                            P.op("dve", (lambda e, pb=pb, st=st, quad=quad: e.tensor_copy(
                                out=st[:, quad * 512:(quad + 1) * 512], in_=pb[:, :])), reads=[pt], writes=[stt])
                    r = t0 + tc * 128
                    P.dma("sp", (lambda e, st=st, r=r: e.dma_start(out=out_d[r:r + 128, :], in_=st[:, :])),
                          s_out[si], reads=[stt])
                t0 += tn

        for c in range(16):
            t_x[c].append(Tok())

        def fence_own():
            P.op("dve", lambda e: e.memset(dummy[:, :], 0.0),
                 writes=[t_x[c][i] for c in range(16) for i in range(3)] + [t_dummy])

        def run_group(gi):
            is_s = gi == 1
            ci = gi
            T = 962 if is_s else 1024
            tiles = [(0, 512), (512, T - 512)]
            xt_all = lambda quad, c0, n: [t_x[quad * 4 + cc][c0 // 512] for cc in range(4)] + \
                ([t_x[quad * 4 + cc][(c0 + n - 1) // 512] for cc in range(4)] if (c0 + n - 1) // 512 != c0 // 512 else [])

            fence(t_scrA)
            fence(t_scrB)
            fence(t_mix)
            if not is_s:
                load_xT(xres, xt_all, 0, xp_d, T)
            need_mod(0, 1)

            ckpt("g%d_load" % gi)
            l = 0
            kT = scrB[:, 0:5120].rearrange("p (g t) -> p g t", g=2)
            vtok = scrB[:, 5120:5120 + 21 * 256].rearrange("p (c n) -> p c n", n=256)
            t_kT, t_vt = t_scrB[4], t_scrB[5]
            koff = 512 if is_s else 0
            if is_s:
                kchunks = [(i * 128, 128, i) for i in range(4)]
                kchunks += [(512 + i * 128, 128, 4 + i) for i in range(7)] + [(512 + 896, 66, 11)]
                kchunks += [(512 + 962 + i * 128, 128, 12 + i) for i in range(8)] + [(512 + 962 + 1024, 62, 20)]

            wk, wktok = wload(wcols(win_d, 4096, 256))
            wkv = v3(wk, 16, 256)
            wv_, wvtok = wload(wcols(win_d, 4352, 256))
            wvv = v3(wv_, 16, 256)

            mixf = mixb[:, :, :].rearrange("p c t -> p (c t)").bitcast(F32)
            kst = mixf[:, 0:2048].rearrange("p (c n) -> p c n", n=256)
            vst = mixf[:, 2048:4096].rearrange("p (c n) -> p c n", n=256)

            def kv_for_tile(h0, tn, hti, key0, vchunk0, tab0, out_tok0):
                for hh in range(2):
                    pb, pt = bank()
                    for k in range(16):
                        P.op("pe", (lambda e, pb=pb, k=k, hh=hh: e.matmul(
                            pb[:, 0:tn], wkv[:, k, hh * 128:(hh + 1) * 128], hbuf[:, k, h0:h0 + tn],
                            start=(k == 0), stop=(k == 15))), reads=[wktok, t_h[hti]], writes=[pt])
                    wr = headnorm(pb, pt, tn, 1)
                    t1 = ring(tmp_c, 3)
                    wr(tmpr[:, t1, 0:tn], [t_tmp[t1]])
                    if is_s:
                        rope(t1, tn, tab0, kT[:, hh, key0:key0 + tn], [t_kT])
                    else:
                        P.op("act", (lambda e, t1=t1, hh=hh: e.activation(
                            out=kT[:, hh, key0:key0 + tn], in_=tmpr[:, t1, 0:tn], func=AF.Copy)),
                             reads=[t_tmp[t1]], writes=[t_kT])
                        pb2, pt2 = bank()
                        for tc in range(tn // 128):
                            P.op("pe", (lambda e, pb2=pb2, t1=t1, tc=tc: e.transpose(
                                pb2[:, tc * 128:(tc + 1) * 128], tmpr[:, t1, tc * 128:(tc + 1) * 128], ident[:, :])),
                                 reads=[t_tmp[t1], t_const], writes=[pt2])
                        c0 = out_tok0 // 128
                        P.op("dve", (lambda e, pb2=pb2, hh=hh, c0=c0: e.tensor_copy(
                            out=kst[:, c0:c0 + tn // 128, hh * 128:(hh + 1) * 128],
                            in_=pb2[:, 0:tn].rearrange("p (c d) -> p c d", d=128))), reads=[pt2], writes=[t_mix[0], t_mix[1]])
                ckpt("g%d_kvK" % gi)
                c = 0
                r0 = 0
                while r0 < tn:
                    n = min(128, tn - r0)
                    pb, pt = bank()
                    for k in range(16):
                        P.op("pe", (lambda e, pb=pb, k=k, r0=r0, n=n: e.matmul(
                            pb[:n, 0:256], hbuf[:, k, h0 + r0:h0 + r0 + n], wvv[:, k, :],
                            start=(k == 0), stop=(k == 15))), reads=[wvtok, t_h[hti]], writes=[pt])
                    vc = vchunk0 + c
                    P.op("act", (lambda e, pb=pb, vc=vc, n=n: e.activation(
                        out=vtok[:n, vc, :], in_=pb[:n, 0:256], func=AF.Copy)), reads=[pt], writes=[t_vt])
                    if not is_s:
                        oc_ = (out_tok0 + r0) // 128
                        P.op("dve", (lambda e, pb=pb, oc_=oc_: e.tensor_copy(out=vst[:, oc_, :], in_=pb[:, 0:256])),
                             reads=[pt], writes=[t_mix[0], t_mix[1]])
                    c += 1
                    r0 += n

            if is_s:
                cks = tmpr[:, 0:2, :].rearrange("p a (b n) -> p (a b) n", n=256)
                P.dma("sp", lambda e: e.dma_start(out=cks, in_=ck0_d.rearrange("(c p) n -> p c n", p=128)),
                      s_in[2], writes=[t_tmp[0], t_tmp[1]])
                for hh in range(2):
                    pb, pt = bank()
                    for c in range(4):
                        P.op("pe", (lambda e, pb=pb, c=c, hh=hh: e.transpose(
                            pb[:, c * 128:(c + 1) * 128], cks[:, c, hh * 128:(hh + 1) * 128], ident[:, :])),
                             reads=[t_tmp[0], t_tmp[1], t_const], writes=[pt])
                    P.op("act", (lambda e, pb=pb, hh=hh: e.activation(out=kT[:, hh, 0:512], in_=pb[:, :], func=AF.Copy)),
                         reads=[pt], writes=[t_kT])
                P.dma("pool", lambda e: e.dma_start(out=vtok[:, 0:4, :], in_=cv0_d.rearrange("(c p) n -> p c n", p=128)),
                      s_in[3], writes=[t_vt])
                rest_tiles = [(962, 0, 512, 12), (1474, 512, 512, 16), (1986, 0, 62, 20)]
                for (r, xc, tn, vch) in rest_tiles:
                    ti_ = xc // 512
                    load_xT(xres, xt_all, xc, xs_d[r:r + tn, :], tn)
                    P.dma("sp", (lambda e, r=r, tn=tn: e.dma_start(out=ropeb[:, 0:tn], in_=cos_d[:, r:r + tn])),
                          s_rope, writes=[t_rope])
                    P.dma("sp", (lambda e, r=r, tn=tn: e.dma_start(out=ropeb[:, 992:992 + tn], in_=sin_d[:, r:r + tn])),
                          s_rope, writes=[t_rope])
                    norm_h(0, ci, 0, xc, xc, tn, ti_, ti_, 40)
                    kv_for_tile(xc, tn, ti_, 512 + r, vch, 0, 0)
                P.dma("sp", lambda e: e.dma_start(out=ropeb[:, 0:962], in_=cos_d[:, 0:962]), s_rope, writes=[t_rope])
                P.dma("sp", lambda e: e.dma_start(out=ropeb[:, 992:992 + 962], in_=sin_d[:, 0:962]), s_rope,
                      writes=[t_rope])

            if is_s:
                load_xT(xres, xt_all, 0, xs_d, T)
            for ti, (t0, tn) in enumerate(tiles):
                norm_h(0, ci, 0, t0, t0, tn, ti, ti, 56 if ti == 0 else 0)

            ckpt("g%d_norm" % gi)
            for ti, (t0, tn) in enumerate(tiles):
                kv_for_tile(t0, tn, ti, koff + t0, (4 if is_s else 0) + t0 // 128, t0, t0)
            ckpt("g%d_kvV" % gi)
            if not is_s:
                P.dma("sp", lambda e: e.dma_start(out=ak_d.rearrange("(c p) n -> p c n", p=128), in_=kst),
                      s_out[2], reads=[t_mix[0], t_mix[1]])
                P.dma("sp", lambda e: e.dma_start(out=av_d.rearrange("(c p) n -> p c n", p=128), in_=vst),
                      s_out[3], reads=[t_mix[0], t_mix[1]])

            ckpt("g%d_kv" % gi)
            nseq, L = (1, 962) if is_s else (4, 256)
            ub = scrA[:, 0:nseq * (L + 2)].rearrange("p (s t) -> p s t", s=nseq)
            vb = [scrA[:, 1040:1040 + T], scrA[:, 2080:2080 + T]]
            t_u, t_v = t_scrA[3], [t_scrA[4], t_scrA[5]]
            P.op("dve", lambda e: e.memset(ub, 0.0), writes=list(t_scrA))
            for cp in range(4):
                for cc in range(2):
                    c = cp * 2 + cc
                    pump(1)
                    wt, wtok = wload([(lambda t: v3(t, 16, 256)[:, :, 0:128],
                                       win_d.rearrange("(k p) n -> p k n", p=128)[:, :, 2048 + c * 128:2048 + (c + 1) * 128]),
                                      (lambda t: v3(t, 16, 256)[:, :, 128:256],
                                       win_d.rearrange("(k p) n -> p k n", p=128)[:, :, 1024 + c * 128:1024 + (c + 1) * 128])])
                    wv = v3(wt, 16, 256)

                    def uview(t0, tn):
                        if is_s:
                            return ub[:, 0, 1 + t0:1 + t0 + tn]
                        return ub[:, t0 // 256:(t0 + tn) // 256, 1:257]

                    def cons_xa(pb, pt, ti):
                        t0, tn = tiles[ti]
                        src = pb[:, 0:tn] if is_s else pb[:, 0:tn].rearrange("p (s t) -> p s t", t=256)
                        P.op("act", (lambda e: e.activation(out=uview(t0, tn), in_=src, func=AF.Copy)),
                             reads=[pt], writes=[t_u])

                    def cons_gc(pb, pt, ti):
                        t0, tn = tiles[ti]
                        src = pb[:, 0:tn] if is_s else pb[:, 0:tn].rearrange("p (s t) -> p s t", t=256)
                        P.op("dve", (lambda e: e.tensor_tensor(out=uview(t0, tn), in0=src, in1=uview(t0, tn), op=ALU.mult)),
                             reads=[pt, t_u], writes=[t_u])
                    proj_fm(wv, wtok, 0, tiles, [0, 1], cons_xa)
                    proj_fm(wv, wtok, 128, tiles, [0, 1], cons_gc)
                    if is_s:
                        P.op("dve", lambda e: e.tensor_scalar(out=ub[:, 0, 257:258], in0=ub[:, 0, 257:258],
                                                              scalar1=mlr[:, 0:1], scalar2=None, op0=ALU.mult),
                             reads=[t_u, t_const], writes=[t_u])
                        P.op("dve", lambda e: e.tensor_scalar(out=ub[:, 0, 770:771], in0=ub[:, 0, 770:771],
                                                              scalar1=mlr[:, 1:2], scalar2=None, op0=ALU.mult),
                             reads=[t_u, t_const], writes=[t_u])
                    v3d = vb[cc].rearrange("p (s t) -> p s t", s=nseq)
                    P.op("dve", (lambda e, c=c, v3d=v3d: e.tensor_scalar(
                        out=v3d, in0=ub[:, :, 1:L + 1], scalar1=convw[:, c, 1:2], scalar2=None, op0=ALU.mult)),
                         reads=[t_u, t_const], writes=[t_v[cc]])
                    P.op("dve", (lambda e, c=c, v3d=v3d: e.scalar_tensor_tensor(
                        out=v3d, in0=ub[:, :, 0:L], scalar=convw[:, c, 0:1], in1=v3d, op0=ALU.mult, op1=ALU.add)),
                         reads=[t_u, t_const, t_v[cc]], writes=[t_v[cc]])
                    P.op("dve", (lambda e, c=c, v3d=v3d: e.scalar_tensor_tensor(
                        out=v3d, in0=ub[:, :, 2:L + 2], scalar=convw[:, c, 2:3], in1=v3d, op0=ALU.mult, op1=ALU.add)),
                         reads=[t_u, t_const, t_v[cc]], writes=[t_v[cc]])
                pump(1)
                wt, wtok = wload(wcols(win_d, cp * 256, 256))
                wv = v3(wt, 16, 256)
                for cc in range(2):
                    c = cp * 2 + cc

                    def cons_gb(pb, pt, ti, c=c, cc=cc):
                        t0, tn = tiles[ti]
                        P.op("dve", (lambda e: e.tensor_tensor(out=mixb[:, c, t0:t0 + tn], in0=pb[:, 0:tn],
                                                               in1=vb[cc][:, t0:t0 + tn], op=ALU.mult)),
                             reads=[pt, t_v[cc]], writes=[t_mix[ti]])
                    proj_fm(wv, wtok, cc * 128, tiles, [0, 1], cons_gb)
            ckpt("g%d_conv" % gi)
            wout_part(wo0_d, 0, 8, 0, ci, 32, lambda k, m0, tn: mixb[:, k, m0:m0 + tn],
                      lambda ti: [t_mix[ti]], tiles, tiles, [0, 1])
            ckpt("g%d_woutA" % gi)

            fence(t_scrA)
            qT = scrA[:, 0:2048].bitcast(BF16).rearrange("p (h t) -> p h t", h=4)
            t_q = t_scrA[6]
            for g in range(2):
                for qb in range(2):
                    pump(1)
                    wt, wtok = wload(wcols(win_d, 3072 + g * 512 + qb * 256, 256))
                    wv = v3(wt, 16, 256)
                    for hh in range(2):
                        hq = qb * 2 + hh

                        def cons_q(pb, pt, ti, hq=hq):
                            t0, tn = tiles[ti]
                            wr = headnorm(pb, pt, tn, 0)
                            if is_s:
                                t1 = ring(tmp_c, 3)
                                wr(tmpr[:, t1, 0:tn], [t_tmp[t1]])
                                rope(t1, tn, t0, qT[:, hq, t0:t0 + tn], [t_q])
                            else:
                                wr(qT[:, hq, t0:t0 + tn], [t_q])
                        proj_fm(wv, wtok, hh * 128, tiles, [0, 1], cons_q)
                if is_s:
                    for hq in range(4):
                        for ti, (t0, tn) in enumerate(tiles):
                            chunks = []
                            for (k0, n, vc) in kchunks:
                                def sfn(pb, pt, k0=k0, n=n, hq=hq, t0=t0, tn=tn, g=g):
                                    P.op("pe", (lambda e: e.matmul(pb[:n, 0:tn], kT[:, g, k0:k0 + n], qT[:, hq, t0:t0 + tn],
                                                                   start=True, stop=True)),
                                         reads=[t_kT, t_q], writes=[pt])
                                chunks.append((n, sfn, None, vtok[:n, vc, g * 128:(g + 1) * 128], [t_vt]))

                            def oout(pO, ptO, rc, rct, hq=hq, t0=t0, tn=tn, ti=ti, g=g):
                                P.op("dve", (lambda e: e.tensor_tensor(out=mixb[:, 4 * g + hq, t0:t0 + tn], in0=pO[:, 0:tn],
                                                                       in1=rc, op=ALU.mult)),
                                     reads=[ptO, rct], writes=[t_mix[ti]])
                            attn_core(chunks, tn, oout, True)
                else:
                    for s in range(4):
                        for hp in range(2):
                            chunks = []
                            for kc in range(2):
                                k0 = s * 256 + kc * 128

                                def sfn(pb, pt, k0=k0, hp=hp, s=s, g=g):
                                    P.op("pe", (lambda e: e.matmul(pb[:, 0:512], kT[:, g, k0:k0 + 128],
                                                                   qT[:, 2 * hp:2 * hp + 2, s * 256:(s + 1) * 256],
                                                                   start=True, stop=True)),
                                         reads=[t_kT, t_q], writes=[pt])
                                chunks.append((128, sfn, None, vtok[:, s * 2 + kc, g * 128:(g + 1) * 128], [t_vt]))

                            def oout(pO, ptO, rc, rct, hp=hp, s=s, g=g):
                                P.op("dve", (lambda e: e.tensor_tensor(
                                    out=mixb[:, 4 * g + 2 * hp:4 * g + 2 * hp + 2, s * 256:(s + 1) * 256],
                                    in0=pO[:, 0:512].rearrange("p (h t) -> p h t", h=2),
                                    in1=rc.rearrange("p (h t) -> p h t", h=2), op=ALU.mult)),
                                     reads=[ptO, rct], writes=[t_mix[s // 2]])
                            attn_core(chunks, 512, oout)
            ckpt("g%d_attn0" % gi)
            wout_part(wo0_d, 1024, 8, 0, ci, 32, lambda k, m0, tn: mixb[:, k, m0:m0 + tn],
                      lambda ti: [t_mix[ti]], tiles, tiles, [0, 1])

            fence(t_scrB)
            for ti, (t0, tn) in enumerate(tiles):
                norm_h(0, ci, 1, t0, t0, tn, ti, ti, 56 if ti == 0 else 0)
            ckpt("g%d_mlpnorm" % gi)
            mlp(0, ci, tiles, tiles, [0, 1], [0, 1])
            ckpt("g%d_l0" % gi)

            fence(t_scrA)
            fence(t_scrB)
            for ti, (t0, tn) in enumerate(tiles):
                norm_h(1, ci, 0, t0, t0, tn, ti, ti, 56 if ti == 0 else 0)
            q2 = scrA[:, 0:1024].bitcast(BF16).rearrange("p (h t) -> p h t", h=2)
            t_q2 = t_scrA[6]
            if is_s:
                kT2 = scrB[:, 0:2 * 1474].rearrange("p (h t) -> p h t", h=2)
                vt2 = scrB[:, 3072:3072 + 12 * 256].rearrange("p (c n) -> p c n", n=256)
                cks1 = tmpr[:, 0:2, :].rearrange("p a (b n) -> p (a b) n", n=256)
                rmb = scrB[0:2, 6144:6144 + 4096]
                P.dma("pool", lambda e: e.dma_start(out=rmb, in_=rm_d), s_rm, writes=[t_scrB[6]])
            else:
                kT2 = scrB[:, 0:2048].rearrange("p (h t) -> p h t", h=2)
                vt2 = scrB[:, 3072:3072 + 8 * 256].rearrange("p (c n) -> p c n", n=256)
                kst1 = scrA[:, 1024:2048].rearrange("p (c n) -> p c n", n=256)
                vst1 = scrA[:, 2048:4096].rearrange("p (c n) -> p c n", n=256)
            t_k2, t_v2 = t_scrB[4], t_scrB[5]
            for half in range(2):
                for pair in range(4):
                    hp0 = (half * 4 + pair) * 2
                    col = hp0 * 128
                    pump(3)
                    wq, wqtok = wload(wcols(wqkv_d, col, 256))
                    wqv = v3(wq, 16, 256)
                    wk2, wk2tok = wload(wcols(wqkv_d, 2048 + col, 256))
                    wk2v = v3(wk2, 16, 256)
                    wv2, wv2tok = wload(wcols(wqkv_d, 4096 + col, 256))
                    wv2v = v3(wv2, 16, 256)
                    for hh in range(2):
                        if is_s:
                            pb, pt = bank()
                            for k in range(16):
                                P.op("pe", (lambda e, pb=pb, k=k, hh=hh, wqv=wqv: e.matmul(
                                    pb[:, 0:512], wqv[:, k, hh * 128:(hh + 1) * 128], hbuf[:, k, 257:769],
                                    start=(k == 0), stop=(k == 15))), reads=[wqtok, t_h[0], t_h[1]], writes=[pt])
                            P.op("act", (lambda e, pb=pb, hh=hh: e.activation(out=q2[:, hh, 0:512], in_=pb[:, :], func=AF.Copy)),
                                 reads=[pt], writes=[t_q2])
                        else:
                            def cons_q2(pb, pt, ti, hh=hh):
                                t0, tn = tiles[ti]
                                P.op("act", (lambda e: e.activation(out=q2[:, hh, t0:t0 + tn], in_=pb[:, 0:tn], func=AF.Copy)),
                                     reads=[pt], writes=[t_q2])
                            proj_fm(wqv, wqtok, hh * 128, tiles, [0, 1], cons_q2)
                    if is_s:
                        P.dma("sp", (lambda e, col=col: e.dma_start(
                            out=cks1, in_=ck1_d[:, col:col + 256].rearrange("(c p) n -> p c n", p=128))),
                              s_in[2], writes=[t_tmp[0], t_tmp[1]])
                        for hh in range(2):
                            pb, pt = bank()
                            for c in range(4):
                                P.op("pe", (lambda e, pb=pb, c=c, hh=hh: e.transpose(
                                    pb[:, c * 128:(c + 1) * 128], cks1[:, c, hh * 128:(hh + 1) * 128], ident[:, :])),
                                     reads=[t_tmp[0], t_tmp[1], t_const], writes=[pt])
                            P.op("act", (lambda e, pb=pb, hh=hh: e.activation(out=kT2[:, hh, 0:512], in_=pb[:, :], func=AF.Copy)),
                                 reads=[pt], writes=[t_k2])
                        P.dma("pool", (lambda e, col=col: e.dma_start(
                            out=vt2[:, 0:4, :], in_=cv1_d[:, col:col + 256].rearrange("(c p) n -> p c n", p=128))),
                              s_in[3], writes=[t_v2])
                    for hh in range(2):
                        def cons_k2(pb, pt, ti, hh=hh, col=col):
                            t0, tn = tiles[ti]
                            if is_s:
                                P.op("act", (lambda e: e.activation(out=kT2[:, hh, 512 + t0:512 + t0 + tn], in_=pb[:, 0:tn],
                                                                    func=AF.Copy)), reads=[pt], writes=[t_k2])
                                return
                            t1 = ring(tmp_c, 3)
                            P.op("act", (lambda e: e.activation(out=tmpr[:, t1, 0:tn], in_=pb[:, 0:tn], func=AF.Copy)),
                                 reads=[pt], writes=[t_tmp[t1]])
                            P.op("dve", (lambda e: e.tensor_copy(out=kT2[:, hh, t0:t0 + tn], in_=pb[:, 0:tn])),
                                 reads=[pt], writes=[t_k2])
                            pb2, pt2 = bank()
                            for tc in range(4):
                                P.op("pe", (lambda e, tc=tc: e.transpose(
                                    pb2[:, tc * 128:(tc + 1) * 128], tmpr[:, t1, tc * 128:(tc + 1) * 128], ident[:, :])),
                                     reads=[t_tmp[t1], t_const], writes=[pt2])
                            P.op("dve", (lambda e: e.tensor_copy(
                                out=kst1[:, :, hh * 128:(hh + 1) * 128],
                                in_=pb2[:, :].rearrange("p (c d) -> p c d", d=128))), reads=[pt2], writes=[t_scrA[3]])
                            if hh == 1:
                                P.dma("sp", (lambda e: e.dma_start(
                                    out=nk_d[t0:t0 + 512, col:col + 256].rearrange("(c p) n -> p c n", p=128), in_=kst1)),
                                      s_out[2], reads=[t_scrA[3]])
                        if is_s:
                            proj_fm(wk2v, wk2tok, hh * 128, tiles, [0, 1], cons_k2)
                    if not is_s:
                        for ti in range(2):
                            for hh in range(2):
                                proj_fm(wk2v, wk2tok, hh * 128, [tiles[ti]], [ti],
                                        (lambda pb, pt, _ti, hh=hh, ti=ti, col=col: cons_k2(pb, pt, ti, hh, col)))
                    if is_s:
                        vrows = [(1 + 128 * m, 128 if m < 7 else 64, 4 + m) for m in range(8)]
                    else:
                        vrows = [(128 * m, 128, m) for m in range(8)]
                    for (r0, n, vc) in vrows:
                        pb, pt = bank()
                        for k in range(16):
                            P.op("pe", (lambda e, pb=pb, k=k, r0=r0, n=n, wv2v=wv2v: e.matmul(
                                pb[:n, 0:256], hbuf[:, k, r0:r0 + n], wv2v[:, k, :], start=(k == 0), stop=(k == 15))),
                                 reads=[wv2tok, t_h[0], t_h[1]], writes=[pt])
                        P.op("act", (lambda e, pb=pb, vc=vc, n=n: e.activation(out=vt2[:n, vc, :], in_=pb[:n, 0:256], func=AF.Copy)),
                             reads=[pt], writes=[t_v2])
                        if not is_s:
                            P.op("dve", (lambda e, pb=pb, vc=vc: e.tensor_copy(out=vst1[:, vc, :], in_=pb[:, 0:256])),
                                 reads=[pt], writes=[t_scrA[4]])
                    if not is_s:
                        P.dma("sp", (lambda e, col=col: e.dma_start(
                            out=nv_d[:, col:col + 256].rearrange("(c p) n -> p c n", p=128), in_=vst1)),
                              s_out[3], reads=[t_scrA[4]])
                    pend = []
                    for hh in range(2):
                        mi = pair * 2 + hh
                        if is_s:
                            head = hp0 + hh
                            P.dma("sp", (lambda e, head=head: e.dma_start(out=ropeb[:, 0:1408], in_=cm_d[head])),
                                  s_rope, writes=[t_rope])
                            chunks = []
                            for m in range(8):
                                n = 128 if m < 7 else 64
                                k0 = 512 + 1 + 128 * m

                                def sfn(pb, pt, m=m, n=n, k0=k0, hh=hh):
                                    P.op("pe", (lambda e: e.matmul(pb[:n, 0:512], kT2[:, hh, k0:k0 + n], q2[:, hh, 0:512],
                                                                   start=True, stop=False)),
                                         reads=[t_k2, t_q2], writes=[pt])
                                    P.op("pe", (lambda e: e.matmul(pb[:n, 0:512], lsel[:, 0:n], rmb[:, m * 512:(m + 1) * 512],
                                                                   start=False, stop=True)),
                                         reads=[t_c2, t_scrB[6]], writes=[pt])
                                b0 = (14 - 2 * m) * 64
                                chunks.append((n, sfn, ropeb[:n, b0:b0 + 512], vt2[:n, 4 + m, hh * 128:(hh + 1) * 128], [t_v2]))
                            for c in range(4):
                                def sfn(pb, pt, c=c, hh=hh):
                                    P.op("pe", (lambda e: e.matmul(pb[:, 0:512], kT2[:, hh, c * 128:(c + 1) * 128], q2[:, hh, 0:512],
                                                                   start=True, stop=True)),
                                         reads=[t_k2, t_q2], writes=[pt])
                                chunks.append((128, sfn, None, vt2[:, c, hh * 128:(hh + 1) * 128], [t_v2]))

                            def oout(pO, ptO, rc, rct, mi=mi):
                                P.op("dve", (lambda e: e.tensor_tensor(out=mixb[:, mi, 0:512], in0=pO[:, 0:512], in1=rc, op=ALU.mult)),
                                     reads=[ptO, rct], writes=[t_mix[0]])
                            attn_core(chunks, 512, oout, True)
                        else:
                            for s in range(4):
                                pb, pt = bank()
                                for kc in range(2):
                                    k0 = s * 256 + kc * 128
                                    P.op("pe", (lambda e, pb=pb, kc=kc, k0=k0, hh=hh, s=s: e.matmul(
                                        pb[:, kc * 256:(kc + 1) * 256], kT2[:, hh, k0:k0 + 128],
                                        q2[:, hh, s * 256:(s + 1) * 256], start=True, stop=True)),
                                         reads=[t_k2, t_q2], writes=[pt])
                                ui = ring(pr_c, 2)
                                P.op("act", (lambda e, pb=pb, ui=ui: e.activation(
                                    out=pr[:, ui, :], in_=pb[:, :], func=AF.Exp, scale=SCALE)),
                                     reads=[pt], writes=[t_pr[ui]])

                                def tail(ui=ui, s=s, hh=hh, mi=mi):
                                    pO, ptO = bank()
                                    for kc in range(2):
                                        P.op("pe", (lambda e, kc=kc: e.matmul(
                                            pO[:, 0:256], vt2[:, s * 2 + kc, hh * 128:(hh + 1) * 128],
                                            pr[:, ui, kc * 256:(kc + 1) * 256], start=(kc == 0), stop=(kc == 1))),
                                             reads=[t_pr[ui], t_v2], writes=[ptO])
                                    for kc in range(2):
                                        P.op("pe", (lambda e, kc=kc: e.matmul(
                                            pO[:, 256:512], ones1[:, :], pr[:, ui, kc * 256:(kc + 1) * 256],
                                            start=(kc == 0), stop=(kc == 1))), reads=[t_pr[ui], t_ones], writes=[ptO])
                                    ri = ring(rs_c, 2)
                                    P.op("act", (lambda e: e.activation(out=rsr[:, ri, 0:256], in_=pO[:, 256:512], func=AF.Ln)),
                                         reads=[ptO], writes=[t_rs[ri]])
                                    P.op("act", (lambda e: e.activation(out=rsr[:, ri, 0:256], in_=rsr[:, ri, 0:256],
                                                                        func=AF.Exp, scale=-1.0)),
                                         reads=[t_rs[ri]], writes=[t_rs[ri]])
                                    P.op("dve", (lambda e: e.tensor_tensor(out=mixb[:, mi, s * 256:(s + 1) * 256],
                                                                           in0=pO[:, 0:256], in1=rsr[:, ri, 0:256], op=ALU.mult)),
                                         reads=[ptO, t_rs[ri]], writes=[t_mix[s // 2]])
                                if pend:
                                    pend.pop(0)()
                                pend.append(tail)
                    while pend:
                        pend.pop(0)()
                if is_s:
                    if half == 0:
                        fence_own()
                    wout_part(wo1_d, half * 1024, 8, 1, ci, 32, lambda k, m0, tn: mixb[:, k, m0:m0 + tn],
                              lambda ti: [t_mix[0]], [(257, 512)], [(0, 512)], [2])
                else:
                    wout_part(wo1_d, half * 1024, 8, 1, ci, 32, lambda k, m0, tn: mixb[:, k, m0:m0 + tn],
                              lambda ti: [t_mix[ti]], tiles, tiles, [0, 1])

            ckpt("g%d_attn1" % gi)
            if is_s:
                pass
            return is_s

        def run_tail(gi):
            is_s = gi == 1
            ci = gi
            if not is_s:
                tiles = [(0, 512), (512, 512)]
                for ti, (t0, tn) in enumerate(tiles):
                    norm_h(1, ci, 1, t0, t0, tn, ti, ti, 56 if ti == 0 else 0)
                fence(t_scrB)
                mlp(1, ci, tiles, tiles, [0, 1], [0, 1])
                fence(t_scrA)
                final_out(yp_d, 0, 1024, lambda t0: t0 // 512)
            else:
                norm_h(1, ci, 1, 257, 0, 512, 2, 0, 56)
                fence(t_scrB)
                mlp(1, ci, [(257, 512)], [(0, 512)], [2], [0])
                fence(t_scrA)
                final_out(ys_d, 257, 512, lambda t0: 2)


        try:
            if _skip_mod or _stop_mod:
                raise _Stop()
            if 0 in groups:
                run_group(0)
                run_tail(0)
            ckpt("g0")
            if 1 in groups:
                run_group(1)
                run_tail(1)
        except _Stop:
            pass

        with nc.Block() as block:
            P.emit(block, s_out)
    return nc


_NC_CACHE = {}


def _fm(v):
    v = np.asarray(v, np.float32)
    return np.ascontiguousarray(v.reshape(-1, 128).T)


def _na_tables(rel_bias):
    H = rel_bias.shape[0]
    tab = np.full((H, 128, 22, 64), NEG, np.float32)
    qc = np.arange(64)
    kc0 = np.clip(qc - 8, 0, 48)
    for half in range(2):
        for kc in range(64):
            p = half * 64 + kc
            inwin = (kc >= kc0) & (kc < kc0 + 16)
            dc = kc - qc + 15
            for j in range(22):
                dr = 17 - j + half
                if 0 <= dr < 15:
                    vals = rel_bias[:, dr, np.clip(dc, 0, 30)]
                    tab[:, p, j, :] = np.where(inwin[None, :], vals, NEG)
    return np.ascontiguousarray(tab.reshape(H, 128, 22 * 64))


def _rm_table(q):
    rm = np.full((2, 8, 8, 64), NEG, np.float32)
    for m in range(8):
        for half in range(2):
            lk = 2 * m + half
            kr = 8 * q - 4 + lk
            for lq in range(4, 12):
                qr = 8 * q - 4 + lq
                kr0 = min(max(qr - 4, 0), 24)
                if 0 <= kr < 32 and kr0 <= kr < kr0 + 8 and lk < 15:
                    rm[half, m, lq - 4, :] = 0.0
    return np.ascontiguousarray(rm.reshape(2, 8 * 512))


def _rope_tables(gtok):
    half = 32
    inv = (10000.0 ** (-np.arange(half, dtype=np.float32) / half)).astype(np.float32)
    row = (gtok // 64).astype(np.float32)
    colp = (gtok % 64).astype(np.float32)
    cos = np.zeros((128, gtok.shape[0]), np.float32)
    sin = np.zeros((128, gtok.shape[0]), np.float32)
    for m in range(128):
        pos = row if m < 64 else colp
        ang = pos * inv[m % 32]
        cos[m] = np.cos(ang.astype(np.float32))
        sin[m] = np.sin(ang.astype(np.float32))
    return cos, sin


def kernel(x_prompt, x_sample, cache_attn_k, cache_attn_v, cache_na_k, cache_na_v, c, c_ctx,
           mod_w, mod_b, norm1_g, norm2_g, ab_w_in, ab_conv_w, ab_q_norm, ab_k_norm, ab_w_out,
           na_w_qkv, na_rel_bias, na_w_out, mlp_w1, mlp_w2, final_norm_g):
    if "nc" not in _NC_CACHE:
        _NC_CACHE["nc"] = build_program()
    nc = _NC_CACHE["nc"]
    in_maps = make_in_maps(x_prompt, x_sample, cache_attn_k, cache_attn_v, cache_na_k, cache_na_v, c, c_ctx,
                           mod_w, mod_b, norm1_g, norm2_g, ab_w_in, ab_conv_w, ab_q_norm, ab_k_norm, ab_w_out,
                           na_w_qkv, na_rel_bias, na_w_out, mlp_w1, mlp_w2, final_norm_g)
    res = run_bass_kernel_spmd(nc, in_maps, core_ids=list(range(8)))
    return gather_outputs(res.results)


def make_in_maps(x_prompt, x_sample, cache_attn_k, cache_attn_v, cache_na_k, cache_na_v, c, c_ctx,
                 mod_w, mod_b, norm1_g, norm2_g, ab_w_in, ab_conv_w, ab_q_norm, ab_k_norm, ab_w_out,
                 na_w_qkv, na_rel_bias, na_w_out, mlp_w1, mlp_w2, final_norm_g):
    f32 = lambda a: np.ascontiguousarray(np.asarray(a, np.float32))

    x_prompt = f32(x_prompt)
    x_sample = f32(x_sample)
    shared = {
        "modw": f32(mod_w),
        "modb": np.ascontiguousarray(np.concatenate([_fm(mod_b[0]), _fm(mod_b[1])], axis=1)),
        "g1": np.ascontiguousarray(np.concatenate([_fm(norm1_g[0]), _fm(norm1_g[1])], axis=1)),
        "g2": np.ascontiguousarray(np.concatenate([_fm(norm2_g[0]), _fm(norm2_g[1])], axis=1)),
        "gf": _fm(final_norm_g),
        "w_in": f32(ab_w_in[0]),
        "convw": np.ascontiguousarray(np.asarray(ab_conv_w[0], np.float32).reshape(3, 8, 128).transpose(2, 1, 0).reshape(128, 24)),
        "qkn": np.ascontiguousarray(np.stack([np.asarray(ab_q_norm[0], np.float32), np.asarray(ab_k_norm[0], np.float32)], axis=1)),
        "w_out0": f32(ab_w_out[0]),
        "w_qkv": f32(na_w_qkv[0]),
        "w_out1": f32(na_w_out[0]),
        "w1": f32(mlp_w1),
        "w2": f32(mlp_w2),
        "cm": _na_tables(np.asarray(na_rel_bias[0], np.float32)),
        "ident": np.eye(128, dtype=np.float32),
    }
    lsel = np.zeros((2, 128), np.float32)
    lsel[0, 0:64] = 1.0
    lsel[1, 64:128] = 1.0
    shared["lsel"] = lsel
    prot = np.zeros((128, 128), np.float32)
    for m in range(128):
        if m % 64 < 32:
            prot[m + 32, m] = -1.0
        else:
            prot[m - 32, m] = 1.0
    shared["prot"] = prot

    in_maps = []
    for core in range(8):
        b, q = core // 4, core % 4
        W0 = 64 * (8 * q - 4)
        gtok = (W0 - 1 + np.arange(2048)) % 2048
        cos, sin = _rope_tables(gtok)
        cond = np.stack([_fm(c_ctx), _fm(c[b])], axis=2).reshape(128, 32)
        mlr = np.ones((128, 2), np.float32)
        if q == 0:
            mlr[:, 0] = 0.0
        if q == 3:
            mlr[:, 1] = 0.0
        m = dict(shared)
        m.update({
            "xp": np.ascontiguousarray(x_prompt[4 * core:4 * core + 4].reshape(1024, D)),
            "xs": np.ascontiguousarray(x_sample[b][gtok]),
            "ck0": f32(cache_attn_k[b, 0]).reshape(512, 256),
            "cv0": f32(cache_attn_v[b, 0]).reshape(512, 256),
            "ck1": f32(cache_na_k[b, 0]).reshape(512, D),
            "cv1": f32(cache_na_v[b, 0]).reshape(512, D),
            "condT": np.ascontiguousarray(cond),
            "rm": _rm_table(q),
            "cos": cos, "sin": sin, "mlr": mlr,
        })
        in_maps.append(m)
    return in_maps


def gather_outputs(r):
    y_prompt = np.concatenate([r[i]["yp"].reshape(4, 256, D) for i in range(8)], axis=0)
    y_sample = np.stack([np.concatenate([r[b * 4 + q]["ys"] for q in range(4)], axis=0) for b in range(2)], axis=0)
    ak = np.concatenate([r[i]["ak"].reshape(4, 1, 256, 2, 128) for i in range(8)], axis=0)
    av = np.concatenate([r[i]["av"].reshape(4, 1, 256, 2, 128) for i in range(8)], axis=0)
    nk = np.concatenate([r[i]["nk"].reshape(4, 1, 256, 16, 128) for i in range(8)], axis=0)
    nv = np.concatenate([r[i]["nv"].reshape(4, 1, 256, 16, 128) for i in range(8)], axis=0)
    return (y_prompt.astype(np.float32), y_sample.astype(np.float32), ak.astype(np.float32),
            av.astype(np.float32), nk.astype(np.float32), nv.astype(np.float32))
```

```python
import contextlib
import numpy as np
import ml_dtypes
import concourse.bass as bass
import concourse.mybir as mybir
from concourse.bass_utils import run_bass_kernel_spmd

F32 = mybir.dt.float32
BF16 = mybir.dt.bfloat16
AF = mybir.ActivationFunctionType
ALU = mybir.AluOpType

D = 2048
NEG = -30000.0
SCALE = 128.0 ** -0.5
EPS = 1e-6


class Tok:
    __slots__ = ("w", "r", "rd", "excl")

    def __init__(self, excl=False):
        self.excl = excl
        self.w = None
        self.r = {}
        self.rd = {}


class Slot:
    __slots__ = ("sem", "count")

    def __init__(self, sem):
        self.sem = sem
        self.count = 0


class Op:
    __slots__ = ("fn", "deps", "flag", "count", "slot")

    def __init__(self, fn, deps, slot=None):
        self.fn = fn
        self.deps = deps
        self.flag = False
        self.count = 0
        self.slot = slot


class Prog:
    ENGS = ("pe", "act", "dve", "pool", "sp")

    def __init__(self, nc, stack):
        self.nc = nc
        self.stack = stack
        self.ops = {e: [] for e in self.ENGS}
        self.esem = {e: stack.enter_context(nc.semaphore("es_" + e)) for e in ("pe", "act", "dve")}
        self.nslots = 0

    def slot(self):
        self.nslots += 1
        return Slot(self.stack.enter_context(self.nc.semaphore("ds%d" % self.nslots)))

    def _deps(self, eng, reads, writes, is_dma):
        deps = []
        for t in reads:
            if t.w is not None:
                w = t.w
                if w[0] == "d" or is_dma or w[1] != eng or eng != "pe":
                    deps.append(w)
        for t in writes:
            if t.w is not None:
                w = t.w
                if w[0] == "d" or is_dma or w[1] != eng or eng != "pe":
                    deps.append(w)
            for e, idx in t.r.items():
                if is_dma or e != eng or eng != "pe":
                    deps.append(("e", e, idx))
            for s, c in t.rd.items():
                deps.append(("d", s, c))
        return deps

    def op(self, eng, fn, reads=(), writes=()):
        if any(t.excl for t in reads):
            writes = list(writes) + [t for t in reads if t.excl]
            reads = [t for t in reads if not t.excl]
        deps = self._deps(eng, reads, writes, False)
        idx = len(self.ops[eng])
        self.ops[eng].append(Op(fn, deps))
        for t in reads:
            t.r[eng] = idx
        for t in writes:
            t.w = ("e", eng, idx)
            t.r = {}
            t.rd = {}

    def dma(self, q, fn, slot, reads=(), writes=()):
        deps = self._deps(q, reads, writes, True)
        slot.count += 16
        self.ops[q].append(Op(fn, deps, slot))
        for t in reads:
            t.rd[slot] = slot.count
        for t in writes:
            t.w = ("d", slot, slot.count)
            t.r = {}
            t.rd = {}

    def emit(self, block, final_slots):
        for e in ("pe", "act", "dve"):
            for o in self.ops[e]:
                for d in o.deps:
                    if d[0] == "e":
                        self.ops[d[1]][d[2]].flag = True
        for e in self.ENGS:
            for o in self.ops[e]:
                for d in o.deps:
                    if d[0] == "e":
                        self.ops[d[1]][d[2]].flag = True
        for e in ("pe", "act", "dve"):
            c = 0
            for o in self.ops[e]:
                if o.flag:
                    c += 1
                    o.count = c

        def runner(ename):
            def f(eng):
                seen = {}
                for o in self.ops[ename]:
                    for d in o.deps:
                        if d[0] == "e":
                            sem = self.esem[d[1]]
                            val = self.ops[d[1]][d[2]].count
                            key = d[1]
                        else:
                            sem = d[1].sem
                            val = d[2]
                            key = d[1]
                        if seen.get(key, 0) >= val:
                            continue
                        seen[key] = val
                        eng.wait_ge(sem, val)
                    ins = o.fn(eng)
                    if o.slot is not None:
                        ins.then_inc(o.slot.sem, 16)
                    elif o.flag:
                        ins.then_inc(self.esem[ename], 1)
                if ename == "sp":
                    for s in final_slots:
                        if s.count:
                            eng.wait_ge(s.sem, s.count)
            return f

        block.tensor(runner("pe"))
        block.scalar(runner("act"))
        block.vector(runner("dve"))
        block.gpsimd(runner("pool"))
        block.sync(runner("sp"))


class _Stop(Exception):
    pass


def build_program(stop=None, groups=(0, 1)):
    nc = bass.Bass("TRN2", target_bir_lowering=False)

    def din(name, shape):
        return nc.dram_tensor(name, list(shape), F32, kind="ExternalInput").ap()

    def dout(name, shape):
        return nc.dram_tensor(name, list(shape), F32, kind="ExternalOutput").ap()

    xp_d = din("xp", (1024, D))
    xs_d = din("xs", (2048, D))
    ck0_d = din("ck0", (512, 256))
    cv0_d = din("cv0", (512, 256))
    ck1_d = din("ck1", (512, D))
    cv1_d = din("cv1", (512, D))
    cond_d = din("condT", (128, 32))
    modw_d = din("modw", (2, D, 6 * D))
    modb_d = din("modb", (128, 192))
    g1_d = din("g1", (128, 32))
    g2_d = din("g2", (128, 32))
    gf_d = din("gf", (128, 16))
    win_d = din("w_in", (D, 4608))
    convw_d = din("convw", (128, 24))
    qk_d = din("qkn", (128, 2))
    wo0_d = din("w_out0", (D, D))
    wqkv_d = din("w_qkv", (D, 3 * D))
    wo1_d = din("w_out1", (D, D))
    w1_d = din("w1", (2, D, 4 * D))
    w2_d = din("w2", (2, 4 * D, D))
    cm_d = din("cm", (16, 128, 1408))
    rm_d = din("rm", (2, 8 * 512))
    lsel_d = din("lsel", (2, 128))
    cos_d = din("cos", (128, 2048))
    sin_d = din("sin", (128, 2048))
    prot_d = din("prot", (128, 128))
    ident_d = din("ident", (128, 128))
    mlr_d = din("mlr", (128, 2))

    yp_d = dout("yp", (1024, D))
    ys_d = dout("ys", (512, D))
    ak_d = dout("ak", (1024, 256))
    av_d = dout("av", (1024, 256))
    nk_d = dout("nk", (1024, D))
    nv_d = dout("nv", (1024, D))

    stack = contextlib.ExitStack()
    with stack:
        P = Prog(nc, stack)

        def sb(name, shape, dt):
            return stack.enter_context(nc.sbuf_tensor("sb_" + name, list(shape), dt))

        xres = sb("xres", (128, 16, 1024), F32)
        hbuf = sb("hbuf", (128, 16, 1024), BF16)
        mixb = sb("mixb", (128, 8, 1024), BF16)
        wsl = [sb("wsl%d" % i, (128, 4096), BF16) for i in range(4)]
        scrA = sb("scrA", (128, 4096), F32)
        scrB = sb("scrB", (128, 10496), BF16)
        ropeb = sb("ropeb", (128, 1960), F32)
        ident = sb("ident", (128, 128), F32)
        prot = sb("prot", (128, 128), F32)
        onesD = sb("onesD", (128, 128), BF16)
        onesH = sb("onesH", (128, 128), BF16)
        ones1 = sb("ones1", (128, 128), BF16)
        lsel = sb("lsel", (2, 128), BF16)
        condT = sb("condT", (128, 16, 2), F32)
        sil = sb("sil", (128, 16, 2), BF16)
        modb = sb("modb", (128, 2, 96), F32)
        modv = sb("modv", (128, 2, 96, 2), F32)
        g1 = sb("g1", (128, 2, 16), F32)
        g2 = sb("g2", (128, 2, 16), F32)
        gf = sb("gf", (128, 16), F32)
        convw = sb("convw", (128, 8, 3), F32)
        qkn = sb("qkn", (128, 2), F32)
        mlr = sb("mlr", (128, 2), F32)
        avec = sb("avec", (128, 2, 2, 2, 16), F32)
        sqr = sb("sqr", (128, 2, 512), BF16)
        rsr = sb("rsr", (128, 2, 512), F32)
        tmpr = sb("tmpr", (128, 3, 512), F32)
        pr = sb("pr", (128, 2, 512), BF16)

        dummy = sb("dummy", (128, 4), F32)
        epsT = sb("epsT", (128, 4), F32)
        psall = stack.enter_context(nc.psum_tensor("psall", [128, 8, 512], F32))
        ps = [psall[:, i, :] for i in range(8)]
        ps_t = [Tok(excl=True) for _ in range(8)]
        bank_ctr = [0]

        reserved = set()
        bank_of = {}

        def bank(hold=False):
            while True:
                b = bank_ctr[0] % 8
                bank_ctr[0] += 1
                if b not in reserved:
                    break
            if hold:
                reserved.add(b)
            bank_of[id(ps[b])] = b
            return ps[b], ps_t[b]

        pair_ctr = [0]

        def bank_pair(hold=False):
            while True:
                p = pair_ctr[0] % 4
                pair_ctr[0] += 1
                if 2 * p not in reserved and 2 * p + 1 not in reserved:
                    break
            if hold:
                reserved.add(2 * p)
                reserved.add(2 * p + 1)
            for b in (2 * p, 2 * p + 1):
                bank_of[id(ps[b])] = b
            return p, psall[:, 2 * p:2 * p + 2, :], [(ps[2 * p], ps_t[2 * p]), (ps[2 * p + 1], ps_t[2 * p + 1])]

        def release(pb):
            reserved.discard(bank_of[id(pb)])

        t_x = [[Tok() for _ in range(2)] for _ in range(16)]
        t_h = [Tok(), Tok()]
        t_mix = [Tok(), Tok()]
        t_w = [Tok() for _ in range(4)]
        w_slots = [P.slot() for _ in range(4)]
        w_ctr = [0]
        t_scrA = [Tok() for _ in range(12)]
        t_scrB = [Tok() for _ in range(8)]
        t_rope = Tok()
        t_const = Tok()
        t_mod = Tok()
        t_avec = Tok()
        t_sq = [Tok() for _ in range(4)]
        t_rs = [Tok() for _ in range(2)]
        t_tmp = [Tok() for _ in range(3)]
        t_pr = [Tok() for _ in range(4)]
        sq_c = [0]
        rs_c = [0]
        tmp_c = [0]
        pr_c = [0]

        t_dummy = Tok()

        def ckpt(name):
            if stop is not None and name == stop:
                raise _Stop()

        def fence(toks):
            P.op("dve", lambda e: e.memset(dummy[:, :], 0.0), writes=list(toks) + [t_dummy])

        def ring(ctr, n):
            i = ctr[0] % n
            ctr[0] += 1
            return i

        s_const = P.slot()
        s_in = [P.slot() for _ in range(4)]
        s_out = [P.slot() for _ in range(6)]
        s_rope = P.slot()
        s_rm = P.slot()

        def wload(parts):
            i = w_ctr[0] % 4
            w_ctr[0] += 1
            for dfn, src in parts:
                dst = dfn(wsl[i])
                P.dma("pool", (lambda e, dst=dst, src=src: e.dma_start(out=dst, in_=src)),
                      w_slots[i], writes=[t_w[i]])
            return wsl[i], t_w[i]

        def v3(t, k, n):
            return t[:, 0:k * n].rearrange("p (k n) -> p k n", k=k)

        def wcols(w_ap, c0, n):
            src = w_ap.rearrange("(k p) n -> p k n", p=128)[:, :, c0:c0 + n]
            return [(lambda t, n=n: v3(t, 16, n), src)]

        def wrows(w_ap, r0, kk, c0, n):
            src = w_ap[r0:r0 + kk * 128, c0:c0 + n].rearrange("(k p) n -> p k n", p=128)
            return [(lambda t, kk=kk, n=n: v3(t, kk, n), src)]

        def cload(dst, src, q="sp"):
            P.dma(q, (lambda e, dst=dst, src=src: e.dma_start(out=dst, in_=src)), s_const, writes=[t_const])

        cload(ident[:, :], ident_d)
        cload(prot[:, :], prot_d)
        cload(condT[:, :, :], cond_d.rearrange("p (k c) -> p k c", c=2))
        cload(modb[:, :, :], modb_d.rearrange("p (l j) -> p l j", l=2))
        cload(g1[:, :, :], g1_d.rearrange("p (l j) -> p l j", l=2))
        cload(g2[:, :, :], g2_d.rearrange("p (l j) -> p l j", l=2))
        cload(gf[:, :], gf_d)
        cload(convw[:, :, :], convw_d.rearrange("p (c t) -> p c t", t=3))
        cload(qkn[:, :], qk_d)
        cload(mlr[:, :], mlr_d)
        s_c2 = P.slot()
        t_c2 = Tok()
        P.dma("pool", lambda e: e.dma_start(out=lsel[:, :], in_=lsel_d), s_c2, writes=[t_c2])
        t_ones = Tok()
        P.op("dve", lambda e: e.memset(onesD[:, :], 1.0 / D), writes=[t_ones])
        P.op("dve", lambda e: e.memset(onesH[:, :], 1.0 / 128), writes=[t_ones])
        P.op("dve", lambda e: e.memset(ones1[:, :], 1.0), writes=[t_ones])
        P.op("dve", lambda e: e.memset(epsT[:, :], EPS), writes=[t_ones])

        _skip_mod = (stop == "consts")
        _stop_mod = (stop == "mod")
        t_modsec = [[Tok() for _ in range(6)] for _ in range(2)]
        t_avs = [[Tok(), Tok()] for _ in range(2)]
        P.op("act", lambda e: e.activation(out=sil[:, :, :], in_=condT[:, :, :], func=AF.Silu),
             reads=[t_const], writes=[t_mod])
        mod_queue = [(l, jb) for l in range(2) for jb in range(48)]

        def mod_block(l, jb):
            sec = jb // 8
            wt, wtok = wload(wcols(modw_d[l], jb * 256, 256))
            wv = v3(wt, 16, 256)
            pb, pt = bank()
            for jj in range(2):
                for k in range(16):
                    P.op("pe", (lambda e, pb=pb, wv=wv, jj=jj, k=k: e.matmul(
                        pb[:, jj * 2:jj * 2 + 2], wv[:, k, jj * 128:(jj + 1) * 128], sil[:, k, :],
                        start=(k == 0), stop=(k == 15))), reads=[wtok, t_mod], writes=[pt])
            for jj in range(2):
                P.op("dve", (lambda e, pb=pb, l=l, jb=jb, jj=jj: e.tensor_scalar(
                    out=modv[:, l, jb * 2 + jj, :], in0=pb[:, jj * 2:jj * 2 + 2],
                    scalar1=modb[:, l, jb * 2 + jj:jb * 2 + jj + 1], scalar2=None, op0=ALU.add)),
                     reads=[pt, t_const], writes=[t_modsec[l][sec]])
            if jb % 8 == 7 and sec in (1, 4):
                ni, gg, off = (0, g1, 16) if sec == 1 else (1, g2, 64)
                for ci in range(2):
                    P.op("dve", (lambda e, l=l, ci=ci, ni=ni, gg=gg, off=off: e.scalar_tensor_tensor(
                        out=avec[:, l, ci, ni, :], in0=modv[:, l, off:off + 16, ci], scalar=1.0,
                        in1=gg[:, l, :], op0=ALU.add, op1=ALU.mult)),
                         reads=[t_modsec[l][sec], t_const], writes=[t_avs[l][ni]])

        def pump(n):
            for _ in range(n):
                if mod_queue and not _skip_mod:
                    mod_block(*mod_queue.pop(0))

        def need_mod(l, sec):
            while mod_queue and mod_queue[0] <= (l, sec * 8 + 7) and not _skip_mod:
                mod_block(*mod_queue.pop(0))

        if _stop_mod:
            need_mod(1, 5)

        def mvec(l, ci, j):
            return modv[:, l, j, ci:ci + 1]

        stage = [scrA[:, 0:2048], scrA[:, 2048:4096]]
        stage_ctr = [0]

        def load_xT(dst, dtoks_fn, dcol0, src_rows, ntok):
            r0 = 0
            ei = 0
            while r0 < ntok:
                n = min(128, ntok - r0)
                si = stage_ctr[0] % 2
                stage_ctr[0] += 1
                st, stt = stage[si], t_scrA[si]
                P.dma("sp", (lambda e, st=st, r0=r0, n=n: e.dma_start(out=st[:n, :], in_=src_rows[r0:r0 + n, :])),
                      s_in[si], writes=[stt])
                for quad in range(4):
                    pb, pt = bank()
                    for cc in range(4):
                        c = quad * 4 + cc
                        P.op("pe", (lambda e, pb=pb, st=st, cc=cc, c=c, n=n: e.transpose(
                            pb[:, cc * 128:cc * 128 + n], st[:n, c * 128:(c + 1) * 128], ident[:n, :n])),
                             reads=[stt, t_const], writes=[pt])
                    eng = "act" if ei % 2 == 0 else "dve"
                    ei += 1
                    dv = dst[:, quad * 4:quad * 4 + 4, dcol0 + r0:dcol0 + r0 + n]
                    sv = pb[:, :].rearrange("p (c t) -> p c t", c=4)[:, :, 0:n]
                    wt = dtoks_fn(quad, dcol0 + r0, n)
                    if eng == "act":
                        P.op("act", (lambda e, dv=dv, sv=sv: e.activation(out=dv, in_=sv, func=AF.Copy)),
                             reads=[pt], writes=wt)
                    else:
                        P.op("dve", (lambda e, dv=dv, sv=sv: e.tensor_copy(out=dv, in_=sv)),
                             reads=[pt], writes=wt)
                r0 += n

        def rstd_tile(xsrc_fn, xtoks, tn, ones_t, nchunks):
            pb, pt = bank()
            for c in range(nchunks):
                qi = ring(sq_c, 2)
                src = xsrc_fn(c)
                if c % 3 == 2:
                    P.op("dve", (lambda e, qi=qi, src=src: e.tensor_tensor(out=sqr[:, qi, 0:tn], in0=src, in1=src, op=ALU.mult)),
                         reads=xtoks(c), writes=[t_sq[qi]])
                else:
                    P.op("act", (lambda e, qi=qi, src=src: e.activation(out=sqr[:, qi, 0:tn], in_=src, func=AF.Square)),
                         reads=xtoks(c), writes=[t_sq[qi]])
                P.op("pe", (lambda e, pb=pb, qi=qi, c=c: e.matmul(
                    pb[:, 0:tn], ones_t[:, :], sqr[:, qi, 0:tn], start=(c == 0), stop=(c == nchunks - 1))),
                     reads=[t_sq[qi], t_ones], writes=[pt])
            ri = ring(rs_c, 2)
            P.op("act", (lambda e, pb=pb, ri=ri: e.activation(
                out=rsr[:, ri, 0:tn], in_=pb[:, 0:tn], func=AF.Ln, bias=epsT[:, 0:1], scale=1.0)),
                 reads=[pt, t_ones], writes=[t_rs[ri]])
            P.op("act", (lambda e, ri=ri: e.activation(
                out=rsr[:, ri, 0:tn], in_=rsr[:, ri, 0:tn], func=AF.Exp, scale=-0.5)),
                 reads=[t_rs[ri]], writes=[t_rs[ri]])
            return rsr[:, ri, 0:tn], t_rs[ri]

        def warm(n):
            pb, pt = bank()
            for _ in range(n):
                P.op("pe", (lambda e, pb=pb: e.matmul(pb[:, 0:128], prot[:, :], ident[:, :], start=True, stop=True)),
                     reads=[t_const], writes=[pt])

        def norm_h(l, ci, ni, xt0, ht0, tn, xti, hti, warm_n=0):
            boff = 0 if ni == 0 else 48
            need_mod(l, 1 if ni == 0 else 4)
            tsh = t_modsec[l][0 if ni == 0 else 3]
            tav = t_avs[l][ni]
            rs, rst = rstd_tile(lambda c: xres[:, c, xt0:xt0 + tn], lambda c: [t_x[c][xti]], tn, onesD, 16)
            if warm_n:
                warm(warm_n)
            for c in range(16):
                ti = ring(tmp_c, 3)
                P.op("dve", (lambda e, c=c, ti=ti: e.scalar_tensor_tensor(
                    out=tmpr[:, ti, 0:tn], in0=xres[:, c, xt0:xt0 + tn], scalar=avec[:, l, ci, ni, c:c + 1],
                    in1=rs, op0=ALU.mult, op1=ALU.mult)), reads=[t_x[c][xti], rst, tav], writes=[t_tmp[ti]])
                P.op("act", (lambda e, c=c, ti=ti: e.activation(
                    out=hbuf[:, c, ht0:ht0 + tn], in_=tmpr[:, ti, 0:tn], func=AF.Identity,
                    bias=mvec(l, ci, boff + c), scale=1.0)), reads=[t_tmp[ti], tsh], writes=[t_h[hti]])

        def proj_fm(wv, wtok, col0, tiles, hts, consume):
            for ti, (h0, tn) in enumerate(tiles):
                pb, pt = bank()
                for k in range(16):
                    P.op("pe", (lambda e, pb=pb, k=k, h0=h0, tn=tn: e.matmul(
                        pb[:, 0:tn], wv[:, k, col0:col0 + 128], hbuf[:, k, h0:h0 + tn],
                        start=(k == 0), stop=(k == 15))), reads=[wtok, t_h[hts[ti]]], writes=[pt])
                consume(pb, pt, ti)

        def headnorm(pb, pt, tn, gcol):
            qi = ring(sq_c, 2)
            P.op("act", (lambda e, qi=qi: e.activation(out=sqr[:, qi, 0:tn], in_=pb[:, 0:tn], func=AF.Square)),
                 reads=[pt], writes=[t_sq[qi]])
            pb2, pt2 = bank()
            P.op("pe", (lambda e, qi=qi: e.matmul(pb2[:, 0:tn], onesH[:, :], sqr[:, qi, 0:tn], start=True, stop=True)),
                 reads=[t_sq[qi], t_ones], writes=[pt2])
            ri = ring(rs_c, 2)
            P.op("act", (lambda e, ri=ri: e.activation(
                out=rsr[:, ri, 0:tn], in_=pb2[:, 0:tn], func=AF.Ln, bias=epsT[:, 0:1], scale=1.0)),
                 reads=[pt2, t_ones], writes=[t_rs[ri]])
            P.op("act", (lambda e, ri=ri: e.activation(
                out=rsr[:, ri, 0:tn], in_=rsr[:, ri, 0:tn], func=AF.Exp, scale=-0.5)),
                 reads=[t_rs[ri]], writes=[t_rs[ri]])

            def write(out_ap, wtoks):
                P.op("dve", (lambda e: e.scalar_tensor_tensor(
                    out=out_ap, in0=pb[:, 0:tn], scalar=qkn[:, gcol:gcol + 1], in1=rsr[:, ri, 0:tn],
                    op0=ALU.mult, op1=ALU.mult)), reads=[pt, t_rs[ri], t_const], writes=wtoks)
            return write

        def rope(src_tmp_i, tn, tab0, out_ap, wtoks):
            src = tmpr[:, src_tmp_i, 0:tn]
            pb, pt = bank()
            P.op("pe", (lambda e: e.matmul(pb[:, 0:tn], prot[:, :], src, start=True, stop=True)),
                 reads=[t_tmp[src_tmp_i], t_const], writes=[pt])
            t2 = ring(tmp_c, 3)
            P.op("dve", (lambda e: e.tensor_tensor(out=tmpr[:, t2, 0:tn], in0=pb[:, 0:tn],
                                                   in1=ropeb[:, 992 + tab0:992 + tab0 + tn], op=ALU.mult)),
                 reads=[pt, t_rope], writes=[t_tmp[t2]])
            P.op("dve", (lambda e: e.tensor_tensor(out=src, in0=src, in1=ropeb[:, tab0:tab0 + tn], op=ALU.mult)),
                 reads=[t_tmp[src_tmp_i], t_rope], writes=[t_tmp[src_tmp_i]])
            P.op("dve", (lambda e: e.tensor_tensor(out=out_ap, in0=src, in1=tmpr[:, t2, 0:tn], op=ALU.add)),
                 reads=[t_tmp[src_tmp_i], t_tmp[t2]], writes=wtoks)

        def wout_part(w_ap, r0, kk, l, ci, gate_off, mix_fn, mtoks, tiles_x, tiles_m, xtis, npump=2):
            need_mod(l, 2)
            for cb in range(4):
                pump(npump)
                wt, wtok = wload(wrows(w_ap, r0, kk, cb * 512, 512))
                wv = v3(wt, kk, 512)
                for oc4 in range(4):
                    oc = cb * 4 + oc4
                    for ti, ((x0, tn), (m0, _)) in enumerate(zip(tiles_x, tiles_m)):
                        pb, pt = bank()
                        for k in range(kk):
                            P.op("pe", (lambda e, pb=pb, k=k, oc4=oc4, m0=m0, tn=tn, wv=wv: e.matmul(
                                pb[:, 0:tn], wv[:, k, oc4 * 128:(oc4 + 1) * 128], mix_fn(k, m0, tn),
                                start=(k == 0), stop=(k == kk - 1))), reads=[wtok] + mtoks(ti), writes=[pt])
                        xt = t_x[oc][xtis[ti]]
                        P.op("dve", (lambda e, pb=pb, oc=oc, x0=x0, tn=tn: e.scalar_tensor_tensor(
                            out=xres[:, oc, x0:x0 + tn], in0=pb[:, 0:tn], scalar=mvec(l, ci, gate_off + oc),
                            in1=xres[:, oc, x0:x0 + tn], op0=ALU.mult, op1=ALU.add)),
                             reads=[pt, t_modsec[l][2], xt], writes=[xt])

        def mlp(l, ci, tiles_x, tiles_h, xtis, htis):
            hid = scrB[:, 0:4096].rearrange("p (r c t) -> p r c t", r=4, c=2)
            need_mod(l, 5)
            for jp in range(16):
                w1s = []
                for jj in range(2):
                    jb = jp * 2 + jj
                    w1t, w1tok = wload(wcols(w1_d[l], jb * 256, 256))
                    w1s.append((v3(w1t, 16, 256), w1tok))
                for ti, ((x0, tn), (h0, _)) in enumerate(zip(tiles_x, tiles_h)):
                    for jj in range(2):
                        w1v, w1tok = w1s[jj]
                        hr = jj * 2 + ti
                        for hc in range(2):
                            pb, pt = bank()
                            for k in range(16):
                                P.op("pe", (lambda e, pb=pb, k=k, hc=hc, h0=h0, tn=tn, w1v=w1v: e.matmul(
                                    pb[:, 0:tn], w1v[:, k, hc * 128:(hc + 1) * 128], hbuf[:, k, h0:h0 + tn],
                                    start=(k == 0), stop=(k == 15))), reads=[w1tok, t_h[htis[ti]]], writes=[pt])
                            t1 = ring(tmp_c, 3)
                            P.op("act", (lambda e, pb=pb, t1=t1, tn=tn: e.activation(
                                out=tmpr[:, t1, 0:tn], in_=pb[:, 0:tn], func=AF.Relu)), reads=[pt], writes=[t_tmp[t1]])
                            P.op("act", (lambda e, t1=t1, hr=hr, hc=hc, tn=tn: e.activation(
                                out=hid[:, hr, hc, 0:tn], in_=tmpr[:, t1, 0:tn], func=AF.Square)),
                                 reads=[t_tmp[t1]], writes=[t_scrB[hr]])
                pump(1)
                w2s = []
                for jj in range(2):
                    jb = jp * 2 + jj
                    w2t, w2tok = wload(wrows(w2_d[l], jb * 256, 2, 0, 2048))
                    w2s.append((v3(w2t, 2, 2048), w2tok))
                for ti, ((x0, tn), (h0, _)) in enumerate(zip(tiles_x, tiles_h)):
                    for oc in range(16):
                        pb, pt = bank()
                        for jj in range(2):
                            w2v, w2tok = w2s[jj]
                            hr = jj * 2 + ti
                            for hc in range(2):
                                P.op("pe", (lambda e, pb=pb, oc=oc, hc=hc, hr=hr, tn=tn, w2v=w2v, jj=jj: e.matmul(
                                    pb[:, 0:tn], w2v[:, hc, oc * 128:(oc + 1) * 128], hid[:, hr, hc, 0:tn],
                                    start=(jj == 0 and hc == 0), stop=(jj == 1 and hc == 1))),
                                     reads=[w2tok, t_scrB[hr]], writes=[pt])
                        xt = t_x[oc][xtis[ti]]
                        P.op("dve", (lambda e, pb=pb, oc=oc, x0=x0, tn=tn: e.scalar_tensor_tensor(
                            out=xres[:, oc, x0:x0 + tn], in0=pb[:, 0:tn], scalar=mvec(l, ci, 80 + oc),
                            in1=xres[:, oc, x0:x0 + tn], op0=ALU.mult, op1=ALU.add)),
                             reads=[pt, t_modsec[l][5], xt], writes=[xt])

        def attn_core(chunks, tn, o_out, big=False):
            nch = len(chunks)
            groups = []
            i = 0
            while i < nch:
                if (big and i + 1 < nch and chunks[i][0] == 128 and chunks[i + 1][0] == 128
                        and chunks[i][2] is None and chunks[i + 1][2] is None):
                    groups.append([i, i + 1])
                    i += 2
                else:
                    groups.append([i])
                    i += 1
            if big:
                LA = 2
                pview = lambda g: prA6[:, 2 * (g % 3):2 * (g % 3) + 2, :]
                ptok = lambda g: t_scrA[8 + g % 3]
                _, _, hp = bank_pair(hold=True)
                (pO, ptO), (pD, ptD) = hp
            else:
                LA = 1
                pview = lambda g: pr[:, g % 2:g % 2 + 1, :]
                ptok = lambda g: t_pr[g % 2]
                pO, ptO = bank(hold=True)
                pD, ptD = bank(hold=True)
            base = pr_c[0]
            pr_c[0] += len(groups)
            for gi_ in range(len(groups) + LA):
                if gi_ < len(groups):
                    g = groups[gi_]
                    pv_, ptk = pview(base + gi_), ptok(base + gi_)
                    if len(g) == 2:
                        _, pview2, bl = bank_pair()
                        for (pb, pt), ci_ in zip(bl, g):
                            chunks[ci_][1](pb, pt)
                        P.op("act", (lambda e, pview2=pview2, pv_=pv_: e.activation(
                            out=pv_[:, :, 0:tn], in_=pview2[:, :, 0:tn], func=AF.Exp, scale=SCALE)),
                             reads=[bl[0][1], bl[1][1]], writes=[ptk])
                    else:
                        n, score_fn, bias_ap, v_ap, vtoks = chunks[g[0]]
                        pb, pt = bank()
                        score_fn(pb, pt)
                        if bias_ap is None:
                            P.op("act", (lambda e, pb=pb, pv_=pv_, n=n: e.activation(
                                out=pv_[:n, 0, 0:tn], in_=pb[:n, 0:tn], func=AF.Exp, scale=SCALE)),
                                 reads=[pt], writes=[ptk])
                        else:
                            t1 = ring(tmp_c, 3)
                            P.op("dve", (lambda e, pb=pb, t1=t1, n=n, bias_ap=bias_ap: e.scalar_tensor_tensor(
                                out=tmpr[:n, t1, 0:tn], in0=pb[:n, 0:tn], scalar=SCALE, in1=bias_ap,
                                op0=ALU.mult, op1=ALU.add)), reads=[pt, t_rope], writes=[t_tmp[t1]])
                            P.op("act", (lambda e, pv_=pv_, t1=t1, n=n: e.activation(
                                out=pv_[:n, 0, 0:tn], in_=tmpr[:n, t1, 0:tn], func=AF.Exp)),
                                 reads=[t_tmp[t1]], writes=[ptk])
                gj = gi_ - LA
                if gj >= 0:
                    pv_, ptk = pview(base + gj), ptok(base + gj)
                    for k_, j in enumerate(groups[gj]):
                        n, _, _, v_ap, vtoks = chunks[j]
                        P.op("pe", (lambda e, pv_=pv_, n=n, v_ap=v_ap, j=j, k_=k_: e.matmul(
                            pO[:, 0:tn], v_ap, pv_[:n, k_, 0:tn], start=(j == 0), stop=(j == nch - 1))),
                             reads=[ptk] + vtoks, writes=[ptO])
                        P.op("pe", (lambda e, pv_=pv_, n=n, j=j, k_=k_: e.matmul(
                            pD[:, 0:tn], ones1[:n, :], pv_[:n, k_, 0:tn], start=(j == 0), stop=(j == nch - 1))),
                             reads=[ptk, t_ones], writes=[ptD])
            ri = ring(rs_c, 2)
            P.op("act", (lambda e, ri=ri: e.activation(out=rsr[:, ri, 0:tn], in_=pD[:, 0:tn], func=AF.Ln)),
                 reads=[ptD], writes=[t_rs[ri]])
            P.op("act", (lambda e, ri=ri: e.activation(out=rsr[:, ri, 0:tn], in_=rsr[:, ri, 0:tn], func=AF.Exp, scale=-1.0)),
                 reads=[t_rs[ri]], writes=[t_rs[ri]])
            o_out(pO, ptO, rsr[:, ri, 0:tn], t_rs[ri])
            release(pO)
            release(pD)

        prA6 = scrA[:, 2048:3584].bitcast(BF16).rearrange("p (r t) -> p r t", r=6)

        def final_out(out_d, x0, ntok, xti_of):
            t0 = 0
            while t0 < ntok:
                tn = min(512, ntok - t0)
                xti = xti_of(t0)
                rs, rst = rstd_tile(lambda c: xres[:, c, x0 + t0:x0 + t0 + tn], lambda c: [t_x[c][xti]], tn, onesD, 16)
                for c in range(16):
                    P.op("dve", (lambda e, c=c, t0=t0, tn=tn, rs=rs: e.scalar_tensor_tensor(
                        out=xres[:, c, x0 + t0:x0 + t0 + tn], in0=xres[:, c, x0 + t0:x0 + t0 + tn],
                        scalar=gf[:, c:c + 1], in1=rs, op0=ALU.mult, op1=ALU.mult)),
                         reads=[t_x[c][xti], rst, t_const], writes=[t_x[c][xti]])
                for tc in range(tn // 128):
                    si = stage_ctr[0] % 2
                    stage_ctr[0] += 1
                    st, stt = stage[si], t_scrA[si]
                    for quad in range(4):
                        pb, pt = bank()
                        for cc in range(4):
                            c = quad * 4 + cc
                            P.op("pe", (lambda e, pb=pb, cc=cc, c=c, tc=tc, t0=t0: e.transpose(
                                pb[:, cc * 128:(cc + 1) * 128],
                                xres[:, c, x0 + t0 + tc * 128:x0 + t0 + (tc + 1) * 128], ident[:, :])),
                                 reads=[t_x[c][xti], t_const], writes=[pt])
                        if quad % 2 == 0:
                            P.op("act", (lambda e, pb=pb, st=st, quad=quad: e.activation(
                                out=st[:, quad * 512:(quad + 1) * 512], in_=pb[:, :], func=AF.Copy)),
                                 reads=[pt], writes=[stt])
                        else:
                            P.op("dve", (lambda e, pb=pb, st=st, quad=quad: e.tensor_copy(
                                out=st[:, quad * 512:(quad + 1) * 512], in_=pb[:, :])), reads=[pt], writes=[stt])
                    r = t0 + tc * 128
                    P.dma("sp", (lambda e, st=st, r=r: e.dma_start(out=out_d[r:r + 128, :], in_=st[:, :])),
                          s_out[si], reads=[stt])
                t0 += tn

        for c in range(16):
            t_x[c].append(Tok())

        def fence_own():
            P.op("dve", lambda e: e.memset(dummy[:, :], 0.0),
                 writes=[t_x[c][i] for c in range(16) for i in range(3)] + [t_dummy])

        def run_group(gi):
            is_s = gi == 1
            ci = gi
            T = 962 if is_s else 1024
            tiles = [(0, 512), (512, T - 512)]
            xt_all = lambda quad, c0, n: [t_x[quad * 4 + cc][c0 // 512] for cc in range(4)] + \
                ([t_x[quad * 4 + cc][(c0 + n - 1) // 512] for cc in range(4)] if (c0 + n - 1) // 512 != c0 // 512 else [])

            fence(t_scrA)
            fence(t_scrB)
            fence(t_mix)
            if not is_s:
                load_xT(xres, xt_all, 0, xp_d, T)
            need_mod(0, 1)

            ckpt("g%d_load" % gi)
            l = 0
            kT = scrB[:, 0:5120].rearrange("p (g t) -> p g t", g=2)
            vtok = scrB[:, 5120:5120 + 21 * 256].rearrange("p (c n) -> p c n", n=256)
            t_kT, t_vt = t_scrB[4], t_scrB[5]
            koff = 512 if is_s else 0
            if is_s:
                kchunks = [(i * 128, 128, i) for i in range(4)]
                kchunks += [(512 + i * 128, 128, 4 + i) for i in range(7)] + [(512 + 896, 66, 11)]
                kchunks += [(512 + 962 + i * 128, 128, 12 + i) for i in range(8)] + [(512 + 962 + 1024, 62, 20)]

            wk, wktok = wload(wcols(win_d, 4096, 256))
            wkv = v3(wk, 16, 256)
            wv_, wvtok = wload(wcols(win_d, 4352, 256))
            wvv = v3(wv_, 16, 256)

            mixf = mixb[:, :, :].rearrange("p c t -> p (c t)").bitcast(F32)
            kst = mixf[:, 0:2048].rearrange("p (c n) -> p c n", n=256)
            vst = mixf[:, 2048:4096].rearrange("p (c n) -> p c n", n=256)

            def kv_for_tile(h0, tn, hti, key0, vchunk0, tab0, out_tok0):
                for hh in range(2):
                    pb, pt = bank()
                    for k in range(16):
                        P.op("pe", (lambda e, pb=pb, k=k, hh=hh: e.matmul(
                            pb[:, 0:tn], wkv[:, k, hh * 128:(hh + 1) * 128], hbuf[:, k, h0:h0 + tn],
                            start=(k == 0), stop=(k == 15))), reads=[wktok, t_h[hti]], writes=[pt])
                    wr = headnorm(pb, pt, tn, 1)
                    t1 = ring(tmp_c, 3)
                    wr(tmpr[:, t1, 0:tn], [t_tmp[t1]])
                    if is_s:
                        rope(t1, tn, tab0, kT[:, hh, key0:key0 + tn], [t_kT])
                    else:
                        P.op("act", (lambda e, t1=t1, hh=hh: e.activation(
                            out=kT[:, hh, key0:key0 + tn], in_=tmpr[:, t1, 0:tn], func=AF.Copy)),
                             reads=[t_tmp[t1]], writes=[t_kT])
                        pb2, pt2 = bank()
                        for tc in range(tn // 128):
                            P.op("pe", (lambda e, pb2=pb2, t1=t1, tc=tc: e.transpose(
                                pb2[:, tc * 128:(tc + 1) * 128], tmpr[:, t1, tc * 128:(tc + 1) * 128], ident[:, :])),
                                 reads=[t_tmp[t1], t_const], writes=[pt2])
                        c0 = out_tok0 // 128
                        P.op("dve", (lambda e, pb2=pb2, hh=hh, c0=c0: e.tensor_copy(
                            out=kst[:, c0:c0 + tn // 128, hh * 128:(hh + 1) * 128],
                            in_=pb2[:, 0:tn].rearrange("p (c d) -> p c d", d=128))), reads=[pt2], writes=[t_mix[0], t_mix[1]])
                ckpt("g%d_kvK" % gi)
                c = 0
                r0 = 0
                while r0 < tn:
                    n = min(128, tn - r0)
                    pb, pt = bank()
                    for k in range(16):
                        P.op("pe", (lambda e, pb=pb, k=k, r0=r0, n=n: e.matmul(
                            pb[:n, 0:256], hbuf[:, k, h0 + r0:h0 + r0 + n], wvv[:, k, :],
                            start=(k == 0), stop=(k == 15))), reads=[wvtok, t_h[hti]], writes=[pt])
                    vc = vchunk0 + c
                    P.op("act", (lambda e, pb=pb, vc=vc, n=n: e.activation(
                        out=vtok[:n, vc, :], in_=pb[:n, 0:256], func=AF.Copy)), reads=[pt], writes=[t_vt])
                    if not is_s:
                        oc_ = (out_tok0 + r0) // 128
                        P.op("dve", (lambda e, pb=pb, oc_=oc_: e.tensor_copy(out=vst[:, oc_, :], in_=pb[:, 0:256])),
                             reads=[pt], writes=[t_mix[0], t_mix[1]])
                    c += 1
                    r0 += n

            if is_s:
                cks = tmpr[:, 0:2, :].rearrange("p a (b n) -> p (a b) n", n=256)
                P.dma("sp", lambda e: e.dma_start(out=cks, in_=ck0_d.rearrange("(c p) n -> p c n", p=128)),
                      s_in[2], writes=[t_tmp[0], t_tmp[1]])
                for hh in range(2):
                    pb, pt = bank()
                    for c in range(4):
                        P.op("pe", (lambda e, pb=pb, c=c, hh=hh: e.transpose(
                            pb[:, c * 128:(c + 1) * 128], cks[:, c, hh * 128:(hh + 1) * 128], ident[:, :])),
                             reads=[t_tmp[0], t_tmp[1], t_const], writes=[pt])
                    P.op("act", (lambda e, pb=pb, hh=hh: e.activation(out=kT[:, hh, 0:512], in_=pb[:, :], func=AF.Copy)),
                         reads=[pt], writes=[t_kT])
                P.dma("pool", lambda e: e.dma_start(out=vtok[:, 0:4, :], in_=cv0_d.rearrange("(c p) n -> p c n", p=128)),
                      s_in[3], writes=[t_vt])
                rest_tiles = [(962, 0, 512, 12), (1474, 512, 512, 16), (1986, 0, 62, 20)]
                for (r, xc, tn, vch) in rest_tiles:
                    ti_ = xc // 512
                    load_xT(xres, xt_all, xc, xs_d[r:r + tn, :], tn)
                    P.dma("sp", (lambda e, r=r, tn=tn: e.dma_start(out=ropeb[:, 0:tn], in_=cos_d[:, r:r + tn])),
                          s_rope, writes=[t_rope])
                    P.dma("sp", (lambda e, r=r, tn=tn: e.dma_start(out=ropeb[:, 992:992 + tn], in_=sin_d[:, r:r + tn])),
                          s_rope, writes=[t_rope])
                    norm_h(0, ci, 0, xc, xc, tn, ti_, ti_, 0)
                    kv_for_tile(xc, tn, ti_, 512 + r, vch, 0, 0)
                P.dma("sp", lambda e: e.dma_start(out=ropeb[:, 0:962], in_=cos_d[:, 0:962]), s_rope, writes=[t_rope])
                P.dma("sp", lambda e: e.dma_start(out=ropeb[:, 992:992 + 962], in_=sin_d[:, 0:962]), s_rope,
                      writes=[t_rope])

            if is_s:
                load_xT(xres, xt_all, 0, xs_d, T)
            for ti, (t0, tn) in enumerate(tiles):
                norm_h(0, ci, 0, t0, t0, tn, ti, ti, 24 if ti == 0 else 0)

            ckpt("g%d_norm" % gi)
            for ti, (t0, tn) in enumerate(tiles):
                kv_for_tile(t0, tn, ti, koff + t0, (4 if is_s else 0) + t0 // 128, t0, t0)
            ckpt("g%d_kvV" % gi)
            if not is_s:
                P.dma("sp", lambda e: e.dma_start(out=ak_d.rearrange("(c p) n -> p c n", p=128), in_=kst),
                      s_out[2], reads=[t_mix[0], t_mix[1]])
                P.dma("sp", lambda e: e.dma_start(out=av_d.rearrange("(c p) n -> p c n", p=128), in_=vst),
                      s_out[3], reads=[t_mix[0], t_mix[1]])

            ckpt("g%d_kv" % gi)
            nseq, L = (1, 962) if is_s else (4, 256)
            ub = scrA[:, 0:nseq * (L + 2)].rearrange("p (s t) -> p s t", s=nseq)
            vb = [scrA[:, 1040:1040 + T], scrA[:, 2080:2080 + T]]
            t_u, t_v = t_scrA[3], [t_scrA[4], t_scrA[5]]
            P.op("dve", lambda e: e.memset(ub, 0.0), writes=list(t_scrA))
            for cp in range(4):
                for cc in range(2):
                    c = cp * 2 + cc
                    pump(1)
                    wt, wtok = wload([(lambda t: v3(t, 16, 256)[:, :, 0:128],
                                       win_d.rearrange("(k p) n -> p k n", p=128)[:, :, 2048 + c * 128:2048 + (c + 1) * 128]),
                                      (lambda t: v3(t, 16, 256)[:, :, 128:256],
                                       win_d.rearrange("(k p) n -> p k n", p=128)[:, :, 1024 + c * 128:1024 + (c + 1) * 128])])
                    wv = v3(wt, 16, 256)

                    def uview(t0, tn):
                        if is_s:
                            return ub[:, 0, 1 + t0:1 + t0 + tn]
                        return ub[:, t0 // 256:(t0 + tn) // 256, 1:257]

                    def cons_xa(pb, pt, ti):
                        t0, tn = tiles[ti]
                        src = pb[:, 0:tn] if is_s else pb[:, 0:tn].rearrange("p (s t) -> p s t", t=256)
                        P.op("act", (lambda e: e.activation(out=uview(t0, tn), in_=src, func=AF.Copy)),
                             reads=[pt], writes=[t_u])

                    def cons_gc(pb, pt, ti):
                        t0, tn = tiles[ti]
                        src = pb[:, 0:tn] if is_s else pb[:, 0:tn].rearrange("p (s t) -> p s t", t=256)
                        P.op("dve", (lambda e: e.tensor_tensor(out=uview(t0, tn), in0=src, in1=uview(t0, tn), op=ALU.mult)),
                             reads=[pt, t_u], writes=[t_u])
                    proj_fm(wv, wtok, 0, tiles, [0, 1], cons_xa)
                    proj_fm(wv, wtok, 128, tiles, [0, 1], cons_gc)
                    if is_s:
                        P.op("dve", lambda e: e.tensor_scalar(out=ub[:, 0, 257:258], in0=ub[:, 0, 257:258],
                                                              scalar1=mlr[:, 0:1], scalar2=None, op0=ALU.mult),
                             reads=[t_u, t_const], writes=[t_u])
                        P.op("dve", lambda e: e.tensor_scalar(out=ub[:, 0, 770:771], in0=ub[:, 0, 770:771],
                                                              scalar1=mlr[:, 1:2], scalar2=None, op0=ALU.mult),
                             reads=[t_u, t_const], writes=[t_u])
                    v3d = vb[cc].rearrange("p (s t) -> p s t", s=nseq)
                    P.op("dve", (lambda e, c=c, v3d=v3d: e.tensor_scalar(
                        out=v3d, in0=ub[:, :, 1:L + 1], scalar1=convw[:, c, 1:2], scalar2=None, op0=ALU.mult)),
                         reads=[t_u, t_const], writes=[t_v[cc]])
                    P.op("dve", (lambda e, c=c, v3d=v3d: e.scalar_tensor_tensor(
                        out=v3d, in0=ub[:, :, 0:L], scalar=convw[:, c, 0:1], in1=v3d, op0=ALU.mult, op1=ALU.add)),
                         reads=[t_u, t_const, t_v[cc]], writes=[t_v[cc]])
                    P.op("dve", (lambda e, c=c, v3d=v3d: e.scalar_tensor_tensor(
                        out=v3d, in0=ub[:, :, 2:L + 2], scalar=convw[:, c, 2:3], in1=v3d, op0=ALU.mult, op1=ALU.add)),
                         reads=[t_u, t_const, t_v[cc]], writes=[t_v[cc]])
                pump(1)
                wt, wtok = wload(wcols(win_d, cp * 256, 256))
                wv = v3(wt, 16, 256)
                for cc in range(2):
                    c = cp * 2 + cc

                    def cons_gb(pb, pt, ti, c=c, cc=cc):
                        t0, tn = tiles[ti]
                        P.op("dve", (lambda e: e.tensor_tensor(out=mixb[:, c, t0:t0 + tn], in0=pb[:, 0:tn],
                                                               in1=vb[cc][:, t0:t0 + tn], op=ALU.mult)),
                             reads=[pt, t_v[cc]], writes=[t_mix[ti]])
                    proj_fm(wv, wtok, cc * 128, tiles, [0, 1], cons_gb)
            ckpt("g%d_conv" % gi)
            wout_part(wo0_d, 0, 8, 0, ci, 32, lambda k, m0, tn: mixb[:, k, m0:m0 + tn],
                      lambda ti: [t_mix[ti]], tiles, tiles, [0, 1])
            ckpt("g%d_woutA" % gi)

            fence(t_scrA)
            qT = scrA[:, 0:2048].bitcast(BF16).rearrange("p (h t) -> p h t", h=4)
            t_q = t_scrA[6]
            for g in range(2):
                for qb in range(2):
                    pump(1)
                    wt, wtok = wload(wcols(win_d, 3072 + g * 512 + qb * 256, 256))
                    wv = v3(wt, 16, 256)
                    for hh in range(2):
                        hq = qb * 2 + hh

                        def cons_q(pb, pt, ti, hq=hq):
                            t0, tn = tiles[ti]
                            wr = headnorm(pb, pt, tn, 0)
                            if is_s:
                                t1 = ring(tmp_c, 3)
                                wr(tmpr[:, t1, 0:tn], [t_tmp[t1]])
                                rope(t1, tn, t0, qT[:, hq, t0:t0 + tn], [t_q])
                            else:
                                wr(qT[:, hq, t0:t0 + tn], [t_q])
                        proj_fm(wv, wtok, hh * 128, tiles, [0, 1], cons_q)
                if is_s:
                    for hq in range(4):
                        for ti, (t0, tn) in enumerate(tiles):
                            chunks = []
                            for (k0, n, vc) in kchunks:
                                def sfn(pb, pt, k0=k0, n=n, hq=hq, t0=t0, tn=tn, g=g):
                                    P.op("pe", (lambda e: e.matmul(pb[:n, 0:tn], kT[:, g, k0:k0 + n], qT[:, hq, t0:t0 + tn],
                                                                   start=True, stop=True)),
                                         reads=[t_kT, t_q], writes=[pt])
                                chunks.append((n, sfn, None, vtok[:n, vc, g * 128:(g + 1) * 128], [t_vt]))

                            def oout(pO, ptO, rc, rct, hq=hq, t0=t0, tn=tn, ti=ti, g=g):
                                P.op("dve", (lambda e: e.tensor_tensor(out=mixb[:, 4 * g + hq, t0:t0 + tn], in0=pO[:, 0:tn],
                                                                       in1=rc, op=ALU.mult)),
                                     reads=[ptO, rct], writes=[t_mix[ti]])
                            attn_core(chunks, tn, oout, True)
                else:
                    for s in range(4):
                        for hp in range(2):
                            chunks = []
                            for kc in range(2):
                                k0 = s * 256 + kc * 128

                                def sfn(pb, pt, k0=k0, hp=hp, s=s, g=g):
                                    P.op("pe", (lambda e: e.matmul(pb[:, 0:512], kT[:, g, k0:k0 + 128],
                                                                   qT[:, 2 * hp:2 * hp + 2, s * 256:(s + 1) * 256],
                                                                   start=True, stop=True)),
                                         reads=[t_kT, t_q], writes=[pt])
                                chunks.append((128, sfn, None, vtok[:, s * 2 + kc, g * 128:(g + 1) * 128], [t_vt]))

                            def oout(pO, ptO, rc, rct, hp=hp, s=s, g=g):
                                P.op("dve", (lambda e: e.tensor_tensor(
                                    out=mixb[:, 4 * g + 2 * hp:4 * g + 2 * hp + 2, s * 256:(s + 1) * 256],
                                    in0=pO[:, 0:512].rearrange("p (h t) -> p h t", h=2),
                                    in1=rc.rearrange("p (h t) -> p h t", h=2), op=ALU.mult)),
                                     reads=[ptO, rct], writes=[t_mix[s // 2]])
                            attn_core(chunks, 512, oout)
            ckpt("g%d_attn0" % gi)
            wout_part(wo0_d, 1024, 8, 0, ci, 32, lambda k, m0, tn: mixb[:, k, m0:m0 + tn],
                      lambda ti: [t_mix[ti]], tiles, tiles, [0, 1])

            fence(t_scrB)
            for ti, (t0, tn) in enumerate(tiles):
                norm_h(0, ci, 1, t0, t0, tn, ti, ti, 24 if ti == 0 else 0)
            ckpt("g%d_mlpnorm" % gi)
            mlp(0, ci, tiles, tiles, [0, 1], [0, 1])
            ckpt("g%d_l0" % gi)

            fence(t_scrA)
            fence(t_scrB)
            for ti, (t0, tn) in enumerate(tiles):
                norm_h(1, ci, 0, t0, t0, tn, ti, ti, 24 if ti == 0 else 0)
            q2 = scrA[:, 0:1024].bitcast(BF16).rearrange("p (h t) -> p h t", h=2)
            t_q2 = t_scrA[6]
            if is_s:
                kT2 = scrB[:, 0:2 * 1474].rearrange("p (h t) -> p h t", h=2)
                vt2 = scrB[:, 3072:3072 + 12 * 256].rearrange("p (c n) -> p c n", n=256)
                cks1 = tmpr[:, 0:2, :].rearrange("p a (b n) -> p (a b) n", n=256)
                rmb = scrB[0:2, 6144:6144 + 4096]
                P.dma("pool", lambda e: e.dma_start(out=rmb, in_=rm_d), s_rm, writes=[t_scrB[6]])
            else:
                kT2 = scrB[:, 0:2048].rearrange("p (h t) -> p h t", h=2)
                vt2 = scrB[:, 3072:3072 + 8 * 256].rearrange("p (c n) -> p c n", n=256)
                kst1 = scrA[:, 1024:2048].rearrange("p (c n) -> p c n", n=256)
                vst1 = scrA[:, 2048:4096].rearrange("p (c n) -> p c n", n=256)
            t_k2, t_v2 = t_scrB[4], t_scrB[5]
            for half in range(2):
                for pair in range(4):
                    hp0 = (half * 4 + pair) * 2
                    col = hp0 * 128
                    pump(3)
                    wq, wqtok = wload(wcols(wqkv_d, col, 256))
                    wqv = v3(wq, 16, 256)
                    wk2, wk2tok = wload(wcols(wqkv_d, 2048 + col, 256))
                    wk2v = v3(wk2, 16, 256)
                    wv2, wv2tok = wload(wcols(wqkv_d, 4096 + col, 256))
                    wv2v = v3(wv2, 16, 256)
                    for hh in range(2):
                        if is_s:
                            pb, pt = bank()
                            for k in range(16):
                                P.op("pe", (lambda e, pb=pb, k=k, hh=hh, wqv=wqv: e.matmul(
                                    pb[:, 0:512], wqv[:, k, hh * 128:(hh + 1) * 128], hbuf[:, k, 257:769],
                                    start=(k == 0), stop=(k == 15))), reads=[wqtok, t_h[0], t_h[1]], writes=[pt])
                            P.op("act", (lambda e, pb=pb, hh=hh: e.activation(out=q2[:, hh, 0:512], in_=pb[:, :], func=AF.Copy)),
                                 reads=[pt], writes=[t_q2])
                        else:
                            def cons_q2(pb, pt, ti, hh=hh):
                                t0, tn = tiles[ti]
                                P.op("act", (lambda e: e.activation(out=q2[:, hh, t0:t0 + tn], in_=pb[:, 0:tn], func=AF.Copy)),
                                     reads=[pt], writes=[t_q2])
                            proj_fm(wqv, wqtok, hh * 128, tiles, [0, 1], cons_q2)
                    if is_s:
                        P.dma("sp", (lambda e, col=col: e.dma_start(
                            out=cks1, in_=ck1_d[:, col:col + 256].rearrange("(c p) n -> p c n", p=128))),
                              s_in[2], writes=[t_tmp[0], t_tmp[1]])
                        for hh in range(2):
                            pb, pt = bank()
                            for c in range(4):
                                P.op("pe", (lambda e, pb=pb, c=c, hh=hh: e.transpose(
                                    pb[:, c * 128:(c + 1) * 128], cks1[:, c, hh * 128:(hh + 1) * 128], ident[:, :])),
                                     reads=[t_tmp[0], t_tmp[1], t_const], writes=[pt])
                            P.op("act", (lambda e, pb=pb, hh=hh: e.activation(out=kT2[:, hh, 0:512], in_=pb[:, :], func=AF.Copy)),
                                 reads=[pt], writes=[t_k2])
                        P.dma("pool", (lambda e, col=col: e.dma_start(
                            out=vt2[:, 0:4, :], in_=cv1_d[:, col:col + 256].rearrange("(c p) n -> p c n", p=128))),
                              s_in[3], writes=[t_v2])
                    for hh in range(2):
                        def cons_k2(pb, pt, ti, hh=hh, col=col):
                            t0, tn = tiles[ti]
                            if is_s:
                                P.op("act", (lambda e: e.activation(out=kT2[:, hh, 512 + t0:512 + t0 + tn], in_=pb[:, 0:tn],
                                                                    func=AF.Copy)), reads=[pt], writes=[t_k2])
                                return
                            t1 = ring(tmp_c, 3)
                            P.op("act", (lambda e: e.activation(out=tmpr[:, t1, 0:tn], in_=pb[:, 0:tn], func=AF.Copy)),
                                 reads=[pt], writes=[t_tmp[t1]])
                            P.op("dve", (lambda e: e.tensor_copy(out=kT2[:, hh, t0:t0 + tn], in_=pb[:, 0:tn])),
                                 reads=[pt], writes=[t_k2])
                            pb2, pt2 = bank()
                            for tc in range(4):
                                P.op("pe", (lambda e, tc=tc: e.transpose(
                                    pb2[:, tc * 128:(tc + 1) * 128], tmpr[:, t1, tc * 128:(tc + 1) * 128], ident[:, :])),
                                     reads=[t_tmp[t1], t_const], writes=[pt2])
                            P.op("dve", (lambda e: e.tensor_copy(
                                out=kst1[:, :, hh * 128:(hh + 1) * 128],
                                in_=pb2[:, :].rearrange("p (c d) -> p c d", d=128))), reads=[pt2], writes=[t_scrA[3]])
                            if hh == 1:
                                P.dma("sp", (lambda e: e.dma_start(
                                    out=nk_d[t0:t0 + 512, col:col + 256].rearrange("(c p) n -> p c n", p=128), in_=kst1)),
                                      s_out[2], reads=[t_scrA[3]])
                        if is_s:
                            proj_fm(wk2v, wk2tok, hh * 128, tiles, [0, 1], cons_k2)
                    if not is_s:
                        for ti in range(2):
                            for hh in range(2):
                                proj_fm(wk2v, wk2tok, hh * 128, [tiles[ti]], [ti],
                                        (lambda pb, pt, _ti, hh=hh, ti=ti, col=col: cons_k2(pb, pt, ti, hh, col)))
                    if is_s:
                        vrows = [(1 + 128 * m, 128 if m < 7 else 64, 4 + m) for m in range(8)]
                    else:
                        vrows = [(128 * m, 128, m) for m in range(8)]
                    for (r0, n, vc) in vrows:
                        pb, pt = bank()
                        for k in range(16):
                            P.op("pe", (lambda e, pb=pb, k=k, r0=r0, n=n, wv2v=wv2v: e.matmul(
                                pb[:n, 0:256], hbuf[:, k, r0:r0 + n], wv2v[:, k, :], start=(k == 0), stop=(k == 15))),
                                 reads=[wv2tok, t_h[0], t_h[1]], writes=[pt])
                        P.op("act", (lambda e, pb=pb, vc=vc, n=n: e.activation(out=vt2[:n, vc, :], in_=pb[:n, 0:256], func=AF.Copy)),
                             reads=[pt], writes=[t_v2])
                        if not is_s:
                            P.op("dve", (lambda e, pb=pb, vc=vc: e.tensor_copy(out=vst1[:, vc, :], in_=pb[:, 0:256])),
                                 reads=[pt], writes=[t_scrA[4]])
                    if not is_s:
                        P.dma("sp", (lambda e, col=col: e.dma_start(
                            out=nv_d[:, col:col + 256].rearrange("(c p) n -> p c n", p=128), in_=vst1)),
                              s_out[3], reads=[t_scrA[4]])
                    pend = []
                    for hh in range(2):
                        mi = pair * 2 + hh
                        if is_s:
                            head = hp0 + hh
                            P.dma("sp", (lambda e, head=head: e.dma_start(out=ropeb[:, 0:1408], in_=cm_d[head])),
                                  s_rope, writes=[t_rope])
                            chunks = []
                            for m in range(8):
                                n = 128 if m < 7 else 64
                                k0 = 512 + 1 + 128 * m

                                def sfn(pb, pt, m=m, n=n, k0=k0, hh=hh):
                                    P.op("pe", (lambda e: e.matmul(pb[:n, 0:512], kT2[:, hh, k0:k0 + n], q2[:, hh, 0:512],
                                                                   start=True, stop=False)),
                                         reads=[t_k2, t_q2], writes=[pt])
                                    P.op("pe", (lambda e: e.matmul(pb[:n, 0:512], lsel[:, 0:n], rmb[:, m * 512:(m + 1) * 512],
                                                                   start=False, stop=True)),
                                         reads=[t_c2, t_scrB[6]], writes=[pt])
                                b0 = (14 - 2 * m) * 64
                                chunks.append((n, sfn, ropeb[:n, b0:b0 + 512], vt2[:n, 4 + m, hh * 128:(hh + 1) * 128], [t_v2]))
                            for c in range(4):
                                def sfn(pb, pt, c=c, hh=hh):
                                    P.op("pe", (lambda e: e.matmul(pb[:, 0:512], kT2[:, hh, c * 128:(c + 1) * 128], q2[:, hh, 0:512],
                                                                   start=True, stop=True)),
                                         reads=[t_k2, t_q2], writes=[pt])
                                chunks.append((128, sfn, None, vt2[:, c, hh * 128:(hh + 1) * 128], [t_v2]))

                            def oout(pO, ptO, rc, rct, mi=mi):
                                P.op("dve", (lambda e: e.tensor_tensor(out=mixb[:, mi, 0:512], in0=pO[:, 0:512], in1=rc, op=ALU.mult)),
                                     reads=[ptO, rct], writes=[t_mix[0]])
                            attn_core(chunks, 512, oout, True)
                        else:
                            for s in range(4):
                                pb, pt = bank()
                                for kc in range(2):
                                    k0 = s * 256 + kc * 128
                                    P.op("pe", (lambda e, pb=pb, kc=kc, k0=k0, hh=hh, s=s: e.matmul(
                                        pb[:, kc * 256:(kc + 1) * 256], kT2[:, hh, k0:k0 + 128],
                                        q2[:, hh, s * 256:(s + 1) * 256], start=True, stop=True)),
                                         reads=[t_k2, t_q2], writes=[pt])
                                ui = ring(pr_c, 2)
                                P.op("act", (lambda e, pb=pb, ui=ui: e.activation(
                                    out=pr[:, ui, :], in_=pb[:, :], func=AF.Exp, scale=SCALE)),
                                     reads=[pt], writes=[t_pr[ui]])

                                def tail(ui=ui, s=s, hh=hh, mi=mi):
                                    pO, ptO = bank()
                                    for kc in range(2):
                                        P.op("pe", (lambda e, kc=kc: e.matmul(
                                            pO[:, 0:256], vt2[:, s * 2 + kc, hh * 128:(hh + 1) * 128],
                                            pr[:, ui, kc * 256:(kc + 1) * 256], start=(kc == 0), stop=(kc == 1))),
                                             reads=[t_pr[ui], t_v2], writes=[ptO])
                                    for kc in range(2):
                                        P.op("pe", (lambda e, kc=kc: e.matmul(
                                            pO[:, 256:512], ones1[:, :], pr[:, ui, kc * 256:(kc + 1) * 256],
                                            start=(kc == 0), stop=(kc == 1))), reads=[t_pr[ui], t_ones], writes=[ptO])
                                    ri = ring(rs_c, 2)
                                    P.op("act", (lambda e: e.activation(out=rsr[:, ri, 0:256], in_=pO[:, 256:512], func=AF.Ln)),
                                         reads=[ptO], writes=[t_rs[ri]])
                                    P.op("act", (lambda e: e.activation(out=rsr[:, ri, 0:256], in_=rsr[:, ri, 0:256],
                                                                        func=AF.Exp, scale=-1.0)),
                                         reads=[t_rs[ri]], writes=[t_rs[ri]])
                                    P.op("dve", (lambda e: e.tensor_tensor(out=mixb[:, mi, s * 256:(s + 1) * 256],
                                                                           in0=pO[:, 0:256], in1=rsr[:, ri, 0:256], op=ALU.mult)),
                                         reads=[ptO, t_rs[ri]], writes=[t_mix[s // 2]])
                                if pend:
                                    pend.pop(0)()
                                pend.append(tail)
                    while pend:
                        pend.pop(0)()
                if is_s:
                    if half == 0:
                        fence_own()
                    wout_part(wo1_d, half * 1024, 8, 1, ci, 32, lambda k, m0, tn: mixb[:, k, m0:m0 + tn],
                              lambda ti: [t_mix[0]], [(257, 512)], [(0, 512)], [2])
                else:
                    wout_part(wo1_d, half * 1024, 8, 1, ci, 32, lambda k, m0, tn: mixb[:, k, m0:m0 + tn],
                              lambda ti: [t_mix[ti]], tiles, tiles, [0, 1])

            ckpt("g%d_attn1" % gi)
            if is_s:
                pass
            return is_s

        def run_tail(gi):
            is_s = gi == 1
            ci = gi
            if not is_s:
                tiles = [(0, 512), (512, 512)]
                for ti, (t0, tn) in enumerate(tiles):
                    norm_h(1, ci, 1, t0, t0, tn, ti, ti, 24 if ti == 0 else 0)
                fence(t_scrB)
                mlp(1, ci, tiles, tiles, [0, 1], [0, 1])
                fence(t_scrA)
                final_out(yp_d, 0, 1024, lambda t0: t0 // 512)
            else:
                norm_h(1, ci, 1, 257, 0, 512, 2, 0, 24)
                fence(t_scrB)
                mlp(1, ci, [(257, 512)], [(0, 512)], [2], [0])
                fence(t_scrA)
                final_out(ys_d, 257, 512, lambda t0: 2)


        try:
            if _skip_mod or _stop_mod:
                raise _Stop()
            if 0 in groups:
                run_group(0)
                run_tail(0)
            ckpt("g0")
            if 1 in groups:
                run_group(1)
                run_tail(1)
        except _Stop:
            pass

        with nc.Block() as block:
            P.emit(block, s_out)
    return nc


_NC_CACHE = {}


def _fm(v):
    v = np.asarray(v, np.float32)
    return np.ascontiguousarray(v.reshape(-1, 128).T)


def _na_tables(rel_bias):
    H = rel_bias.shape[0]
    tab = np.full((H, 128, 22, 64), NEG, np.float32)
    qc = np.arange(64)
    kc0 = np.clip(qc - 8, 0, 48)
    for half in range(2):
        for kc in range(64):
            p = half * 64 + kc
            inwin = (kc >= kc0) & (kc < kc0 + 16)
            dc = kc - qc + 15
            for j in range(22):
                dr = 17 - j + half
                if 0 <= dr < 15:
                    vals = rel_bias[:, dr, np.clip(dc, 0, 30)]
                    tab[:, p, j, :] = np.where(inwin[None, :], vals, NEG)
    return np.ascontiguousarray(tab.reshape(H, 128, 22 * 64))


def _rm_table(q):
    rm = np.full((2, 8, 8, 64), NEG, np.float32)
    for m in range(8):
        for half in range(2):
            lk = 2 * m + half
            kr = 8 * q - 4 + lk
            for lq in range(4, 12):
                qr = 8 * q - 4 + lq
                kr0 = min(max(qr - 4, 0), 24)
                if 0 <= kr < 32 and kr0 <= kr < kr0 + 8 and lk < 15:
                    rm[half, m, lq - 4, :] = 0.0
    return np.ascontiguousarray(rm.reshape(2, 8 * 512))


def _rope_tables(gtok):
    half = 32
    inv = (10000.0 ** (-np.arange(half, dtype=np.float32) / half)).astype(np.float32)
    row = (gtok // 64).astype(np.float32)
    colp = (gtok % 64).astype(np.float32)
    cos = np.zeros((128, gtok.shape[0]), np.float32)
    sin = np.zeros((128, gtok.shape[0]), np.float32)
    for m in range(128):
        pos = row if m < 64 else colp
        ang = pos * inv[m % 32]
        cos[m] = np.cos(ang.astype(np.float32))
        sin[m] = np.sin(ang.astype(np.float32))
    return cos, sin


def kernel(x_prompt, x_sample, cache_attn_k, cache_attn_v, cache_na_k, cache_na_v, c, c_ctx,
           mod_w, mod_b, norm1_g, norm2_g, ab_w_in, ab_conv_w, ab_q_norm, ab_k_norm, ab_w_out,
           na_w_qkv, na_rel_bias, na_w_out, mlp_w1, mlp_w2, final_norm_g):
    if "nc" not in _NC_CACHE:
        _NC_CACHE["nc"] = build_program()
    nc = _NC_CACHE["nc"]
    in_maps = make_in_maps(x_prompt, x_sample, cache_attn_k, cache_attn_v, cache_na_k, cache_na_v, c, c_ctx,
                           mod_w, mod_b, norm1_g, norm2_g, ab_w_in, ab_conv_w, ab_q_norm, ab_k_norm, ab_w_out,
                           na_w_qkv, na_rel_bias, na_w_out, mlp_w1, mlp_w2, final_norm_g)
    res = run_bass_kernel_spmd(nc, in_maps, core_ids=list(range(8)))
    return gather_outputs(res.results)


def make_in_maps(x_prompt, x_sample, cache_attn_k, cache_attn_v, cache_na_k, cache_na_v, c, c_ctx,
                 mod_w, mod_b, norm1_g, norm2_g, ab_w_in, ab_conv_w, ab_q_norm, ab_k_norm, ab_w_out,
                 na_w_qkv, na_rel_bias, na_w_out, mlp_w1, mlp_w2, final_norm_g):
    f32 = lambda a: np.ascontiguousarray(np.asarray(a, np.float32))

    x_prompt = f32(x_prompt)
    x_sample = f32(x_sample)
    shared = {
        "modw": f32(mod_w),
        "modb": np.ascontiguousarray(np.concatenate([_fm(mod_b[0]), _fm(mod_b[1])], axis=1)),
        "g1": np.ascontiguousarray(np.concatenate([_fm(norm1_g[0]), _fm(norm1_g[1])], axis=1)),
        "g2": np.ascontiguousarray(np.concatenate([_fm(norm2_g[0]), _fm(norm2_g[1])], axis=1)),
        "gf": _fm(final_norm_g),
        "w_in": f32(ab_w_in[0]),
        "convw": np.ascontiguousarray(np.asarray(ab_conv_w[0], np.float32).reshape(3, 8, 128).transpose(2, 1, 0).reshape(128, 24)),
        "qkn": np.ascontiguousarray(np.stack([np.asarray(ab_q_norm[0], np.float32), np.asarray(ab_k_norm[0], np.float32)], axis=1)),
        "w_out0": f32(ab_w_out[0]),
        "w_qkv": f32(na_w_qkv[0]),
        "w_out1": f32(na_w_out[0]),
        "w1": f32(mlp_w1),
        "w2": f32(mlp_w2),
        "cm": _na_tables(np.asarray(na_rel_bias[0], np.float32)),
        "ident": np.eye(128, dtype=np.float32),
    }
    lsel = np.zeros((2, 128), np.float32)
    lsel[0, 0:64] = 1.0
    lsel[1, 64:128] = 1.0
    shared["lsel"] = lsel
    prot = np.zeros((128, 128), np.float32)
    for m in range(128):
        if m % 64 < 32:
            prot[m + 32, m] = -1.0
        else:
            prot[m - 32, m] = 1.0
    shared["prot"] = prot

    in_maps = []
    for core in range(8):
        b, q = core // 4, core % 4
        W0 = 64 * (8 * q - 4)
        gtok = (W0 - 1 + np.arange(2048)) % 2048
        cos, sin = _rope_tables(gtok)
        cond = np.stack([_fm(c_ctx), _fm(c[b])], axis=2).reshape(128, 32)
        mlr = np.ones((128, 2), np.float32)
        if q == 0:
            mlr[:, 0] = 0.0
        if q == 3:
            mlr[:, 1] = 0.0
        m = dict(shared)
        m.update({
            "xp": np.ascontiguousarray(x_prompt[4 * core:4 * core + 4].reshape(1024, D)),
            "xs": np.ascontiguousarray(x_sample[b][gtok]),
            "ck0": f32(cache_attn_k[b, 0]).reshape(512, 256),
            "cv0": f32(cache_attn_v[b, 0]).reshape(512, 256),
            "ck1": f32(cache_na_k[b, 0]).reshape(512, D),
            "cv1": f32(cache_na_v[b, 0]).reshape(512, D),
            "condT": np.ascontiguousarray(cond),
            "rm": _rm_table(q),
            "cos": cos, "sin": sin, "mlr": mlr,
        })
        in_maps.append(m)
    return in_maps


def gather_outputs(r):
    y_prompt = np.concatenate([r[i]["yp"].reshape(4, 256, D) for i in range(8)], axis=0)
    y_sample = np.stack([np.concatenate([r[b * 4 + q]["ys"] for q in range(4)], axis=0) for b in range(2)], axis=0)
    ak = np.concatenate([r[i]["ak"].reshape(4, 1, 256, 2, 128) for i in range(8)], axis=0)
    av = np.concatenate([r[i]["av"].reshape(4, 1, 256, 2, 128) for i in range(8)], axis=0)
    nk = np.concatenate([r[i]["nk"].reshape(4, 1, 256, 16, 128) for i in range(8)], axis=0)
    nv = np.concatenate([r[i]["nv"].reshape(4, 1, 256, 16, 128) for i in range(8)], axis=0)
    return (y_prompt.astype(np.float32), y_sample.astype(np.float32), ak.astype(np.float32),
            av.astype(np.float32), nk.astype(np.float32), nv.astype(np.float32))
```

```python
import contextlib
import numpy as np
import ml_dtypes
import concourse.bass as bass
import concourse.mybir as mybir
from concourse.bass_utils import run_bass_kernel_spmd

F32 = mybir.dt.float32
BF16 = mybir.dt.bfloat16
AF = mybir.ActivationFunctionType
ALU = mybir.AluOpType

D = 2048
NEG = -30000.0
SCALE = 128.0 ** -0.5
EPS = 1e-6


class Tok:
    __slots__ = ("w", "r", "rd", "excl")

    def __init__(self, excl=False):
        self.excl = excl
        self.w = None
        self.r = {}
        self.rd = {}


class Slot:
    __slots__ = ("sem", "count")

    def __init__(self, sem):
        self.sem = sem
        self.count = 0


class Op:
    __slots__ = ("fn", "deps", "flag", "count", "slot")

    def __init__(self, fn, deps, slot=None):
        self.fn = fn
        self.deps = deps
        self.flag = False
        self.count = 0
        self.slot = slot


class Prog:
    ENGS = ("pe", "act", "dve", "pool", "sp")

    def __init__(self, nc, stack):
        self.nc = nc
        self.stack = stack
        self.ops = {e: [] for e in self.ENGS}
        self.esem = {e: stack.enter_context(nc.semaphore("es_" + e)) for e in ("pe", "act", "dve")}
        self.nslots = 0

    def slot(self):
        self.nslots += 1
        return Slot(self.stack.enter_context(self.nc.semaphore("ds%d" % self.nslots)))

    def _deps(self, eng, reads, writes, is_dma):
        deps = []
        for t in reads:
            if t.w is not None:
                w = t.w
                if w[0] == "d" or is_dma or w[1] != eng or eng != "pe":
                    deps.append(w)
        for t in writes:
            if t.w is not None:
                w = t.w
                if w[0] == "d" or is_dma or w[1] != eng or eng != "pe":
                    deps.append(w)
            for e, idx in t.r.items():
                if is_dma or e != eng or eng != "pe":
                    deps.append(("e", e, idx))
            for s, c in t.rd.items():
                deps.append(("d", s, c))
        return deps

    def op(self, eng, fn, reads=(), writes=()):
        if any(t.excl for t in reads):
            writes = list(writes) + [t for t in reads if t.excl]
            reads = [t for t in reads if not t.excl]
        deps = self._deps(eng, reads, writes, False)
        idx = len(self.ops[eng])
        self.ops[eng].append(Op(fn, deps))
        for t in reads:
            t.r[eng] = idx
        for t in writes:
            t.w = ("e", eng, idx)
            t.r = {}
            t.rd = {}

    def dma(self, q, fn, slot, reads=(), writes=()):
        deps = self._deps(q, reads, writes, True)
        slot.count += 16
        self.ops[q].append(Op(fn, deps, slot))
        for t in reads:
            t.rd[slot] = slot.count
        for t in writes:
            t.w = ("d", slot, slot.count)
            t.r = {}
            t.rd = {}

    def emit(self, block, final_slots):
        for e in ("pe", "act", "dve"):
            for o in self.ops[e]:
                for d in o.deps:
                    if d[0] == "e":
                        self.ops[d[1]][d[2]].flag = True
        for e in self.ENGS:
            for o in self.ops[e]:
                for d in o.deps:
                    if d[0] == "e":
                        self.ops[d[1]][d[2]].flag = True
        for e in ("pe", "act", "dve"):
            c = 0
            for o in self.ops[e]:
                if o.flag:
                    c += 1
                    o.count = c

        def runner(ename):
            def f(eng):
                seen = {}
                for o in self.ops[ename]:
                    for d in o.deps:
                        if d[0] == "e":
                            sem = self.esem[d[1]]
                            val = self.ops[d[1]][d[2]].count
                            key = d[1]
                        else:
                            sem = d[1].sem
                            val = d[2]
                            key = d[1]
                        if seen.get(key, 0) >= val:
                            continue
                        seen[key] = val
                        eng.wait_ge(sem, val)
                    ins = o.fn(eng)
                    if o.slot is not None:
                        ins.then_inc(o.slot.sem, 16)
                    elif o.flag:
                        ins.then_inc(self.esem[ename], 1)
                if ename == "sp":
                    for s in final_slots:
                        if s.count:
                            eng.wait_ge(s.sem, s.count)
            return f

        block.tensor(runner("pe"))
        block.scalar(runner("act"))
        block.vector(runner("dve"))
        block.gpsimd(runner("pool"))
        block.sync(runner("sp"))


class _Stop(Exception):
    pass


def build_program(stop=None, groups=(0, 1)):
    nc = bass.Bass("TRN2", target_bir_lowering=False)

    def din(name, shape):
        return nc.dram_tensor(name, list(shape), F32, kind="ExternalInput").ap()

    def dout(name, shape):
        return nc.dram_tensor(name, list(shape), F32, kind="ExternalOutput").ap()

    xp_d = din("xp", (1024, D))
    xs_d = din("xs", (2048, D))
    ck0_d = din("ck0", (512, 256))
    cv0_d = din("cv0", (512, 256))
    ck1_d = din("ck1", (512, D))
    cv1_d = din("cv1", (512, D))
    cond_d = din("condT", (128, 32))
    modw_d = din("modw", (2, D, 6 * D))
    modb_d = din("modb", (128, 192))
    g1_d = din("g1", (128, 32))
    g2_d = din("g2", (128, 32))
    gf_d = din("gf", (128, 16))
    win_d = din("w_in", (D, 4608))
    convw_d = din("convw", (128, 24))
    qk_d = din("qkn", (128, 2))
    wo0_d = din("w_out0", (D, D))
    wqkv_d = din("w_qkv", (D, 3 * D))
    wo1_d = din("w_out1", (D, D))
    w1_d = din("w1", (2, D, 4 * D))
    w2_d = din("w2", (2, 4 * D, D))
    cm_d = din("cm", (16, 128, 1408))
    rm_d = din("rm", (2, 8 * 512))
    lsel_d = din("lsel", (2, 128))
    cos_d = din("cos", (128, 2048))
    sin_d = din("sin", (128, 2048))
    prot_d = din("prot", (128, 128))
    ident_d = din("ident", (128, 128))
    mlr_d = din("mlr", (128, 2))

    yp_d = dout("yp", (1024, D))
    ys_d = dout("ys", (512, D))
    ak_d = dout("ak", (1024, 256))
    av_d = dout("av", (1024, 256))
    nk_d = dout("nk", (1024, D))
    nv_d = dout("nv", (1024, D))

    stack = contextlib.ExitStack()
    with stack:
        P = Prog(nc, stack)

        def sb(name, shape, dt):
            return stack.enter_context(nc.sbuf_tensor("sb_" + name, list(shape), dt))

        xres = sb("xres", (128, 16, 1024), F32)
        hbuf = sb("hbuf", (128, 16, 1024), BF16)
        mixb = sb("mixb", (128, 8, 1024), BF16)
        wsl = [sb("wsl%d" % i, (128, 4096), BF16) for i in range(4)]
        scrA = sb("scrA", (128, 4096), F32)
        scrB = sb("scrB", (128, 10496), BF16)
        ropeb = sb("ropeb", (128, 1960), F32)
        ident = sb("ident", (128, 128), F32)
        prot = sb("prot", (128, 128), F32)
        onesD = sb("onesD", (128, 128), BF16)
        onesH = sb("onesH", (128, 128), BF16)
        ones1 = sb("ones1", (128, 128), BF16)
        lsel = sb("lsel", (2, 128), BF16)
        condT = sb("condT", (128, 16, 2), F32)
        sil = sb("sil", (128, 16, 2), BF16)
        modb = sb("modb", (128, 2, 96), F32)
        modv = sb("modv", (128, 2, 96, 2), F32)
        g1 = sb("g1", (128, 2, 16), F32)
        g2 = sb("g2", (128, 2, 16), F32)
        gf = sb("gf", (128, 16), F32)
        convw = sb("convw", (128, 8, 3), F32)
        qkn = sb("qkn", (128, 2), F32)
        mlr = sb("mlr", (128, 2), F32)
        avec = sb("avec", (128, 2, 2, 2, 16), F32)
        sqr = sb("sqr", (128, 2, 512), BF16)
        rsr = sb("rsr", (128, 2, 512), F32)
        tmpr = sb("tmpr", (128, 3, 512), F32)
        pr = sb("pr", (128, 2, 512), BF16)

        dummy = sb("dummy", (128, 4), F32)
        epsT = sb("epsT", (128, 4), F32)
        psall = stack.enter_context(nc.psum_tensor("psall", [128, 8, 512], F32))
        ps = [psall[:, i, :] for i in range(8)]
        ps_t = [Tok(excl=True) for _ in range(8)]
        bank_ctr = [0]

        reserved = set()
        bank_of = {}

        def bank(hold=False):
            while True:
                b = bank_ctr[0] % 8
                bank_ctr[0] += 1
                if b not in reserved:
                    break
            if hold:
                reserved.add(b)
            bank_of[id(ps[b])] = b
            return ps[b], ps_t[b]

        pair_ctr = [0]

        def bank_pair(hold=False):
            while True:
                p = pair_ctr[0] % 4
                pair_ctr[0] += 1
                if 2 * p not in reserved and 2 * p + 1 not in reserved:
                    break
            if hold:
                reserved.add(2 * p)
                reserved.add(2 * p + 1)
            for b in (2 * p, 2 * p + 1):
                bank_of[id(ps[b])] = b
            return p, psall[:, 2 * p:2 * p + 2, :], [(ps[2 * p], ps_t[2 * p]), (ps[2 * p + 1], ps_t[2 * p + 1])]

        def release(pb):
            reserved.discard(bank_of[id(pb)])

        t_x = [[Tok() for _ in range(2)] for _ in range(16)]
        t_h = [Tok(), Tok()]
        t_mix = [Tok(), Tok()]
        t_w = [Tok() for _ in range(4)]
        w_slots = [P.slot() for _ in range(4)]
        w_ctr = [0]
        t_scrA = [Tok() for _ in range(12)]
        t_scrB = [Tok() for _ in range(8)]
        t_rope = Tok()
        t_const = Tok()
        t_mod = Tok()
        t_avec = Tok()
        t_sq = [Tok() for _ in range(4)]
        t_rs = [Tok() for _ in range(2)]
        t_tmp = [Tok() for _ in range(3)]
        t_pr = [Tok() for _ in range(4)]
        sq_c = [0]
        rs_c = [0]
        tmp_c = [0]
        pr_c = [0]

        t_dummy = Tok()

        def ckpt(name):
            if stop is not None and name == stop:
                raise _Stop()

        def fence(toks):
            P.op("dve", lambda e: e.memset(dummy[:, :], 0.0), writes=list(toks) + [t_dummy])

        def ring(ctr, n):
            i = ctr[0] % n
            ctr[0] += 1
            return i

        s_const = P.slot()
        s_in = [P.slot() for _ in range(4)]
        s_out = [P.slot() for _ in range(6)]
        s_rope = P.slot()
        s_rm = P.slot()

        def wload(parts):
            i = w_ctr[0] % 4
            w_ctr[0] += 1
            for dfn, src in parts:
                dst = dfn(wsl[i])
                P.dma("pool", (lambda e, dst=dst, src=src: e.dma_start(out=dst, in_=src)),
                      w_slots[i], writes=[t_w[i]])
            return wsl[i], t_w[i]

        def v3(t, k, n):
            return t[:, 0:k * n].rearrange("p (k n) -> p k n", k=k)

        def wcols(w_ap, c0, n):
            src = w_ap.rearrange("(k p) n -> p k n", p=128)[:, :, c0:c0 + n]
            return [(lambda t, n=n: v3(t, 16, n), src)]

        def wrows(w_ap, r0, kk, c0, n):
            src = w_ap[r0:r0 + kk * 128, c0:c0 + n].rearrange("(k p) n -> p k n", p=128)
            return [(lambda t, kk=kk, n=n: v3(t, kk, n), src)]

        def cload(dst, src, q="sp"):
            P.dma(q, (lambda e, dst=dst, src=src: e.dma_start(out=dst, in_=src)), s_const, writes=[t_const])

        cload(ident[:, :], ident_d)
        cload(prot[:, :], prot_d)
        cload(condT[:, :, :], cond_d.rearrange("p (k c) -> p k c", c=2))
        cload(modb[:, :, :], modb_d.rearrange("p (l j) -> p l j", l=2))
        cload(g1[:, :, :], g1_d.rearrange("p (l j) -> p l j", l=2))
        cload(g2[:, :, :], g2_d.rearrange("p (l j) -> p l j", l=2))
        cload(gf[:, :], gf_d)
        cload(convw[:, :, :], convw_d.rearrange("p (c t) -> p c t", t=3))
        cload(qkn[:, :], qk_d)
        cload(mlr[:, :], mlr_d)
        s_c2 = P.slot()
        t_c2 = Tok()
        P.dma("pool", lambda e: e.dma_start(out=lsel[:, :], in_=lsel_d), s_c2, writes=[t_c2])
        t_ones = Tok()
        P.op("dve", lambda e: e.memset(onesD[:, :], 1.0 / D), writes=[t_ones])
        P.op("dve", lambda e: e.memset(onesH[:, :], 1.0 / 128), writes=[t_ones])
        P.op("dve", lambda e: e.memset(ones1[:, :], 1.0), writes=[t_ones])
        P.op("dve", lambda e: e.memset(epsT[:, :], EPS), writes=[t_ones])

        _skip_mod = (stop == "consts")
        _stop_mod = (stop == "mod")
        t_modsec = [[Tok() for _ in range(6)] for _ in range(2)]
        t_avs = [[Tok(), Tok()] for _ in range(2)]
        P.op("act", lambda e: e.activation(out=sil[:, :, :], in_=condT[:, :, :], func=AF.Silu),
             reads=[t_const], writes=[t_mod])
        mod_queue = [(l, jb) for l in range(2) for jb in range(48)]

        s_modalt = P.slot()

        def mod_block(l, jb, alt=False):
            sec = jb // 8
            if alt:
                wv = scrB[:, 4096:8192].rearrange("p (k n) -> p k n", k=16)
                wtok = t_scrB[7]
                src = modw_d[l].rearrange("(k p) n -> p k n", p=128)[:, :, jb * 256:(jb + 1) * 256]
                P.dma("pool", (lambda e, wv=wv, src=src: e.dma_start(out=wv, in_=src)), s_modalt, writes=[wtok])
            else:
                wt, wtok = wload(wcols(modw_d[l], jb * 256, 256))
                wv = v3(wt, 16, 256)
            pb, pt = bank()
            for jj in range(2):
                for k in range(16):
                    P.op("pe", (lambda e, pb=pb, wv=wv, jj=jj, k=k: e.matmul(
                        pb[:, jj * 2:jj * 2 + 2], wv[:, k, jj * 128:(jj + 1) * 128], sil[:, k, :],
                        start=(k == 0), stop=(k == 15))), reads=[wtok, t_mod], writes=[pt])
            for jj in range(2):
                P.op("dve", (lambda e, pb=pb, l=l, jb=jb, jj=jj: e.tensor_scalar(
                    out=modv[:, l, jb * 2 + jj, :], in0=pb[:, jj * 2:jj * 2 + 2],
                    scalar1=modb[:, l, jb * 2 + jj:jb * 2 + jj + 1], scalar2=None, op0=ALU.add)),
                     reads=[pt, t_const], writes=[t_modsec[l][sec]])
            if jb % 8 == 7 and sec in (1, 4):
                ni, gg, off = (0, g1, 16) if sec == 1 else (1, g2, 64)
                for ci in range(2):
                    P.op("dve", (lambda e, l=l, ci=ci, ni=ni, gg=gg, off=off: e.scalar_tensor_tensor(
                        out=avec[:, l, ci, ni, :], in0=modv[:, l, off:off + 16, ci], scalar=1.0,
                        in1=gg[:, l, :], op0=ALU.add, op1=ALU.mult)),
                         reads=[t_modsec[l][sec], t_const], writes=[t_avs[l][ni]])

        def pump(n, alt=False):
            for _ in range(n):
                if mod_queue and not _skip_mod:
                    mod_block(*mod_queue.pop(0), alt=alt)

        def need_mod(l, sec):
            while mod_queue and mod_queue[0] <= (l, sec * 8 + 7) and not _skip_mod:
                mod_block(*mod_queue.pop(0))

        if _stop_mod:
            need_mod(1, 5)

        def mvec(l, ci, j):
            return modv[:, l, j, ci:ci + 1]

        stage = [scrA[:, 0:2048], scrA[:, 2048:4096]]
        stage_ctr = [0]

        def load_xT(dst, dtoks_fn, dcol0, src_rows, ntok):
            r0 = 0
            ei = 0
            while r0 < ntok:
                n = min(128, ntok - r0)
                si = stage_ctr[0] % 2
                stage_ctr[0] += 1
                st, stt = stage[si], t_scrA[si]
                P.dma("sp", (lambda e, st=st, r0=r0, n=n: e.dma_start(out=st[:n, :], in_=src_rows[r0:r0 + n, :])),
                      s_in[si], writes=[stt])
                for quad in range(4):
                    pb, pt = bank()
                    for cc in range(4):
                        c = quad * 4 + cc
                        P.op("pe", (lambda e, pb=pb, st=st, cc=cc, c=c, n=n: e.transpose(
                            pb[:, cc * 128:cc * 128 + n], st[:n, c * 128:(c + 1) * 128], ident[:n, :n])),
                             reads=[stt, t_const], writes=[pt])
                    eng = "act" if ei % 2 == 0 else "dve"
                    ei += 1
                    dv = dst[:, quad * 4:quad * 4 + 4, dcol0 + r0:dcol0 + r0 + n]
                    sv = pb[:, :].rearrange("p (c t) -> p c t", c=4)[:, :, 0:n]
                    wt = dtoks_fn(quad, dcol0 + r0, n)
                    if eng == "act":
                        P.op("act", (lambda e, dv=dv, sv=sv: e.activation(out=dv, in_=sv, func=AF.Copy)),
                             reads=[pt], writes=wt)
                    else:
                        P.op("dve", (lambda e, dv=dv, sv=sv: e.tensor_copy(out=dv, in_=sv)),
                             reads=[pt], writes=wt)
                r0 += n

        def rstd_tile(xsrc_fn, xtoks, tn, ones_t, nchunks):
            pb, pt = bank()
            for c in range(nchunks):
                qi = ring(sq_c, 2)
                src = xsrc_fn(c)
                if c % 3 == 2:
                    P.op("dve", (lambda e, qi=qi, src=src: e.tensor_tensor(out=sqr[:, qi, 0:tn], in0=src, in1=src, op=ALU.mult)),
                         reads=xtoks(c), writes=[t_sq[qi]])
                else:
                    P.op("act", (lambda e, qi=qi, src=src: e.activation(out=sqr[:, qi, 0:tn], in_=src, func=AF.Square)),
                         reads=xtoks(c), writes=[t_sq[qi]])
                P.op("pe", (lambda e, pb=pb, qi=qi, c=c: e.matmul(
                    pb[:, 0:tn], ones_t[:, :], sqr[:, qi, 0:tn], start=(c == 0), stop=(c == nchunks - 1))),
                     reads=[t_sq[qi], t_ones], writes=[pt])
            ri = ring(rs_c, 2)
            P.op("act", (lambda e, pb=pb, ri=ri: e.activation(
                out=rsr[:, ri, 0:tn], in_=pb[:, 0:tn], func=AF.Ln, bias=epsT[:, 0:1], scale=1.0)),
                 reads=[pt, t_ones], writes=[t_rs[ri]])
            P.op("act", (lambda e, ri=ri: e.activation(
                out=rsr[:, ri, 0:tn], in_=rsr[:, ri, 0:tn], func=AF.Exp, scale=-0.5)),
                 reads=[t_rs[ri]], writes=[t_rs[ri]])
            return rsr[:, ri, 0:tn], t_rs[ri]

        def norm_h(l, ci, ni, xt0, ht0, tn, xti, hti):
            boff = 0 if ni == 0 else 48
            need_mod(l, 1 if ni == 0 else 4)
            tsh = t_modsec[l][0 if ni == 0 else 3]
            tav = t_avs[l][ni]
            rs, rst = rstd_tile(lambda c: xres[:, c, xt0:xt0 + tn], lambda c: [t_x[c][xti]], tn, onesD, 16)
            for c in range(16):
                ti = ring(tmp_c, 3)
                P.op("dve", (lambda e, c=c, ti=ti: e.scalar_tensor_tensor(
                    out=tmpr[:, ti, 0:tn], in0=xres[:, c, xt0:xt0 + tn], scalar=avec[:, l, ci, ni, c:c + 1],
                    in1=rs, op0=ALU.mult, op1=ALU.mult)), reads=[t_x[c][xti], rst, tav], writes=[t_tmp[ti]])
                P.op("act", (lambda e, c=c, ti=ti: e.activation(
                    out=hbuf[:, c, ht0:ht0 + tn], in_=tmpr[:, ti, 0:tn], func=AF.Identity,
                    bias=mvec(l, ci, boff + c), scale=1.0)), reads=[t_tmp[ti], tsh], writes=[t_h[hti]])

        def proj_fm(wv, wtok, col0, tiles, hts, consume):
            for ti, (h0, tn) in enumerate(tiles):
                pb, pt = bank()
                for k in range(16):
                    P.op("pe", (lambda e, pb=pb, k=k, h0=h0, tn=tn: e.matmul(
                        pb[:, 0:tn], wv[:, k, col0:col0 + 128], hbuf[:, k, h0:h0 + tn],
                        start=(k == 0), stop=(k == 15))), reads=[wtok, t_h[hts[ti]]], writes=[pt])
                consume(pb, pt, ti)

        def headnorm(pb, pt, tn, gcol):
            qi = ring(sq_c, 2)
            P.op("act", (lambda e, qi=qi: e.activation(out=sqr[:, qi, 0:tn], in_=pb[:, 0:tn], func=AF.Square)),
                 reads=[pt], writes=[t_sq[qi]])
            pb2, pt2 = bank()
            P.op("pe", (lambda e, qi=qi: e.matmul(pb2[:, 0:tn], onesH[:, :], sqr[:, qi, 0:tn], start=True, stop=True)),
                 reads=[t_sq[qi], t_ones], writes=[pt2])
            ri = ring(rs_c, 2)
            P.op("act", (lambda e, ri=ri: e.activation(
                out=rsr[:, ri, 0:tn], in_=pb2[:, 0:tn], func=AF.Ln, bias=epsT[:, 0:1], scale=1.0)),
                 reads=[pt2, t_ones], writes=[t_rs[ri]])
            P.op("act", (lambda e, ri=ri: e.activation(
                out=rsr[:, ri, 0:tn], in_=rsr[:, ri, 0:tn], func=AF.Exp, scale=-0.5)),
                 reads=[t_rs[ri]], writes=[t_rs[ri]])

            def write(out_ap, wtoks):
                P.op("dve", (lambda e: e.scalar_tensor_tensor(
                    out=out_ap, in0=pb[:, 0:tn], scalar=qkn[:, gcol:gcol + 1], in1=rsr[:, ri, 0:tn],
                    op0=ALU.mult, op1=ALU.mult)), reads=[pt, t_rs[ri], t_const], writes=wtoks)
            return write

        def rope(src_tmp_i, tn, tab0, out_ap, wtoks):
            src = tmpr[:, src_tmp_i, 0:tn]
            pb, pt = bank()
            P.op("pe", (lambda e: e.matmul(pb[:, 0:tn], prot[:, :], src, start=True, stop=True)),
                 reads=[t_tmp[src_tmp_i], t_const], writes=[pt])
            t2 = ring(tmp_c, 3)
            P.op("dve", (lambda e: e.tensor_tensor(out=tmpr[:, t2, 0:tn], in0=pb[:, 0:tn],
                                                   in1=ropeb[:, 992 + tab0:992 + tab0 + tn], op=ALU.mult)),
                 reads=[pt, t_rope], writes=[t_tmp[t2]])
            P.op("dve", (lambda e: e.tensor_tensor(out=src, in0=src, in1=ropeb[:, tab0:tab0 + tn], op=ALU.mult)),
                 reads=[t_tmp[src_tmp_i], t_rope], writes=[t_tmp[src_tmp_i]])
            P.op("dve", (lambda e: e.tensor_tensor(out=out_ap, in0=src, in1=tmpr[:, t2, 0:tn], op=ALU.add)),
                 reads=[t_tmp[src_tmp_i], t_tmp[t2]], writes=wtoks)

        def wout_part(w_ap, r0, kk, l, ci, gate_off, mix_fn, mtoks, tiles_x, tiles_m, xtis, npump=2):
            need_mod(l, 2)
            for cb in range(4):
                pump(npump)
                wt, wtok = wload(wrows(w_ap, r0, kk, cb * 512, 512))
                wv = v3(wt, kk, 512)
                for oc4 in range(4):
                    oc = cb * 4 + oc4
                    for ti, ((x0, tn), (m0, _)) in enumerate(zip(tiles_x, tiles_m)):
                        pb, pt = bank()
                        for k in range(kk):
                            P.op("pe", (lambda e, pb=pb, k=k, oc4=oc4, m0=m0, tn=tn, wv=wv: e.matmul(
                                pb[:, 0:tn], wv[:, k, oc4 * 128:(oc4 + 1) * 128], mix_fn(k, m0, tn),
                                start=(k == 0), stop=(k == kk - 1))), reads=[wtok] + mtoks(ti), writes=[pt])
                        xt = t_x[oc][xtis[ti]]
                        P.op("dve", (lambda e, pb=pb, oc=oc, x0=x0, tn=tn: e.scalar_tensor_tensor(
                            out=xres[:, oc, x0:x0 + tn], in0=pb[:, 0:tn], scalar=mvec(l, ci, gate_off + oc),
                            in1=xres[:, oc, x0:x0 + tn], op0=ALU.mult, op1=ALU.add)),
                             reads=[pt, t_modsec[l][2], xt], writes=[xt])

        def mlp(l, ci, tiles_x, tiles_h, xtis, htis):
            hid = scrB[:, 0:4096].rearrange("p (r c t) -> p r c t", r=4, c=2)
            need_mod(l, 5)
            for jp in range(16):
                w1s = []
                for jj in range(2):
                    jb = jp * 2 + jj
                    w1t, w1tok = wload(wcols(w1_d[l], jb * 256, 256))
                    w1s.append((v3(w1t, 16, 256), w1tok))
                for ti, ((x0, tn), (h0, _)) in enumerate(zip(tiles_x, tiles_h)):
                    for jj in range(2):
                        w1v, w1tok = w1s[jj]
                        hr = jj * 2 + ti
                        for hc in range(2):
                            pb, pt = bank()
                            for k in range(16):
                                P.op("pe", (lambda e, pb=pb, k=k, hc=hc, h0=h0, tn=tn, w1v=w1v: e.matmul(
                                    pb[:, 0:tn], w1v[:, k, hc * 128:(hc + 1) * 128], hbuf[:, k, h0:h0 + tn],
                                    start=(k == 0), stop=(k == 15))), reads=[w1tok, t_h[htis[ti]]], writes=[pt])
                            t1 = ring(tmp_c, 3)
                            P.op("act", (lambda e, pb=pb, t1=t1, tn=tn: e.activation(
                                out=tmpr[:, t1, 0:tn], in_=pb[:, 0:tn], func=AF.Relu)), reads=[pt], writes=[t_tmp[t1]])
                            P.op("act", (lambda e, t1=t1, hr=hr, hc=hc, tn=tn: e.activation(
                                out=hid[:, hr, hc, 0:tn], in_=tmpr[:, t1, 0:tn], func=AF.Square)),
                                 reads=[t_tmp[t1]], writes=[t_scrB[hr]])
                pump(1, alt=True)
                w2s = []
                for jj in range(2):
                    jb = jp * 2 + jj
                    w2t, w2tok = wload(wrows(w2_d[l], jb * 256, 2, 0, 2048))
                    w2s.append((v3(w2t, 2, 2048), w2tok))
                for ti, ((x0, tn), (h0, _)) in enumerate(zip(tiles_x, tiles_h)):
                    for oc in range(16):
                        pb, pt = bank()
                        for jj in range(2):
                            w2v, w2tok = w2s[jj]
                            hr = jj * 2 + ti
                            for hc in range(2):
                                P.op("pe", (lambda e, pb=pb, oc=oc, hc=hc, hr=hr, tn=tn, w2v=w2v, jj=jj: e.matmul(
                                    pb[:, 0:tn], w2v[:, hc, oc * 128:(oc + 1) * 128], hid[:, hr, hc, 0:tn],
                                    start=(jj == 0 and hc == 0), stop=(jj == 1 and hc == 1))),
                                     reads=[w2tok, t_scrB[hr]], writes=[pt])
                        xt = t_x[oc][xtis[ti]]
                        P.op("dve", (lambda e, pb=pb, oc=oc, x0=x0, tn=tn: e.scalar_tensor_tensor(
                            out=xres[:, oc, x0:x0 + tn], in0=pb[:, 0:tn], scalar=mvec(l, ci, 80 + oc),
                            in1=xres[:, oc, x0:x0 + tn], op0=ALU.mult, op1=ALU.add)),
                             reads=[pt, t_modsec[l][5], xt], writes=[xt])

        def attn_core(chunks, tn, o_out, big=False):
            nch = len(chunks)
            groups = []
            i = 0
            while i < nch:
                if (big and i + 1 < nch and chunks[i][0] == 128 and chunks[i + 1][0] == 128
                        and chunks[i][2] is None and chunks[i + 1][2] is None):
                    groups.append([i, i + 1])
                    i += 2
                else:
                    groups.append([i])
                    i += 1
            if big:
                LA = 2
                pview = lambda g: prA6[:, 2 * (g % 3):2 * (g % 3) + 2, :]
                ptok = lambda g: t_scrA[8 + g % 3]
                _, _, hp = bank_pair(hold=True)
                (pO, ptO), (pD, ptD) = hp
            else:
                LA = 1
                pview = lambda g: pr[:, g % 2:g % 2 + 1, :]
                ptok = lambda g: t_pr[g % 2]
                pO, ptO = bank(hold=True)
                pD, ptD = bank(hold=True)
            base = pr_c[0]
            pr_c[0] += len(groups)
            for gi_ in range(len(groups) + LA):
                if gi_ < len(groups):
                    g = groups[gi_]
                    pv_, ptk = pview(base + gi_), ptok(base + gi_)
                    if len(g) == 2:
                        _, pview2, bl = bank_pair()
                        for (pb, pt), ci_ in zip(bl, g):
                            chunks[ci_][1](pb, pt)
                        P.op("act", (lambda e, pview2=pview2, pv_=pv_: e.activation(
                            out=pv_[:, :, 0:tn], in_=pview2[:, :, 0:tn], func=AF.Exp, scale=SCALE)),
                             reads=[bl[0][1], bl[1][1]], writes=[ptk])
                    else:
                        n, score_fn, bias_ap, v_ap, vtoks = chunks[g[0]]
                        pb, pt = bank()
                        score_fn(pb, pt)
                        if bias_ap is None:
                            P.op("act", (lambda e, pb=pb, pv_=pv_, n=n: e.activation(
                                out=pv_[:n, 0, 0:tn], in_=pb[:n, 0:tn], func=AF.Exp, scale=SCALE)),
                                 reads=[pt], writes=[ptk])
                        else:
                            t1 = ring(tmp_c, 3)
                            P.op("dve", (lambda e, pb=pb, t1=t1, n=n, bias_ap=bias_ap: e.scalar_tensor_tensor(
                                out=tmpr[:n, t1, 0:tn], in0=pb[:n, 0:tn], scalar=SCALE, in1=bias_ap,
                                op0=ALU.mult, op1=ALU.add)), reads=[pt, t_rope], writes=[t_tmp[t1]])
                            P.op("act", (lambda e, pv_=pv_, t1=t1, n=n: e.activation(
                                out=pv_[:n, 0, 0:tn], in_=tmpr[:n, t1, 0:tn], func=AF.Exp)),
                                 reads=[t_tmp[t1]], writes=[ptk])
                gj = gi_ - LA
                if gj >= 0:
                    pv_, ptk = pview(base + gj), ptok(base + gj)
                    for k_, j in enumerate(groups[gj]):
                        n, _, _, v_ap, vtoks = chunks[j]
                        P.op("pe", (lambda e, pv_=pv_, n=n, v_ap=v_ap, j=j, k_=k_: e.matmul(
                            pO[:, 0:tn], v_ap, pv_[:n, k_, 0:tn], start=(j == 0), stop=(j == nch - 1))),
                             reads=[ptk] + vtoks, writes=[ptO])
                        P.op("pe", (lambda e, pv_=pv_, n=n, j=j, k_=k_: e.matmul(
                            pD[:, 0:tn], ones1[:n, :], pv_[:n, k_, 0:tn], start=(j == 0), stop=(j == nch - 1))),
                             reads=[ptk, t_ones], writes=[ptD])
            ri = ring(rs_c, 2)
            P.op("act", (lambda e, ri=ri: e.activation(out=rsr[:, ri, 0:tn], in_=pD[:, 0:tn], func=AF.Ln)),
                 reads=[ptD], writes=[t_rs[ri]])
            P.op("act", (lambda e, ri=ri: e.activation(out=rsr[:, ri, 0:tn], in_=rsr[:, ri, 0:tn], func=AF.Exp, scale=-1.0)),
                 reads=[t_rs[ri]], writes=[t_rs[ri]])
            o_out(pO, ptO, rsr[:, ri, 0:tn], t_rs[ri])
            release(pO)
            release(pD)

        prA6 = scrA[:, 2048:3584].bitcast(BF16).rearrange("p (r t) -> p r t", r=6)

        def final_out(out_d, x0, ntok, xti_of):
            t0 = 0
            while t0 < ntok:
                tn = min(512, ntok - t0)
                xti = xti_of(t0)
                rs, rst = rstd_tile(lambda c: xres[:, c, x0 + t0:x0 + t0 + tn], lambda c: [t_x[c][xti]], tn, onesD, 16)
                for c in range(16):
                    P.op("dve", (lambda e, c=c, t0=t0, tn=tn, rs=rs: e.scalar_tensor_tensor(
                        out=xres[:, c, x0 + t0:x0 + t0 + tn], in0=xres[:, c, x0 + t0:x0 + t0 + tn],
                        scalar=gf[:, c:c + 1], in1=rs, op0=ALU.mult, op1=ALU.mult)),
                         reads=[t_x[c][xti], rst, t_const], writes=[t_x[c][xti]])
                for tc in range(tn // 128):
                    si = stage_ctr[0] % 2
                    stage_ctr[0] += 1
                    st, stt = stage[si], t_scrA[si]
                    for quad in range(4):
                        pb, pt = bank()
                        for cc in range(4):
                            c = quad * 4 + cc
                            P.op("pe", (lambda e, pb=pb, cc=cc, c=c, tc=tc, t0=t0: e.transpose(
                                pb[:, cc * 128:(cc + 1) * 128],
                                xres[:, c, x0 + t0 + tc * 128:x0 + t0 + (tc + 1) * 128], ident[:, :])),
                                 reads=[t_x[c][xti], t_const], writes=[pt])
                        if quad % 2 == 0:
                            P.op("act", (lambda e, pb=pb, st=st, quad=quad: e.activation(
                                out=st[:, quad * 512:(quad + 1) * 512], in_=pb[:, :], func=AF.Copy)),
                                 reads=[pt], writes=[stt])
                        else:
                            P.op("dve", (lambda e, pb=pb, st=st, quad=quad: e.tensor_copy(
                                out=st[:, quad * 512:(quad + 1) * 512], in_=pb[:, :])), reads=[pt], writes=[stt])
                    r = t0 + tc * 128
                    P.dma("sp", (lambda e, st=st, r=r: e.dma_start(out=out_d[r:r + 128, :], in_=st[:, :])),
                          s_out[si], reads=[stt])
                t0 += tn

        for c in range(16):
            t_x[c].append(Tok())

        def fence_own():
            P.op("dve", lambda e: e.memset(dummy[:, :], 0.0),
                 writes=[t_x[c][i] for c in range(16) for i in range(3)] + [t_dummy])

        def run_group(gi):
            is_s = gi == 1
            ci = gi
            T = 962 if is_s else 1024
            tiles = [(0, 512), (512, T - 512)]
            xt_all = lambda quad, c0, n: [t_x[quad * 4 + cc][c0 // 512] for cc in range(4)] + \
                ([t_x[quad * 4 + cc][(c0 + n - 1) // 512] for cc in range(4)] if (c0 + n - 1) // 512 != c0 // 512 else [])

            fence(t_scrA)
            fence(t_scrB)
            fence(t_mix)
            if not is_s:
                load_xT(xres, xt_all, 0, xp_d, T)
            need_mod(0, 1)

            ckpt("g%d_load" % gi)
            l = 0
            kT = scrB[:, 0:5120].rearrange("p (g t) -> p g t", g=2)
            vtok = scrB[:, 5120:5120 + 21 * 256].rearrange("p (c n) -> p c n", n=256)
            t_kT, t_vt = t_scrB[4], t_scrB[5]
            koff = 512 if is_s else 0
            if is_s:
                kchunks = [(i * 128, 128, i) for i in range(4)]
                kchunks += [(512 + i * 128, 128, 4 + i) for i in range(7)] + [(512 + 896, 66, 11)]
                kchunks += [(512 + 962 + i * 128, 128, 12 + i) for i in range(8)] + [(512 + 962 + 1024, 62, 20)]

            wk, wktok = wload(wcols(win_d, 4096, 256))
            wkv = v3(wk, 16, 256)
            wv_, wvtok = wload(wcols(win_d, 4352, 256))
            wvv = v3(wv_, 16, 256)

            mixf = mixb[:, :, :].rearrange("p c t -> p (c t)").bitcast(F32)
            kst = mixf[:, 0:2048].rearrange("p (c n) -> p c n", n=256)
            vst = mixf[:, 2048:4096].rearrange("p (c n) -> p c n", n=256)

            def kv_for_tile(h0, tn, hti, key0, vchunk0, tab0, out_tok0):
                for hh in range(2):
                    pb, pt = bank()
                    for k in range(16):
                        P.op("pe", (lambda e, pb=pb, k=k, hh=hh: e.matmul(
                            pb[:, 0:tn], wkv[:, k, hh * 128:(hh + 1) * 128], hbuf[:, k, h0:h0 + tn],
                            start=(k == 0), stop=(k == 15))), reads=[wktok, t_h[hti]], writes=[pt])
                    wr = headnorm(pb, pt, tn, 1)
                    t1 = ring(tmp_c, 3)
                    wr(tmpr[:, t1, 0:tn], [t_tmp[t1]])
                    if is_s:
                        rope(t1, tn, tab0, kT[:, hh, key0:key0 + tn], [t_kT])
                    else:
                        P.op("act", (lambda e, t1=t1, hh=hh: e.activation(
                            out=kT[:, hh, key0:key0 + tn], in_=tmpr[:, t1, 0:tn], func=AF.Copy)),
                             reads=[t_tmp[t1]], writes=[t_kT])
                        pb2, pt2 = bank()
                        for tc in range(tn // 128):
                            P.op("pe", (lambda e, pb2=pb2, t1=t1, tc=tc: e.transpose(
                                pb2[:, tc * 128:(tc + 1) * 128], tmpr[:, t1, tc * 128:(tc + 1) * 128], ident[:, :])),
                                 reads=[t_tmp[t1], t_const], writes=[pt2])
                        c0 = out_tok0 // 128
                        P.op("dve", (lambda e, pb2=pb2, hh=hh, c0=c0: e.tensor_copy(
                            out=kst[:, c0:c0 + tn // 128, hh * 128:(hh + 1) * 128],
                            in_=pb2[:, 0:tn].rearrange("p (c d) -> p c d", d=128))), reads=[pt2], writes=[t_mix[0], t_mix[1]])
                ckpt("g%d_kvK" % gi)
                c = 0
                r0 = 0
                while r0 < tn:
                    n = min(128, tn - r0)
                    pb, pt = bank()
                    for k in range(16):
                        P.op("pe", (lambda e, pb=pb, k=k, r0=r0, n=n: e.matmul(
                            pb[:n, 0:256], hbuf[:, k, h0 + r0:h0 + r0 + n], wvv[:, k, :],
                            start=(k == 0), stop=(k == 15))), reads=[wvtok, t_h[hti]], writes=[pt])
                    vc = vchunk0 + c
                    P.op("act", (lambda e, pb=pb, vc=vc, n=n: e.activation(
                        out=vtok[:n, vc, :], in_=pb[:n, 0:256], func=AF.Copy)), reads=[pt], writes=[t_vt])
                    if not is_s:
                        oc_ = (out_tok0 + r0) // 128
                        P.op("dve", (lambda e, pb=pb, oc_=oc_: e.tensor_copy(out=vst[:, oc_, :], in_=pb[:, 0:256])),
                             reads=[pt], writes=[t_mix[0], t_mix[1]])
                    c += 1
                    r0 += n

            if is_s:
                cks = tmpr[:, 0:2, :].rearrange("p a (b n) -> p (a b) n", n=256)
                P.dma("sp", lambda e: e.dma_start(out=cks, in_=ck0_d.rearrange("(c p) n -> p c n", p=128)),
                      s_in[2], writes=[t_tmp[0], t_tmp[1]])
                for hh in range(2):
                    pb, pt = bank()
                    for c in range(4):
                        P.op("pe", (lambda e, pb=pb, c=c, hh=hh: e.transpose(
                            pb[:, c * 128:(c + 1) * 128], cks[:, c, hh * 128:(hh + 1) * 128], ident[:, :])),
                             reads=[t_tmp[0], t_tmp[1], t_const], writes=[pt])
                    P.op("act", (lambda e, pb=pb, hh=hh: e.activation(out=kT[:, hh, 0:512], in_=pb[:, :], func=AF.Copy)),
                         reads=[pt], writes=[t_kT])
                P.dma("pool", lambda e: e.dma_start(out=vtok[:, 0:4, :], in_=cv0_d.rearrange("(c p) n -> p c n", p=128)),
                      s_in[3], writes=[t_vt])
                rest_tiles = [(962, 0, 512, 12), (1474, 512, 512, 16), (1986, 0, 62, 20)]
                for (r, xc, tn, vch) in rest_tiles:
                    ti_ = xc // 512
                    load_xT(xres, xt_all, xc, xs_d[r:r + tn, :], tn)
                    P.dma("sp", (lambda e, r=r, tn=tn: e.dma_start(out=ropeb[:, 0:tn], in_=cos_d[:, r:r + tn])),
                          s_rope, writes=[t_rope])
                    P.dma("sp", (lambda e, r=r, tn=tn: e.dma_start(out=ropeb[:, 992:992 + tn], in_=sin_d[:, r:r + tn])),
                          s_rope, writes=[t_rope])
                    norm_h(0, ci, 0, xc, xc, tn, ti_, ti_)
                    kv_for_tile(xc, tn, ti_, 512 + r, vch, 0, 0)
                P.dma("sp", lambda e: e.dma_start(out=ropeb[:, 0:962], in_=cos_d[:, 0:962]), s_rope, writes=[t_rope])
                P.dma("sp", lambda e: e.dma_start(out=ropeb[:, 992:992 + 962], in_=sin_d[:, 0:962]), s_rope,
                      writes=[t_rope])

            if is_s:
                load_xT(xres, xt_all, 0, xs_d, T)
            for ti, (t0, tn) in enumerate(tiles):
                norm_h(0, ci, 0, t0, t0, tn, ti, ti)

            ckpt("g%d_norm" % gi)
            for ti, (t0, tn) in enumerate(tiles):
                kv_for_tile(t0, tn, ti, koff + t0, (4 if is_s else 0) + t0 // 128, t0, t0)
            ckpt("g%d_kvV" % gi)
            if not is_s:
                P.dma("sp", lambda e: e.dma_start(out=ak_d.rearrange("(c p) n -> p c n", p=128), in_=kst),
                      s_out[2], reads=[t_mix[0], t_mix[1]])
                P.dma("sp", lambda e: e.dma_start(out=av_d.rearrange("(c p) n -> p c n", p=128), in_=vst),
                      s_out[3], reads=[t_mix[0], t_mix[1]])

            ckpt("g%d_kv" % gi)
            nseq, L = (1, 962) if is_s else (4, 256)
            ub = scrA[:, 0:nseq * (L + 2)].rearrange("p (s t) -> p s t", s=nseq)
            vb = [scrA[:, 1040:1040 + T], scrA[:, 2080:2080 + T]]
            t_u, t_v = t_scrA[3], [t_scrA[4], t_scrA[5]]
            P.op("dve", lambda e: e.memset(ub, 0.0), writes=list(t_scrA))
            for cp in range(4):
                for cc in range(2):
                    c = cp * 2 + cc
                    pump(1)
                    wt, wtok = wload([(lambda t: v3(t, 16, 256)[:, :, 0:128],
                                       win_d.rearrange("(k p) n -> p k n", p=128)[:, :, 2048 + c * 128:2048 + (c + 1) * 128]),
                                      (lambda t: v3(t, 16, 256)[:, :, 128:256],
                                       win_d.rearrange("(k p) n -> p k n", p=128)[:, :, 1024 + c * 128:1024 + (c + 1) * 128])])
                    wv = v3(wt, 16, 256)

                    def uview(t0, tn):
                        if is_s:
                            return ub[:, 0, 1 + t0:1 + t0 + tn]
                        return ub[:, t0 // 256:(t0 + tn) // 256, 1:257]

                    def cons_xa(pb, pt, ti):
                        t0, tn = tiles[ti]
                        src = pb[:, 0:tn] if is_s else pb[:, 0:tn].rearrange("p (s t) -> p s t", t=256)
                        P.op("act", (lambda e: e.activation(out=uview(t0, tn), in_=src, func=AF.Copy)),
                             reads=[pt], writes=[t_u])

                    def cons_gc(pb, pt, ti):
                        t0, tn = tiles[ti]
                        src = pb[:, 0:tn] if is_s else pb[:, 0:tn].rearrange("p (s t) -> p s t", t=256)
                        P.op("dve", (lambda e: e.tensor_tensor(out=uview(t0, tn), in0=src, in1=uview(t0, tn), op=ALU.mult)),
                             reads=[pt, t_u], writes=[t_u])
                    proj_fm(wv, wtok, 0, tiles, [0, 1], cons_xa)
                    proj_fm(wv, wtok, 128, tiles, [0, 1], cons_gc)
                    if is_s:
                        P.op("dve", lambda e: e.tensor_scalar(out=ub[:, 0, 257:258], in0=ub[:, 0, 257:258],
                                                              scalar1=mlr[:, 0:1], scalar2=None, op0=ALU.mult),
                             reads=[t_u, t_const], writes=[t_u])
                        P.op("dve", lambda e: e.tensor_scalar(out=ub[:, 0, 770:771], in0=ub[:, 0, 770:771],
                                                              scalar1=mlr[:, 1:2], scalar2=None, op0=ALU.mult),
                             reads=[t_u, t_const], writes=[t_u])
                    v3d = vb[cc].rearrange("p (s t) -> p s t", s=nseq)
                    P.op("dve", (lambda e, c=c, v3d=v3d: e.tensor_scalar(
                        out=v3d, in0=ub[:, :, 1:L + 1], scalar1=convw[:, c, 1:2], scalar2=None, op0=ALU.mult)),
                         reads=[t_u, t_const], writes=[t_v[cc]])
                    P.op("dve", (lambda e, c=c, v3d=v3d: e.scalar_tensor_tensor(
                        out=v3d, in0=ub[:, :, 0:L], scalar=convw[:, c, 0:1], in1=v3d, op0=ALU.mult, op1=ALU.add)),
                         reads=[t_u, t_const, t_v[cc]], writes=[t_v[cc]])
                    P.op("dve", (lambda e, c=c, v3d=v3d: e.scalar_tensor_tensor(
                        out=v3d, in0=ub[:, :, 2:L + 2], scalar=convw[:, c, 2:3], in1=v3d, op0=ALU.mult, op1=ALU.add)),
                         reads=[t_u, t_const, t_v[cc]], writes=[t_v[cc]])
                pump(1)
                wt, wtok = wload(wcols(win_d, cp * 256, 256))
                wv = v3(wt, 16, 256)
                for cc in range(2):
                    c = cp * 2 + cc

                    def cons_gb(pb, pt, ti, c=c, cc=cc):
                        t0, tn = tiles[ti]
                        P.op("dve", (lambda e: e.tensor_tensor(out=mixb[:, c, t0:t0 + tn], in0=pb[:, 0:tn],
                                                               in1=vb[cc][:, t0:t0 + tn], op=ALU.mult)),
                             reads=[pt, t_v[cc]], writes=[t_mix[ti]])
                    proj_fm(wv, wtok, cc * 128, tiles, [0, 1], cons_gb)
            ckpt("g%d_conv" % gi)
            wout_part(wo0_d, 0, 8, 0, ci, 32, lambda k, m0, tn: mixb[:, k, m0:m0 + tn],
                      lambda ti: [t_mix[ti]], tiles, tiles, [0, 1])
            ckpt("g%d_woutA" % gi)

            fence(t_scrA)
            qT = scrA[:, 0:2048].bitcast(BF16).rearrange("p (h t) -> p h t", h=4)
            t_q = t_scrA[6]
            for g in range(2):
                for qb in range(2):
                    pump(1)
                    wt, wtok = wload(wcols(win_d, 3072 + g * 512 + qb * 256, 256))
                    wv = v3(wt, 16, 256)
                    for hh in range(2):
                        hq = qb * 2 + hh

                        def cons_q(pb, pt, ti, hq=hq):
                            t0, tn = tiles[ti]
                            wr = headnorm(pb, pt, tn, 0)
                            if is_s:
                                t1 = ring(tmp_c, 3)
                                wr(tmpr[:, t1, 0:tn], [t_tmp[t1]])
                                rope(t1, tn, t0, qT[:, hq, t0:t0 + tn], [t_q])
                            else:
                                wr(qT[:, hq, t0:t0 + tn], [t_q])
                        proj_fm(wv, wtok, hh * 128, tiles, [0, 1], cons_q)
                if is_s:
                    for hq in range(4):
                        for ti, (t0, tn) in enumerate(tiles):
                            chunks = []
                            for (k0, n, vc) in kchunks:
                                def sfn(pb, pt, k0=k0, n=n, hq=hq, t0=t0, tn=tn, g=g):
                                    P.op("pe", (lambda e: e.matmul(pb[:n, 0:tn], kT[:, g, k0:k0 + n], qT[:, hq, t0:t0 + tn],
                                                                   start=True, stop=True)),
                                         reads=[t_kT, t_q], writes=[pt])
                                chunks.append((n, sfn, None, vtok[:n, vc, g * 128:(g + 1) * 128], [t_vt]))

                            def oout(pO, ptO, rc, rct, hq=hq, t0=t0, tn=tn, ti=ti, g=g):
                                P.op("dve", (lambda e: e.tensor_tensor(out=mixb[:, 4 * g + hq, t0:t0 + tn], in0=pO[:, 0:tn],
                                                                       in1=rc, op=ALU.mult)),
                                     reads=[ptO, rct], writes=[t_mix[ti]])
                            attn_core(chunks, tn, oout, True)
                else:
                    for s in range(4):
                        for hp in range(2):
                            chunks = []
                            for kc in range(2):
                                k0 = s * 256 + kc * 128

                                def sfn(pb, pt, k0=k0, hp=hp, s=s, g=g):
                                    P.op("pe", (lambda e: e.matmul(pb[:, 0:512], kT[:, g, k0:k0 + 128],
                                                                   qT[:, 2 * hp:2 * hp + 2, s * 256:(s + 1) * 256],
                                                                   start=True, stop=True)),
                                         reads=[t_kT, t_q], writes=[pt])
                                chunks.append((128, sfn, None, vtok[:, s * 2 + kc, g * 128:(g + 1) * 128], [t_vt]))

                            def oout(pO, ptO, rc, rct, hp=hp, s=s, g=g):
                                P.op("dve", (lambda e: e.tensor_tensor(
                                    out=mixb[:, 4 * g + 2 * hp:4 * g + 2 * hp + 2, s * 256:(s + 1) * 256],
                                    in0=pO[:, 0:512].rearrange("p (h t) -> p h t", h=2),
                                    in1=rc.rearrange("p (h t) -> p h t", h=2), op=ALU.mult)),
                                     reads=[ptO, rct], writes=[t_mix[s // 2]])
                            attn_core(chunks, 512, oout)
            ckpt("g%d_attn0" % gi)
            wout_part(wo0_d, 1024, 8, 0, ci, 32, lambda k, m0, tn: mixb[:, k, m0:m0 + tn],
                      lambda ti: [t_mix[ti]], tiles, tiles, [0, 1])

            fence(t_scrB)
            for ti, (t0, tn) in enumerate(tiles):
                norm_h(0, ci, 1, t0, t0, tn, ti, ti)
            ckpt("g%d_mlpnorm" % gi)
            mlp(0, ci, tiles, tiles, [0, 1], [0, 1])
            ckpt("g%d_l0" % gi)

            fence(t_scrA)
            fence(t_scrB)
            for ti, (t0, tn) in enumerate(tiles):
                norm_h(1, ci, 0, t0, t0, tn, ti, ti)
            q2 = scrA[:, 0:1024].bitcast(BF16).rearrange("p (h t) -> p h t", h=2)
            t_q2 = t_scrA[6]
            if is_s:
                kT2 = scrB[:, 0:2 * 1474].rearrange("p (h t) -> p h t", h=2)
                vt2 = scrB[:, 3072:3072 + 12 * 256].rearrange("p (c n) -> p c n", n=256)
                cks1 = tmpr[:, 0:2, :].rearrange("p a (b n) -> p (a b) n", n=256)
                rmb = scrB[0:2, 6144:6144 + 4096]
                P.dma("pool", lambda e: e.dma_start(out=rmb, in_=rm_d), s_rm, writes=[t_scrB[6]])
            else:
                kT2 = scrB[:, 0:2048].rearrange("p (h t) -> p h t", h=2)
                vt2 = scrB[:, 3072:3072 + 8 * 256].rearrange("p (c n) -> p c n", n=256)
                kst1 = scrA[:, 1024:2048].rearrange("p (c n) -> p c n", n=256)
                vst1 = scrA[:, 2048:4096].rearrange("p (c n) -> p c n", n=256)
            t_k2, t_v2 = t_scrB[4], t_scrB[5]
            for half in range(2):
                for pair in range(4):
                    hp0 = (half * 4 + pair) * 2
                    col = hp0 * 128
                    pump(3)
                    wq, wqtok = wload(wcols(wqkv_d, col, 256))
                    wqv = v3(wq, 16, 256)
                    wk2, wk2tok = wload(wcols(wqkv_d, 2048 + col, 256))
                    wk2v = v3(wk2, 16, 256)
                    wv2, wv2tok = wload(wcols(wqkv_d, 4096 + col, 256))
                    wv2v = v3(wv2, 16, 256)
                    for hh in range(2):
                        if is_s:
                            pb, pt = bank()
                            for k in range(16):
                                P.op("pe", (lambda e, pb=pb, k=k, hh=hh, wqv=wqv: e.matmul(
                                    pb[:, 0:512], wqv[:, k, hh * 128:(hh + 1) * 128], hbuf[:, k, 257:769],
                                    start=(k == 0), stop=(k == 15))), reads=[wqtok, t_h[0], t_h[1]], writes=[pt])
                            P.op("act", (lambda e, pb=pb, hh=hh: e.activation(out=q2[:, hh, 0:512], in_=pb[:, :], func=AF.Copy)),
                                 reads=[pt], writes=[t_q2])
                        else:
                            def cons_q2(pb, pt, ti, hh=hh):
                                t0, tn = tiles[ti]
                                P.op("act", (lambda e: e.activation(out=q2[:, hh, t0:t0 + tn], in_=pb[:, 0:tn], func=AF.Copy)),
                                     reads=[pt], writes=[t_q2])
                            proj_fm(wqv, wqtok, hh * 128, tiles, [0, 1], cons_q2)
                    if is_s:
                        P.dma("sp", (lambda e, col=col: e.dma_start(
                            out=cks1, in_=ck1_d[:, col:col + 256].rearrange("(c p) n -> p c n", p=128))),
                              s_in[2], writes=[t_tmp[0], t_tmp[1]])
                        for hh in range(2):
                            pb, pt = bank()
                            for c in range(4):
                                P.op("pe", (lambda e, pb=pb, c=c, hh=hh: e.transpose(
                                    pb[:, c * 128:(c + 1) * 128], cks1[:, c, hh * 128:(hh + 1) * 128], ident[:, :])),
                                     reads=[t_tmp[0], t_tmp[1], t_const], writes=[pt])
                            P.op("act", (lambda e, pb=pb, hh=hh: e.activation(out=kT2[:, hh, 0:512], in_=pb[:, :], func=AF.Copy)),
                                 reads=[pt], writes=[t_k2])
                        P.dma("pool", (lambda e, col=col: e.dma_start(
                            out=vt2[:, 0:4, :], in_=cv1_d[:, col:col + 256].rearrange("(c p) n -> p c n", p=128))),
                              s_in[3], writes=[t_v2])
                    for hh in range(2):
                        def cons_k2(pb, pt, ti, hh=hh, col=col):
                            t0, tn = tiles[ti]
                            if is_s:
                                P.op("act", (lambda e: e.activation(out=kT2[:, hh, 512 + t0:512 + t0 + tn], in_=pb[:, 0:tn],
                                                                    func=AF.Copy)), reads=[pt], writes=[t_k2])
                                return
                            t1 = ring(tmp_c, 3)
                            P.op("act", (lambda e: e.activation(out=tmpr[:, t1, 0:tn], in_=pb[:, 0:tn], func=AF.Copy)),
                                 reads=[pt], writes=[t_tmp[t1]])
                            P.op("dve", (lambda e: e.tensor_copy(out=kT2[:, hh, t0:t0 + tn], in_=pb[:, 0:tn])),
                                 reads=[pt], writes=[t_k2])
                            pb2, pt2 = bank()
                            for tc in range(4):
                                P.op("pe", (lambda e, tc=tc: e.transpose(
                                    pb2[:, tc * 128:(tc + 1) * 128], tmpr[:, t1, tc * 128:(tc + 1) * 128], ident[:, :])),
                                     reads=[t_tmp[t1], t_const], writes=[pt2])
                            P.op("dve", (lambda e: e.tensor_copy(
                                out=kst1[:, :, hh * 128:(hh + 1) * 128],
                                in_=pb2[:, :].rearrange("p (c d) -> p c d", d=128))), reads=[pt2], writes=[t_scrA[3]])
                            if hh == 1:
                                P.dma("sp", (lambda e: e.dma_start(
                                    out=nk_d[t0:t0 + 512, col:col + 256].rearrange("(c p) n -> p c n", p=128), in_=kst1)),
                                      s_out[2], reads=[t_scrA[3]])
                        if is_s:
                            proj_fm(wk2v, wk2tok, hh * 128, tiles, [0, 1], cons_k2)
                    if not is_s:
                        for ti in range(2):
                            for hh in range(2):
                                proj_fm(wk2v, wk2tok, hh * 128, [tiles[ti]], [ti],
                                        (lambda pb, pt, _ti, hh=hh, ti=ti, col=col: cons_k2(pb, pt, ti, hh, col)))
                    if is_s:
                        vrows = [(1 + 128 * m, 128 if m < 7 else 64, 4 + m) for m in range(8)]
                    else:
                        vrows = [(128 * m, 128, m) for m in range(8)]
                    for (r0, n, vc) in vrows:
                        pb, pt = bank()
                        for k in range(16):
                            P.op("pe", (lambda e, pb=pb, k=k, r0=r0, n=n, wv2v=wv2v: e.matmul(
                                pb[:n, 0:256], hbuf[:, k, r0:r0 + n], wv2v[:, k, :], start=(k == 0), stop=(k == 15))),
                                 reads=[wv2tok, t_h[0], t_h[1]], writes=[pt])
                        P.op("act", (lambda e, pb=pb, vc=vc, n=n: e.activation(out=vt2[:n, vc, :], in_=pb[:n, 0:256], func=AF.Copy)),
                             reads=[pt], writes=[t_v2])
                        if not is_s:
                            P.op("dve", (lambda e, pb=pb, vc=vc: e.tensor_copy(out=vst1[:, vc, :], in_=pb[:, 0:256])),
                                 reads=[pt], writes=[t_scrA[4]])
                    if not is_s:
                        P.dma("sp", (lambda e, col=col: e.dma_start(
                            out=nv_d[:, col:col + 256].rearrange("(c p) n -> p c n", p=128), in_=vst1)),
                              s_out[3], reads=[t_scrA[4]])
                    pend = []
                    for hh in range(2):
                        mi = pair * 2 + hh
                        if is_s:
                            head = hp0 + hh
                            P.dma("sp", (lambda e, head=head: e.dma_start(out=ropeb[:, 0:1408], in_=cm_d[head])),
                                  s_rope, writes=[t_rope])
                            chunks = []
                            for m in range(8):
                                n = 128 if m < 7 else 64
                                k0 = 512 + 1 + 128 * m

                                def sfn(pb, pt, m=m, n=n, k0=k0, hh=hh):
                                    P.op("pe", (lambda e: e.matmul(pb[:n, 0:512], kT2[:, hh, k0:k0 + n], q2[:, hh, 0:512],
                                                                   start=True, stop=False)),
                                         reads=[t_k2, t_q2], writes=[pt])
                                    P.op("pe", (lambda e: e.matmul(pb[:n, 0:512], lsel[:, 0:n], rmb[:, m * 512:(m + 1) * 512],
                                                                   start=False, stop=True)),
                                         reads=[t_c2, t_scrB[6]], writes=[pt])
                                b0 = (14 - 2 * m) * 64
                                chunks.append((n, sfn, ropeb[:n, b0:b0 + 512], vt2[:n, 4 + m, hh * 128:(hh + 1) * 128], [t_v2]))
                            for c in range(4):
                                def sfn(pb, pt, c=c, hh=hh):
                                    P.op("pe", (lambda e: e.matmul(pb[:, 0:512], kT2[:, hh, c * 128:(c + 1) * 128], q2[:, hh, 0:512],
                                                                   start=True, stop=True)),
                                         reads=[t_k2, t_q2], writes=[pt])
                                chunks.append((128, sfn, None, vt2[:, c, hh * 128:(hh + 1) * 128], [t_v2]))

                            def oout(pO, ptO, rc, rct, mi=mi):
                                P.op("dve", (lambda e: e.tensor_tensor(out=mixb[:, mi, 0:512], in0=pO[:, 0:512], in1=rc, op=ALU.mult)),
                                     reads=[ptO, rct], writes=[t_mix[0]])
                            attn_core(chunks, 512, oout, True)
                        else:
                            for s in range(4):
                                pb, pt = bank()
                                for kc in range(2):
                                    k0 = s * 256 + kc * 128
                                    P.op("pe", (lambda e, pb=pb, kc=kc, k0=k0, hh=hh, s=s: e.matmul(
                                        pb[:, kc * 256:(kc + 1) * 256], kT2[:, hh, k0:k0 + 128],
                                        q2[:, hh, s * 256:(s + 1) * 256], start=True, stop=True)),
                                         reads=[t_k2, t_q2], writes=[pt])
                                ui = ring(pr_c, 2)
                                P.op("act", (lambda e, pb=pb, ui=ui: e.activation(
                                    out=pr[:, ui, :], in_=pb[:, :], func=AF.Exp, scale=SCALE)),
                                     reads=[pt], writes=[t_pr[ui]])

                                def tail(ui=ui, s=s, hh=hh, mi=mi):
                                    pO, ptO = bank()
                                    for kc in range(2):
                                        P.op("pe", (lambda e, kc=kc: e.matmul(
                                            pO[:, 0:256], vt2[:, s * 2 + kc, hh * 128:(hh + 1) * 128],
                                            pr[:, ui, kc * 256:(kc + 1) * 256], start=(kc == 0), stop=(kc == 1))),
                                             reads=[t_pr[ui], t_v2], writes=[ptO])
                                    for kc in range(2):
                                        P.op("pe", (lambda e, kc=kc: e.matmul(
                                            pO[:, 256:512], ones1[:, :], pr[:, ui, kc * 256:(kc + 1) * 256],
                                            start=(kc == 0), stop=(kc == 1))), reads=[t_pr[ui], t_ones], writes=[ptO])
                                    ri = ring(rs_c, 2)
                                    P.op("act", (lambda e: e.activation(out=rsr[:, ri, 0:256], in_=pO[:, 256:512], func=AF.Ln)),
                                         reads=[ptO], writes=[t_rs[ri]])
                                    P.op("act", (lambda e: e.activation(out=rsr[:, ri, 0:256], in_=rsr[:, ri, 0:256],
                                                                        func=AF.Exp, scale=-1.0)),
                                         reads=[t_rs[ri]], writes=[t_rs[ri]])
                                    P.op("dve", (lambda e: e.tensor_tensor(out=mixb[:, mi, s * 256:(s + 1) * 256],
                                                                           in0=pO[:, 0:256], in1=rsr[:, ri, 0:256], op=ALU.mult)),
                                         reads=[ptO, t_rs[ri]], writes=[t_mix[s // 2]])
                                if pend:
                                    pend.pop(0)()
                                pend.append(tail)
                    while pend:
                        pend.pop(0)()
                if is_s:
                    if half == 0:
                        fence_own()
                    wout_part(wo1_d, half * 1024, 8, 1, ci, 32, lambda k, m0, tn: mixb[:, k, m0:m0 + tn],
                              lambda ti: [t_mix[0]], [(257, 512)], [(0, 512)], [2])
                else:
                    wout_part(wo1_d, half * 1024, 8, 1, ci, 32, lambda k, m0, tn: mixb[:, k, m0:m0 + tn],
                              lambda ti: [t_mix[ti]], tiles, tiles, [0, 1])

            ckpt("g%d_attn1" % gi)
            if is_s:
                pass
            return is_s

        def run_tail(gi):
            is_s = gi == 1
            ci = gi
            if not is_s:
                tiles = [(0, 512), (512, 512)]
                for ti, (t0, tn) in enumerate(tiles):
                    norm_h(1, ci, 1, t0, t0, tn, ti, ti)
                fence(t_scrB)
                mlp(1, ci, tiles, tiles, [0, 1], [0, 1])
                fence(t_scrA)
                final_out(yp_d, 0, 1024, lambda t0: t0 // 512)
            else:
                norm_h(1, ci, 1, 257, 0, 512, 2, 0)
                fence(t_scrB)
                mlp(1, ci, [(257, 512)], [(0, 512)], [2], [0])
                fence(t_scrA)
                final_out(ys_d, 257, 512, lambda t0: 2)


        try:
            if _skip_mod or _stop_mod:
                raise _Stop()
            if 0 in groups:
                run_group(0)
                run_tail(0)
            ckpt("g0")
            if 1 in groups:
                run_group(1)
                run_tail(1)
        except _Stop:
            pass

        with nc.Block() as block:
            P.emit(block, s_out)
    return nc


_NC_CACHE = {}


def _fm(v):
    v = np.asarray(v, np.float32)
    return np.ascontiguousarray(v.reshape(-1, 128).T)


def _na_tables(rel_bias):
    H = rel_bias.shape[0]
    tab = np.full((H, 128, 22, 64), NEG, np.float32)
    qc = np.arange(64)
    kc0 = np.clip(qc - 8, 0, 48)
    for half in range(2):
        for kc in range(64):
            p = half * 64 + kc
            inwin = (kc >= kc0) & (kc < kc0 + 16)
            dc = kc - qc + 15
            for j in range(22):
                dr = 17 - j + half
                if 0 <= dr < 15:
                    vals = rel_bias[:, dr, np.clip(dc, 0, 30)]
                    tab[:, p, j, :] = np.where(inwin[None, :], vals, NEG)
    return np.ascontiguousarray(tab.reshape(H, 128, 22 * 64))


def _rm_table(q):
    rm = np.full((2, 8, 8, 64), NEG, np.float32)
    for m in range(8):
        for half in range(2):
            lk = 2 * m + half
            kr = 8 * q - 4 + lk
            for lq in range(4, 12):
                qr = 8 * q - 4 + lq
                kr0 = min(max(qr - 4, 0), 24)
                if 0 <= kr < 32 and kr0 <= kr < kr0 + 8 and lk < 15:
                    rm[half, m, lq - 4, :] = 0.0
    return np.ascontiguousarray(rm.reshape(2, 8 * 512))


def _rope_tables(gtok):
    half = 32
    inv = (10000.0 ** (-np.arange(half, dtype=np.float32) / half)).astype(np.float32)
    row = (gtok // 64).astype(np.float32)
    colp = (gtok % 64).astype(np.float32)
    cos = np.zeros((128, gtok.shape[0]), np.float32)
    sin = np.zeros((128, gtok.shape[0]), np.float32)
    for m in range(128):
        pos = row if m < 64 else colp
        ang = pos * inv[m % 32]
        cos[m] = np.cos(ang.astype(np.float32))
        sin[m] = np.sin(ang.astype(np.float32))
    return cos, sin


def kernel(x_prompt, x_sample, cache_attn_k, cache_attn_v, cache_na_k, cache_na_v, c, c_ctx,
           mod_w, mod_b, norm1_g, norm2_g, ab_w_in, ab_conv_w, ab_q_norm, ab_k_norm, ab_w_out,
           na_w_qkv, na_rel_bias, na_w_out, mlp_w1, mlp_w2, final_norm_g):
    if "nc" not in _NC_CACHE:
        _NC_CACHE["nc"] = build_program()
    nc = _NC_CACHE["nc"]
    in_maps = make_in_maps(x_prompt, x_sample, cache_attn_k, cache_attn_v, cache_na_k, cache_na_v, c, c_ctx,
                           mod_w, mod_b, norm1_g, norm2_g, ab_w_in, ab_conv_w, ab_q_norm, ab_k_norm, ab_w_out,
                           na_w_qkv, na_rel_bias, na_w_out, mlp_w1, mlp_w2, final_norm_g)
    res = run_bass_kernel_spmd(nc, in_maps, core_ids=list(range(8)))
    return gather_outputs(res.results)


def make_in_maps(x_prompt, x_sample, cache_attn_k, cache_attn_v, cache_na_k, cache_na_v, c, c_ctx,
                 mod_w, mod_b, norm1_g, norm2_g, ab_w_in, ab_conv_w, ab_q_norm, ab_k_norm, ab_w_out,
                 na_w_qkv, na_rel_bias, na_w_out, mlp_w1, mlp_w2, final_norm_g):
    f32 = lambda a: np.ascontiguousarray(np.asarray(a, np.float32))

    x_prompt = f32(x_prompt)
    x_sample = f32(x_sample)
    shared = {
        "modw": f32(mod_w),
        "modb": np.ascontiguousarray(np.concatenate([_fm(mod_b[0]), _fm(mod_b[1])], axis=1)),
        "g1": np.ascontiguousarray(np.concatenate([_fm(norm1_g[0]), _fm(norm1_g[1])], axis=1)),
        "g2": np.ascontiguousarray(np.concatenate([_fm(norm2_g[0]), _fm(norm2_g[1])], axis=1)),
        "gf": _fm(final_norm_g),
        "w_in": f32(ab_w_in[0]),
        "convw": np.ascontiguousarray(np.asarray(ab_conv_w[0], np.float32).reshape(3, 8, 128).transpose(2, 1, 0).reshape(128, 24)),
        "qkn": np.ascontiguousarray(np.stack([np.asarray(ab_q_norm[0], np.float32), np.asarray(ab_k_norm[0], np.float32)], axis=1)),
        "w_out0": f32(ab_w_out[0]),
        "w_qkv": f32(na_w_qkv[0]),
        "w_out1": f32(na_w_out[0]),
        "w1": f32(mlp_w1),
        "w2": f32(mlp_w2),
        "cm": _na_tables(np.asarray(na_rel_bias[0], np.float32)),
        "ident": np.eye(128, dtype=np.float32),
    }
    lsel = np.zeros((2, 128), np.float32)
    lsel[0, 0:64] = 1.0
    lsel[1, 64:128] = 1.0
    shared["lsel"] = lsel
    prot = np.zeros((128, 128), np.float32)
    for m in range(128):
        if m % 64 < 32:
            prot[m + 32, m] = -1.0
        else:
            prot[m - 32, m] = 1.0
    shared["prot"] = prot

    in_maps = []
    for core in range(8):
        b, q = core // 4, core % 4
        W0 = 64 * (8 * q - 4)
        gtok = (W0 - 1 + np.arange(2048)) % 2048
        cos, sin = _rope_tables(gtok)
        cond = np.stack([_fm(c_ctx), _fm(c[b])], axis=2).reshape(128, 32)
        mlr = np.ones((128, 2), np.float32)
        if q == 0:
            mlr[:, 0] = 0.0
        if q == 3:
            mlr[:, 1] = 0.0
        m = dict(shared)
        m.update({
            "xp": np.ascontiguousarray(x_prompt[4 * core:4 * core + 4].reshape(1024, D)),
            "xs": np.ascontiguousarray(x_sample[b][gtok]),
            "ck0": f32(cache_attn_k[b, 0]).reshape(512, 256),
            "cv0": f32(cache_attn_v[b, 0]).reshape(512, 256),
            "ck1": f32(cache_na_k[b, 0]).reshape(512, D),
            "cv1": f32(cache_na_v[b, 0]).reshape(512, D),
            "condT": np.ascontiguousarray(cond),
            "rm": _rm_table(q),
            "cos": cos, "sin": sin, "mlr": mlr,
        })
        in_maps.append(m)
    return in_maps


def gather_outputs(r):
    y_prompt = np.concatenate([r[i]["yp"].reshape(4, 256, D) for i in range(8)], axis=0)
    y_sample = np.stack([np.concatenate([r[b * 4 + q]["ys"] for q in range(4)], axis=0) for b in range(2)], axis=0)
    ak = np.concatenate([r[i]["ak"].reshape(4, 1, 256, 2, 128) for i in range(8)], axis=0)
    av = np.concatenate([r[i]["av"].reshape(4, 1, 256, 2, 128) for i in range(8)], axis=0)
    nk = np.concatenate([r[i]["nk"].reshape(4, 1, 256, 16, 128) for i in range(8)], axis=0)
    nv = np.concatenate([r[i]["nv"].reshape(4, 1, 256, 16, 128) for i in range(8)], axis=0)
    return (y_prompt.astype(np.float32), y_sample.astype(np.float32), ak.astype(np.float32),
            av.astype(np.float32), nk.astype(np.float32), nv.astype(np.float32))
```
